# Optimizing a Trainium2 kernel written in Bass

```python
import jax, jax.numpy as jnp
from jax import lax
import numpy as np

D_MODEL = 2048
BATCH = 1
SEQ = 16384
DEPTH = 2

CONV_WIDTH = D_MODEL // 2
CONV_K = 3
HEAD_DIM = 128
ATTN_WIDTH = D_MODEL // 2
N_HEADS = ATTN_WIDTH // HEAD_DIM
D_FF = 4 * D_MODEL
BLOCK_Q = 128
EPS = 1e-6

SPLIT_SIZES = (CONV_WIDTH, CONV_WIDTH, CONV_WIDTH,
               ATTN_WIDTH, ATTN_WIDTH, ATTN_WIDTH,
               N_HEADS, D_MODEL, D_MODEL)
IN_COLS = sum(SPLIT_SIZES)
SPLIT_POINTS = tuple(int(v) for v in np.cumsum(SPLIT_SIZES)[:-1])

kernel_name = "hybrid_conv_fox_gated_block"


def rmsnorm(x, g):
    xf = x.astype(jnp.float32)
    y = xf * lax.rsqrt(jnp.mean(xf * xf, axis=-1, keepdims=True) + EPS)
    return (y * g.astype(jnp.float32)).astype(x.dtype)


def causal_dwconv(u, w):
    s = u.shape[1]
    up = jnp.pad(u, ((0, 0), (CONV_K - 1, 0), (0, 0)))
    y = w[0] * up[:, 0:s, :]
    for j in range(1, CONV_K):
        y = y + w[j] * up[:, j:j + s, :]
    return y


def forgetting_attention(q, k, v, log_f):
    b, s, h, d = q.shape
    nb = s // BLOCK_Q
    c = jnp.cumsum(log_f, axis=1).transpose(0, 2, 1)
    qh = q.transpose(0, 2, 1, 3)
    kh = k.transpose(0, 2, 1, 3)
    vh = v.transpose(0, 2, 1, 3)
    scale = float(d) ** -0.5
    q_blocks = qh.reshape(b, h, nb, BLOCK_Q, d).transpose(2, 0, 1, 3, 4)
    c_blocks = c.reshape(b, h, nb, BLOCK_Q).transpose(2, 0, 1, 3)
    k_pos = jnp.arange(s)

    def one_block(args):
        qb, cb, i = args
        logits = jnp.einsum('bhqd,bhkd->bhqk', qb, kh).astype(jnp.float32) * scale
        logits = logits + cb[..., :, None] - c[..., None, :]
        q_pos = i * BLOCK_Q + jnp.arange(BLOCK_Q)
        causal = k_pos[None, :] <= q_pos[:, None]
        logits = jnp.where(causal, logits, -jnp.inf)
        p = jax.nn.softmax(logits, axis=-1)
        return jnp.einsum('bhqk,bhkd->bhqd', p.astype(vh.dtype), vh)

    out = lax.map(one_block, (q_blocks, c_blocks, jnp.arange(nb)))
    return out.transpose(1, 0, 3, 2, 4).reshape(b, s, h * d)


def setup_inputs(seed: int = 0) -> dict:
    key = jax.random.key(seed)
    ks = jax.random.split(key, 14)
    nrm = jax.random.normal
    x = nrm(ks[0], (BATCH, SEQ, D_MODEL), jnp.float32)
    g_mix = 1.0 + 0.02 * nrm(ks[1], (DEPTH, D_MODEL), jnp.float32)
    w_in = nrm(ks[2], (DEPTH, D_MODEL, IN_COLS), jnp.float32) * D_MODEL ** -0.5
    b_f = 2.0 + 0.5 * nrm(ks[3], (DEPTH, N_HEADS), jnp.float32)
    b_gate = 0.02 * nrm(ks[4], (DEPTH, 2 * D_MODEL), jnp.float32)
    conv_w = nrm(ks[5], (DEPTH, CONV_K, CONV_WIDTH), jnp.float32) * CONV_K ** -0.5
    w_conv_out = nrm(ks[6], (DEPTH, CONV_WIDTH, D_MODEL), jnp.float32) * CONV_WIDTH ** -0.5
    w_attn_out = nrm(ks[7], (DEPTH, ATTN_WIDTH, D_MODEL), jnp.float32) * ATTN_WIDTH ** -0.5
    w_mix_out = nrm(ks[8], (DEPTH, D_MODEL, D_MODEL), jnp.float32) * D_MODEL ** -0.5
    g_mlp = 1.0 + 0.02 * nrm(ks[9], (DEPTH, D_MODEL), jnp.float32)
    w_ff1 = nrm(ks[10], (DEPTH, D_MODEL, D_FF), jnp.float32) * D_MODEL ** -0.5
    w_ff2 = nrm(ks[11], (DEPTH, D_FF, D_MODEL), jnp.float32) * D_FF ** -0.5
    g_final = 1.0 + 0.02 * nrm(ks[12], (D_MODEL,), jnp.float32)
    return {"x": x, "g_mix": g_mix, "w_in": w_in, "b_f": b_f, "b_gate": b_gate,
            "conv_w": conv_w, "w_conv_out": w_conv_out, "w_attn_out": w_attn_out,
            "w_mix_out": w_mix_out, "g_mlp": g_mlp, "w_ff1": w_ff1, "w_ff2": w_ff2,
            "g_final": g_final}


def reference(x, g_mix, w_in, b_f, b_gate, conv_w, w_conv_out, w_attn_out,
              w_mix_out, g_mlp, w_ff1, w_ff2, g_final):
    b, s, _ = x.shape
    for l in range(DEPTH):
        h = rmsnorm(x, g_mix[l])
        z = jnp.einsum('bsd,dc->bsc', h, w_in[l])
        cb, cc, cv, q, k, v, f_logit, gate_c, gate_a = jnp.split(z, SPLIT_POINTS, axis=-1)

        conv_y = cb * causal_dwconv(cc * cv, conv_w[l])
        conv_branch = jnp.einsum('bsc,cd->bsd', conv_y, w_conv_out[l])

        log_f = jax.nn.log_sigmoid(f_logit.astype(jnp.float32) + b_f[l].astype(jnp.float32))
        attn = forgetting_attention(q.reshape(b, s, N_HEADS, HEAD_DIM),
                                    k.reshape(b, s, N_HEADS, HEAD_DIM),
                                    v.reshape(b, s, N_HEADS, HEAD_DIM), log_f)
        attn_branch = jnp.einsum('bsc,cd->bsd', attn, w_attn_out[l])

        gc = jax.nn.sigmoid(gate_c + b_gate[l, :D_MODEL])
        ga = jax.nn.sigmoid(gate_a + b_gate[l, D_MODEL:])
        merged = gc * conv_branch + ga * attn_branch
        x = x + jnp.einsum('bsd,de->bse', merged, w_mix_out[l])

        h = rmsnorm(x, g_mlp[l])
        u = jnp.square(jax.nn.relu(jnp.einsum('bsd,df->bsf', h, w_ff1[l])))
        x = x + jnp.einsum('bsf,fd->bsd', u, w_ff2[l])
    return rmsnorm(x, g_final)
```

```python
import contextlib
import numpy as np
import ml_dtypes
import concourse.bass as bass
import concourse.mybir as mybir
from concourse.bass_utils import run_bass_kernel_spmd

F32 = mybir.dt.float32
BF16 = mybir.dt.bfloat16
AF = mybir.ActivationFunctionType
ALU = mybir.AluOpType

NCORES = 8
D = 2048
S = 16384
TL = S // NCORES
T = 512
NT = TL // T
NH = 8
HD = 128
DFF = 8192
EPS = 1e-6
SCALE = float(HD) ** -0.5
NGRP = 15
G_CBCC, G_CVQ, G_KV, G_GC, G_GA, G_MIX, G_CA = 0, 1, 2, 3, 4, 5, 6
G_FF1 = 7
G_FF2 = 11


class Op:
    __slots__ = ("eng", "fn", "deps", "dma_key", "ms", "is_mm", "waits", "cc", "idx")

    def __init__(self, eng, fn, dma_key=None, is_mm=False):
        self.eng, self.fn, self.dma_key, self.is_mm = eng, fn, dma_key, is_mm
        self.cc = False
        self.deps = []
        self.ms = None
        self.waits = []


class Prog:
    ENGS = ("pe", "act", "dve", "pool", "sp")

    def __init__(self):
        self.ops = {e: [] for e in self.ENGS}
        self.last_w = {}
        self.readers = {}
        self.all_ops = []
        self.dma_keys = []
        self.last_dma = {}
        self.bar_deps = []
        self.bar_pending = set()

    def add(self, eng, fn, reads=(), writes=(), dma_key=None, is_mm=False):
        op = Op(eng, fn, dma_key, is_mm)
        deps = []
        for t in reads:
            w = self.last_w.get(t)
            if w is not None:
                deps.append(w)
        for t in writes:
            w = self.last_w.get(t)
            if w is not None:
                deps.append(w)
            deps.extend(self.readers.get(t, ()))
        seen = set()
        for d in deps:
            if d is op or id(d) in seen:
                continue
            if is_mm and d.is_mm and d.dma_key is None:
                continue
            seen.add(id(d))
            op.deps.append(d)
        if eng in self.bar_pending:
            self.bar_pending.discard(eng)
            for d in self.bar_deps:
                if d is not op and id(d) not in seen and not (d.eng == eng and d.dma_key is None):
                    seen.add(id(d))
                    op.deps.append(d)
        for t in reads:
            self.readers.setdefault(t, []).append(op)
        for t in writes:
            self.last_w[t] = op
            self.readers[t] = []
        if dma_key is not None:
            self.last_dma[dma_key] = op
        best = {}
        kept = []
        for d in op.deps:
            if d.dma_key is not None:
                kept.append(d)
            elif d.eng not in best or d.idx > best[d.eng].idx:
                best[d.eng] = d
        op.deps = kept + list(best.values())
        op.idx = len(self.ops[eng])
        self.ops[eng].append(op)
        self.all_ops.append(op)
        if dma_key is not None and dma_key not in self.dma_keys:
            self.dma_keys.append(dma_key)
        return op

    def dma(self, queue, out, in_, reads=(), writes=(), key=None):
        assert key is not None
        return self.add(queue, lambda e: e.dma_start(out=out, in_=in_), reads, writes, dma_key=key)

    def barrier(self):
        deps = [self.ops[e][-1] for e in self.ENGS if self.ops[e] and self.ops[e][-1].dma_key is None]
        deps += list(self.last_dma.values())
        self.bar_deps = deps
        self.bar_pending = set(self.ENGS)

    def collective(self, in_ap, out_ap, reads=(), writes=(), key=None):
        op = self.add("pool", lambda e: e.collective_compute(
            "AllGather", ALU.bypass, replica_groups=[list(range(NCORES))], ins=[in_ap], outs=[out_ap]),
            reads, writes, dma_key=key)
        op.cc = True
        return op

    def finalize(self):
        needed = set()
        for op in self.all_ops:
            for d in op.deps:
                needed.add(id(d))
        cnt = {e: 0 for e in self.ENGS}
        dcnt = {}
        final_dma = {}
        for op in self.all_ops:
            if op.dma_key is not None:
                dcnt[op.dma_key] = dcnt.get(op.dma_key, 0) + (1 if op.cc else 16)
                op.ms = ("dma:%s" % (op.dma_key,), dcnt[op.dma_key])
                final_dma[op.dma_key] = dcnt[op.dma_key]
            elif id(op) in needed:
                cnt[op.eng] += 1
                op.ms = ("eng:" + op.eng, cnt[op.eng])
        waited = {e: {} for e in self.ENGS}
        for e in self.ENGS:
            for op in self.ops[e]:
                w = {}
                for d in op.deps:
                    s, v = d.ms
                    if waited[e].get(s, 0) >= v:
                        continue
                    w[s] = max(w.get(s, 0), v)
                for s, v in w.items():
                    waited[e][s] = v
                op.waits = list(w.items())
        self.final_dma = final_dma
        self.max_counts = dict(cnt)

    def sem_names(self):
        return ["eng:" + e for e in self.ENGS] + ["dma:%s" % (k,) for k in self.dma_keys]

    def emit(self, nc, sems, final_wait_eng="sp"):
        engmap = {"pe": "tensor", "act": "scalar", "dve": "vector", "pool": "gpsimd", "sp": "sync"}
        with nc.Block() as block:
            for e in self.ENGS:
                ops = self.ops[e]
                fw = self.final_dma if e == final_wait_eng else {}

                def body(engine, ops=ops, fw=fw):
                    for op in ops:
                        for s, v in op.waits:
                            engine.wait_ge(sems[s], v)
                        ins = op.fn(engine)
                        if op.cc:
                            ins.then_inc(sems[op.ms[0]])
                        elif op.ms is not None:
                            ins.then_inc(sems[op.ms[0]], 16 if op.dma_key is not None else 1)
                    for k, v in fw.items():
                        engine.wait_ge(sems["dma:%s" % (k,)], v)

                getattr(block, engmap[e])(body)


class Ctx:
    def __init__(self):
        self.nc = bass.Bass("TRN2", target_bir_lowering=False)
        self.P = Prog()
        self.stack = contextlib.ExitStack()
        self.nsb = 0

    def dram_in(self, name, shape, dt):
        return self.nc.dram_tensor(name, list(shape), dt, kind="ExternalInput").ap()

    def dram_out(self, name, shape, dt):
        return self.nc.dram_tensor(name, list(shape), dt, kind="ExternalOutput").ap()

    def sb(self, name, shape, dt):
        return self.stack.enter_context(self.nc.sbuf_tensor(name, list(shape), dt))

    def ps(self, name):
        return self.stack.enter_context(self.nc.psum_tensor(name, [128, 512], F32))

    def finish(self):
        P = self.P
        P.finalize()
        sems = {}
        for i, n in enumerate(P.sem_names()):
            sems[n] = self.stack.enter_context(self.nc.semaphore("s%d" % i))
        P.emit(self.nc, sems)
        self.stack.close()
        return self.nc


GPC = 4


def build_W():
    c = Ctx()
    P = c.P
    wsrc = c.dram_in("wsrc", [GPC, 2048, 2048], F32)
    wb = c.dram_out("wb", [GPC, 16, 128, 2048], BF16)
    stage = [c.sb("stage%d" % i, [128, 2048], F32) for i in range(3)]
    wt = [c.sb("wt%d" % i, [128, 16, 16, 128], BF16) for i in range(2)]
    n = 0
    for g in range(GPC):
        w = wt[g % 2]
        for k in range(16):
            st = stage[n % 3]
            tok = ("stage", n % 3)
            P.dma("sp", st[:, :], wsrc[g, k * 128:(k + 1) * 128, :], writes=[tok], key=tok)
            src = st[:, :].rearrange("p (m c) -> p m c", m=16)
            dst = w[:, :, k, :]
            eng = "dve" if n % 2 == 0 else "pool"
            P.add(eng, lambda e, dst=dst, src=src: e.tensor_copy(out=dst, in_=src),
                  reads=[tok], writes=[("wt", g % 2, k)])
            n += 1
        P.dma("pool", wb[g].rearrange("m p x -> p m x"),
              w[:, :, :, :].rearrange("p m k c -> p m (k c)"),
              reads=[("wt", g % 2, k) for k in range(16)], key=("wtst", g % 2))
    return c.finish()


def emit_norm(c, xt, xtok, h, htok, gcol, ones_bf, sq, ps_stat, rt, rstd, uid):
    P = c.P
    for k in range(16):
        P.add("act", lambda e, k=k: e.activation(out=sq[:, k, :], in_=xt[:, k, :], func=AF.Square),
              reads=[xtok], writes=[("sq", k)])
    for k in range(16):
        P.add("pe", lambda e, k=k: e.matmul(ps_stat[:, :], lhsT=ones_bf[:, :], rhs=sq[:, k, :],
                                             start=(k == 0), stop=(k == 15)),
              reads=[("sq", k), "consts"], writes=[("ps", id(ps_stat))], is_mm=True)
    P.add("act", lambda e: e.activation(out=rt[:, :], in_=ps_stat[:, :], func=AF.Sqrt,
                                        bias=EPS, scale=1.0 / D),
          reads=[("ps", id(ps_stat))], writes=["rt"])
    P.add("dve", lambda e: e.reciprocal(out=rstd[:, :], in_=rt[:, :]), reads=["rt"], writes=["rstd"])
    for k in range(16):
        P.add("dve", lambda e, k=k: e.scalar_tensor_tensor(out=h[:, k, :], in0=xt[:, k, :],
                                                            scalar=gcol[:, k:k + 1], in1=rstd[:, :],
                                                            op0=ALU.mult, op1=ALU.mult),
              reads=[xtok, "rstd", "consts"], writes=[(htok, k)])


def build_A():
    c = Ctx()
    P = c.P
    xT = c.dram_in("xT", [D, TL], F32)
    gmix = c.dram_in("gmix", [128, 16], F32)
    wq = c.dram_in("wq", [8, 128, 2048], BF16)
    wk = c.dram_in("wk", [8, 128, 2048], BF16)
    wv = c.dram_in("wv", [8, 128, 2048], BF16)
    wf = c.dram_in("wf", [128, 16, 8], F32)
    bfb = c.dram_in("bfb", [128, 32], F32)
    ones_in = c.dram_in("ones_bf", [128, 128], BF16)
    qT_o = c.dram_out("qT", [NH, 128, TL], BF16)
    kT_o = c.dram_out("kT", [NH, 128, TL], BF16)
    v_o = c.dram_out("v", [TL, 1024], BF16)
    lp_o = c.dram_out("lp", [TL, 8], F32)

    g_sb = c.sb("g_sb", [128, 16], F32)
    ones_bf = c.sb("ones", [128, 128], BF16)
    wq_sb = c.sb("wq_sb", [128, 8, 16, 128], BF16)
    wk_sb = c.sb("wk_sb", [128, 8, 16, 128], BF16)
    wv_sb = c.sb("wv_sb", [128, 8, 16, 128], BF16)
    wf32 = c.sb("wf32", [128, 16, 8], F32)
    wf_sb = c.sb("wf_sb", [128, 16, 8], BF16)
    bf_sb = c.sb("bf_sb", [128, 32], F32)
    xt = [c.sb("xt%d" % i, [128, 16, T], F32) for i in range(2)]
    sq = c.sb("sq", [128, 16, T], BF16)
    h = c.sb("h", [128, 16, T], BF16)
    rt = c.sb("rt", [128, T], F32)
    rstd = c.sb("rstd", [128, T], F32)
    ev = [c.sb("ev%d" % i, [128, T], BF16) for i in range(4)]
    lpt = c.sb("lpt", [128, 32], F32)
    lpe = c.sb("lpe", [128, 32], F32)
    lps = [c.sb("lps%d" % i, [128, 32], F32) for i in range(2)]
    ps_stat = c.ps("ps_stat")
    psb = [c.ps("psb%d" % i) for i in range(4)]
    ps_f = c.ps("ps_f")

    P.dma("sp", g_sb[:, :], gmix[:, :], writes=["consts"], key="c0")
    P.dma("sp", ones_bf[:, :], ones_in[:, :], writes=["consts"], key="c0")
    P.dma("sp", bf_sb[:, :], bfb[:, :], writes=["consts"], key="c0")
    P.dma("sp", wf32[:, :, :], wf[:, :, :], writes=["wf32"], key="c1")
    P.add("dve", lambda e: e.tensor_copy(out=wf_sb[:, :, :], in_=wf32[:, :, :]), reads=["wf32"], writes=["wf"])
    P.dma("sp", xt[0][:, :, :], xT.rearrange("(k p) t -> p k t", p=128)[:, :, 0:T],
          writes=[("xt", 0)], key=("xt", 0))
    for nm, src, dst in (("wk", wk, wk_sb), ("wv", wv, wv_sb), ("wq", wq, wq_sb)):
        P.dma("pool", dst[:, :, :, :].rearrange("p m k c -> p m (k c)"),
              src.rearrange("m p x -> p m x"), writes=[nm], key=nm)

    nev = 0
    for t in range(NT):
        xs = xt[t % 2]
        xtok = ("xt", t % 2)
        if t + 1 < NT:
            P.dma("sp", xt[(t + 1) % 2][:, :, :],
                  xT.rearrange("(k p) t -> p k t", p=128)[:, :, (t + 1) * T:(t + 2) * T],
                  writes=[("xt", (t + 1) % 2)], key=("xt", (t + 1) % 2))
        emit_norm(c, xs, xtok, h, "h", g_sb, ones_bf, sq, ps_stat, rt, rstd, t)
        hreads = [("h", k) for k in range(16)]
        for nm, wsb, dst, sc in (("wk", wk_sb, kT_o, 1.0), ("wq", wq_sb, qT_o, SCALE)):
            for m in range(NH):
                pb = psb[nev % 4]
                ptok = ("ps", id(pb))
                for k in range(16):
                    P.add("pe", lambda e, pb=pb, wsb=wsb, m=m, k=k: e.matmul(
                        pb[:, :], lhsT=wsb[:, m, k, :], rhs=h[:, k, :], start=(k == 0), stop=(k == 15)),
                        reads=[nm] + hreads, writes=[ptok], is_mm=True)
                es = ev[nev % 4]
                etok = ("ev", nev % 4)
                P.add("act", lambda e, es=es, pb=pb, sc=sc: e.activation(
                    out=es[:, :], in_=pb[:, :], func=AF.Copy, scale=sc), reads=[ptok], writes=[etok])
                P.dma("pool", dst[m, :, t * T:(t + 1) * T], es[:, :], reads=[etok], key=etok)
                nev += 1
        for tb in range(4):
            for half in range(2):
                pb = psb[nev % 4]
                ptok = ("ps", id(pb))
                for k in range(16):
                    P.add("pe", lambda e, pb=pb, tb=tb, half=half, k=k: e.matmul(
                        pb[:, :], lhsT=h[:, k, tb * 128:(tb + 1) * 128],
                        rhs=wv_sb[:, 4 * half:4 * half + 4, k, :], start=(k == 0), stop=(k == 15)),
                        reads=["wv"] + hreads, writes=[ptok], is_mm=True)
                es = ev[nev % 4]
                etok = ("ev", nev % 4)
                P.add("dve", lambda e, es=es, pb=pb: e.tensor_copy(out=es[:, :], in_=pb[:, :]),
                      reads=[ptok], writes=[etok])
                r0 = t * T + tb * 128
                P.dma("pool", v_o[r0:r0 + 128, half * 512:(half + 1) * 512], es[:, :], reads=[etok], key=etok)
                nev += 1
        ftok = ("ps", id(ps_f))
        for tb in range(4):
            for k in range(16):
                P.add("pe", lambda e, tb=tb, k=k: e.matmul(
                    ps_f[:, tb * 8:(tb + 1) * 8], lhsT=h[:, k, tb * 128:(tb + 1) * 128],
                    rhs=wf_sb[:, k, :], start=(k == 0), stop=(k == 15)),
                    reads=["wf"] + hreads, writes=[ftok], is_mm=True)
        ls = lps[t % 2]
        ltok = ("lps", t % 2)
        P.add("dve", lambda e: e.tensor_tensor(out=lpt[:, :], in0=ps_f[:, 0:32], in1=bf_sb[:, :], op=ALU.add),
              reads=[ftok, "consts"], writes=["lpt"])
        P.add("act", lambda e: e.activation(out=lpe[:, :], in_=lpt[:, :], func=AF.Exp, scale=-1.0),
              reads=["lpt"], writes=["lpe"])
        P.add("act", lambda e, ls=ls: e.activation(out=ls[:, :], in_=lpe[:, :], func=AF.Ln, bias=1.0),
              reads=["lpe"], writes=[ltok])
        P.dma("pool", lp_o[t * T:(t + 1) * T, :].rearrange("(b p) h -> p b h", p=128),
              ls[:, :].rearrange("p (b h) -> p b h", b=4), reads=[ltok], key=ltok)
    return c.finish()


NQT = S // T
NKB = S // 128
VW = 130


def build_ATT():
    c = Ctx()
    P = c.P
    qT = c.dram_in("qT", [128, S], BF16)
    kT = c.dram_in("kT", [128, S], BF16)
    vB = c.dram_in("vB", [128, NKB, 128], BF16)
    lp = c.dram_in("lp", [128, NKB], F32)
    cf32 = c.dram_in("cf32", [4, 128, 128], F32)
    oTok = c.dram_out("oTok", [S, 128], BF16)

    k_sb = c.sb("k_sb", [128, S], BF16)
    v_sb = c.sb("v_sb", [128, NKB, VW], BF16)
    lp_sb = c.sb("lp_sb", [128, NKB], F32)
    cf = c.sb("cf", [128, 4, 128], F32)
    Tsb = c.sb("Tsb", [128, 2], F32)
    Dm = c.sb("Dm", [128, 128], F32)
    Cb = c.sb("Cb", [128, NKB], F32)
    Zf = c.sb("Zf", [128, T], F32)
    aug = [c.sb("aug%d" % i, [128, T], F32) for i in range(2)]
    q_sb = [c.sb("q_sb%d" % i, [128, T], BF16) for i in range(2)]
    e_sb = [c.sb("e_sb%d" % i, [128, T], F32) for i in range(3)]
    p_sb = [c.sb("p_sb%d" % i, [128, T], BF16) for i in range(4)]
    rl_sb = c.sb("rl_sb", [128, 4], F32)
    o_sb = [c.sb("o_sb%d" % i, [128, 4, 128], BF16) for i in range(2)]
    ps_s = [c.ps("ps_s%d" % i) for i in range(3)]
    ps_o = [[c.ps("ps_o%d_%d" % (i, j)) for j in range(2)] for i in range(2)]
    ps_aug = c.ps("ps_aug")
    tri, tris, ones_f, ident_f = cf[:, 0, :], cf[:, 1, :], cf[:, 2, :], cf[:, 3, :]

    P.dma("sp", cf[:, :, :], cf32.rearrange("n p c -> p n c"), writes=["cf"], key="cf")
    P.dma("sp", lp_sb[:, :], lp[:, :], writes=["lp"], key="lp")
    P.dma("sp", q_sb[0][:, :], qT[:, 0:T], writes=[("q", 0)], key=("q", 0))
    P.dma("sp", k_sb[:, :], kT[:, :], writes=["k"], key="k")
    for hv in range(4):
        P.dma("pool", v_sb[:, 32 * hv:32 * (hv + 1), 0:128], vB[:, 32 * hv:32 * (hv + 1), :],
              writes=[("v", hv)], key=("v", hv))
    P.add("dve", lambda e: e.memset(v_sb[:, :, 128:VW], 1.0), writes=["v1"])

    P.add("pe", lambda e: e.matmul(ps_aug[:, 0:2], lhsT=lp_sb[:, :], rhs=ones_f[:, 0:2], start=True, stop=True),
          reads=["lp", "cf"], writes=["ps_aug"], is_mm=True)
    P.add("dve", lambda e: e.tensor_copy(out=Tsb[:, :], in_=ps_aug[:, 0:2]), reads=["ps_aug"], writes=["Tsb"])
    P.add("dve", lambda e: e.tensor_scalar(out=Dm[:, :], in0=tris, scalar1=Tsb[:, 0:1], scalar2=None, op0=ALU.mult),
          reads=["Tsb", "cf"], writes=["Dm"])
    P.add("pe", lambda e: e.matmul(ps_s[0][:, 0:128], lhsT=tri, rhs=lp_sb[:, :], start=True, stop=False),
          reads=["lp", "cf"], writes=[("ps_s", 0)], is_mm=True)
    P.add("pe", lambda e: e.matmul(ps_s[0][:, 0:128], lhsT=ones_f, rhs=Dm[:, :], start=False, stop=True),
          reads=["Dm", "cf"], writes=[("ps_s", 0)], is_mm=True)
    P.add("dve", lambda e: e.tensor_copy(out=Cb[:, :], in_=ps_s[0][:, 0:128]), reads=[("ps_s", 0)], writes=["Cb"])

    cnt = {"s": 0, "p": 0}

    def prep_aug(g):
        in0 = bass.AP(cf, 3 * 128, [[4 * 128, 128], [0, 4], [1, 128]])
        in1 = bass.AP(Cb, 4 * g, [[NKB, 128], [1, 4], [0, 128]])
        outz = Zf[:, :].rearrange("p (j t) -> p j t", j=4)
        P.add("dve", lambda e: e.tensor_tensor(out=outz, in0=in0, in1=in1, op=ALU.mult),
              reads=["cf", "Cb"], writes=["Zf"])
        P.add("pe", lambda e: e.matmul(ps_aug[:, :], lhsT=ones_f, rhs=Zf[:, :], start=True, stop=True),
              reads=["cf", "Zf"], writes=["ps_aug"], is_mm=True)
        a = aug[g % 2]
        P.add("act", lambda e: e.activation(out=a[:, :], in_=ps_aug[:, :], func=AF.Copy, scale=-1.0),
              reads=["ps_aug"], writes=[("aug", g % 2)])

    def tile(g):
        qs = q_sb[g % 2]
        qtok = ("q", g % 2)
        ag = aug[g % 2]
        agtok = ("aug", g % 2)
        if g + 1 < NQT:
            P.dma("sp", q_sb[(g + 1) % 2][:, :], qT[:, (g + 1) * T:(g + 2) * T],
                  writes=[("q", (g + 1) % 2)], key=("q", (g + 1) % 2))
        nkb = 4 * g + 4
        po = ps_o[g % 2]
        potok = [("ps_o", g % 2, 0), ("ps_o", g % 2, 1)]
        slots = []
        started = [False, False]
        last_c = {}
        def emit_st(kb):
            j = kb - 4 * g
            c0 = 128 * j if j > 0 else 0
            si = cnt["s"] % 3
            cnt["s"] += 1
            pss = ps_s[si]
            stok = ("ps_s", si)
            P.add("pe", lambda e: e.matmul(pss[:, c0:T], lhsT=k_sb[:, kb * 128:(kb + 1) * 128], rhs=qs[:, c0:T],
                                           start=True, stop=True), reads=["k", qtok], writes=[stok], is_mm=True)
            es = e_sb[si]
            etok = ("e", si)
            P.add("dve", lambda e: e.tensor_tensor(out=es[:, c0:T], in0=pss[:, c0:T], in1=ag[:, c0:T], op=ALU.add),
                  reads=[stok, agtok], writes=[etok])
            pi = cnt["p"] % 4
            cnt["p"] += 1
            pt = p_sb[pi]
            ptok = ("p", pi)
            P.add("act", lambda e: e.activation(out=pt[:, c0:T], in_=es[:, c0:T], func=AF.Exp,
                                                bias=Cb[:, kb:kb + 1], scale=1.0),
                  reads=[etok, "Cb"], writes=[ptok])
            if j >= 0:
                P.add("pool", lambda e: e.affine_select(out=pt[:, c0:c0 + 128], in_=pt[:, c0:c0 + 128],
                                                        pattern=[[1, 128]], compare_op=ALU.is_ge, fill=0.0,
                                                        base=0, channel_multiplier=-1),
                      reads=[ptok], writes=[ptok])
            slots.append((pt, ptok, j))

        def emit_pv(kb):
            pt, ptok, j = slots[kb]
            for cc in range(max(j, 0), 4):
                bnk = cc // 2
                first = not started[bnk]
                started[bnk] = True
                last = (kb == 4 * g + 2 * bnk + 1) and (cc == 2 * bnk + 1)
                out = po[bnk][:, (cc % 2) * VW:(cc % 2) * VW + 129]
                P.add("pe", lambda e, out=out, cc=cc, first=first, last=last: e.matmul(
                    out, lhsT=pt[:, cc * 128:(cc + 1) * 128], rhs=v_sb[:, kb, 0:129], start=first, stop=last),
                    reads=[("v", kb // 32), "v1", ptok], writes=[potok[bnk]], is_mm=True)

        for i in range(nkb + 2):
            if i < nkb:
                emit_st(i)
            if i == min(8, nkb - 1) and g + 1 < NQT:
                prep_aug(g + 1)
            if i >= 2:
                emit_pv(i - 2)
        os_ = o_sb[g % 2]
        otok = ("o", g % 2)
        for bnk in range(2):
            src = bass.AP(po[bnk], 128, [[512, 128], [VW, 2]])
            P.add("dve", lambda e, src=src, bnk=bnk: e.reciprocal(out=rl_sb[:, 2 * bnk:2 * bnk + 2], in_=src),
                  reads=[potok[bnk]], writes=[("rl", bnk)])
            for h2 in range(2):
                cc = 2 * bnk + h2
                P.add("dve", lambda e, bnk=bnk, h2=h2, cc=cc: e.tensor_scalar(
                    out=os_[:, cc, :], in0=po[bnk][:, h2 * VW:h2 * VW + 128], scalar1=rl_sb[:, cc:cc + 1],
                    scalar2=None, op0=ALU.mult), reads=[potok[bnk], ("rl", bnk)], writes=[otok])
        P.dma("pool", oTok[g * T:(g + 1) * T, :].rearrange("(c p) d -> p c d", p=128), os_[:, :, :],
              reads=[otok], key=otok)

    prep_aug(0)
    for g in range(NQT):
        tile(g)
    return c.finish()


NW = 5


def build_B(final):
    c = Ctx()
    P = c.P
    xT = c.dram_in("xT", [D, TL], F32)
    xh = c.dram_in("xh", [128, 16, 2], F32)
    attnT = c.dram_in("attnT", [1024, TL], BF16)
    wb = c.dram_in("wb", [NGRP, 16, 128, 2048], BF16)
    vecs = c.dram_in("vecs", [128, 5, 16], F32)
    cwd = c.dram_in("cw", [128, 3, 8], F32)
    ones_in = c.dram_in("ones_bf", [128, 128], BF16)
    xo = c.dram_out("xo", [D, TL], F32)

    vec = c.sb("vec", [128, 5, 16], F32)
    cw = c.sb("cw_sb", [128, 3, 8], F32)
    ones_bf = c.sb("ones", [128, 128], BF16)
    xt = c.sb("xt", [128, 16, T], F32)
    sq = c.sb("sq", [128, 16, T], BF16)
    h = c.sb("h", [128, 16, T], BF16)
    rt = c.sb("rt", [128, T], F32)
    rstd = c.sb("rstd", [128, T], F32)
    xh_sb = c.sb("xh_sb", [128, 16, 2], F32)
    hh = c.sb("hh", [128, 16, 2], BF16)
    convy = c.sb("convy", [128, 8, T], BF16)
    at_sb = c.sb("at_sb", [128, 8, T], BF16)
    merged = c.sb("merged", [128, 16, T], BF16)
    u = c.sb("u", [128, 32, T], BF16)
    wt = [c.sb("wt%d" % i, [128, 16, 128], BF16) for i in range(NW)]
    w2t = [c.sb("w2t%d" % i, [128, 32, 128], BF16) for i in range(2)]
    tcc = c.sb("tcc", [128, T], F32)
    tch = c.sb("tch", [128, 2], F32)
    usb = c.sb("usb", [128, T + 2], F32)
    uh = c.sb("uh", [128, 8, 2], F32)
    t1 = c.sb("t1", [128, T], F32)
    sg = c.sb("sg", [128, T], F32)
    sa = c.sb("sa", [128, T], F32)
    m1 = c.sb("m1", [128, T], F32)
    m2 = c.sb("m2", [128, T], F32)
    rr = [c.sb("rr%d" % i, [128, T], BF16) for i in range(2)]
    ps_stat = c.ps("ps_stat")
    psH = c.ps("psH")
    psg = [c.ps("psg%d" % i) for i in range(6)]
    st = {"nw": 0, "nw2": 0, "np": 0, "nr": 0}
    gmix, gmlp, gfin, bgc, bga = (vec[:, i, :] for i in range(5))

    P.dma("sp", vec[:, :, :], vecs[:, :, :], writes=["consts"], key="c0")
    P.dma("sp", cw[:, :, :], cwd[:, :, :], writes=["consts"], key="c0")
    P.dma("sp", ones_bf[:, :], ones_in[:, :], writes=["consts"], key="c0")
    P.dma("sp", xh_sb[:, :, :], xh[:, :, :], writes=["xh"], key="xh")

    XT = [("x", k) for k in range(16)]
    HT = [("h", k) for k in range(16)]

    def load_w(g, m):
        sl = st["nw"] % NW
        st["nw"] += 1
        tok = ("w", sl)
        P.dma("sp", wt[sl][:, :, :].rearrange("p k c -> p (k c)"), wb[g, m], writes=[tok], key=tok)
        return wt[sl], tok

    def bank():
        b = psg[st["np"] % 6]
        st["np"] += 1
        return b, ("psg", id(b))

    def gemm(w, wtok, ks, rhs_of, rtoks, ncols=T, dest=None, koff=0):
        if dest is None:
            b, btok = bank()
            out = b[:, 0:ncols]
        else:
            out, btok = dest
        n = len(ks)
        for i, k in enumerate(ks):
            P.add("pe", lambda e, out=out, k=k, i=i: e.matmul(out, lhsT=w[:, koff + k, :], rhs=rhs_of(k),
                                                              start=(i == 0), stop=(i == n - 1)),
                  reads=[wtok] + rtoks, writes=[btok], is_mm=True)
        return out, btok

    def norm_small():
        for k in range(16):
            P.add("act", lambda e, k=k: e.activation(out=sq[:, k, 0:2], in_=xh_sb[:, k, :], func=AF.Square),
                  reads=["xh"], writes=[("sq", k)])
        for k in range(16):
            P.add("pe", lambda e, k=k: e.matmul(ps_stat[:, 0:2], lhsT=ones_bf[:, :], rhs=sq[:, k, 0:2],
                                                 start=(k == 0), stop=(k == 15)),
                  reads=[("sq", k), "consts"], writes=[("ps", id(ps_stat))], is_mm=True)
        P.add("act", lambda e: e.activation(out=rt[:, 0:2], in_=ps_stat[:, 0:2], func=AF.Sqrt, bias=EPS, scale=1.0 / D),
              reads=[("ps", id(ps_stat))], writes=["rt"])
        P.add("dve", lambda e: e.reciprocal(out=rstd[:, 0:2], in_=rt[:, 0:2]), reads=["rt"], writes=["rstd"])
        for k in range(16):
            P.add("dve", lambda e, k=k: e.scalar_tensor_tensor(out=hh[:, k, :], in0=xh_sb[:, k, :],
                                                                scalar=gmix[:, k:k + 1], in1=rstd[:, 0:2],
                                                                op0=ALU.mult, op1=ALU.mult),
                  reads=["xh", "rstd", "consts"], writes=["hh"])

    def norm_main(gcol, out_h=True):
        for k in range(16):
            P.add("act", lambda e, k=k: e.activation(out=sq[:, k, :], in_=xt[:, k, :], func=AF.Square),
                  reads=[("x", k)], writes=[("sq", k)])
        for k in range(16):
            P.add("pe", lambda e, k=k: e.matmul(ps_stat[:, :], lhsT=ones_bf[:, :], rhs=sq[:, k, :],
                                                 start=(k == 0), stop=(k == 15)),
                  reads=[("sq", k), "consts"], writes=[("ps", id(ps_stat))], is_mm=True)
        P.add("act", lambda e: e.activation(out=rt[:, :], in_=ps_stat[:, :], func=AF.Sqrt, bias=EPS, scale=1.0 / D),
              reads=[("ps", id(ps_stat))], writes=["rt"])
        P.add("dve", lambda e: e.reciprocal(out=rstd[:, :], in_=rt[:, :]), reads=["rt"], writes=["rstd"])
        for k in range(16):
            if out_h:
                P.add("dve", lambda e, k=k: e.scalar_tensor_tensor(out=h[:, k, :], in0=xt[:, k, :],
                                                                    scalar=gcol[:, k:k + 1], in1=rstd[:, :],
                                                                    op0=ALU.mult, op1=ALU.mult),
                      reads=[("x", k), "rstd", "consts"], writes=[("h", k)])
            else:
                P.add("dve", lambda e, k=k: e.scalar_tensor_tensor(out=xt[:, k, :], in0=xt[:, k, :],
                                                                    scalar=gcol[:, k:k + 1], in1=rstd[:, :],
                                                                    op0=ALU.mult, op1=ALU.mult),
                      reads=[("x", k), "rstd", "consts"], writes=[("x", k)])

    def tile(t):
        cs = slice(t * T, (t + 1) * T)
        P.dma("sp", xt[:, :, :], xT.rearrange("(k p) t -> p k t", p=128)[:, :, cs], writes=XT, key="xt")
        P.dma("pool", at_sb[:, :, :], attnT.rearrange("(k p) t -> p k t", p=128)[:, :, cs],
              writes=["at"], key="at")
        if t == 0:
            norm_small()
        norm_main(gmix)
        for j in range(8):
            wcc, tcc_w = load_w(G_CBCC, 8 + j)
            wcv, tcv_w = load_w(G_CVQ, j)
            wcb, tcb_w = load_w(G_CBCC, j)
            pa, patok = gemm(wcc, tcc_w, range(16), lambda k: h[:, k, :], HT)
            P.add("act", lambda e, pa=pa: e.activation(out=tcc[:, :], in_=pa, func=AF.Copy),
                  reads=[patok], writes=["tcc"])
            if t == 0:
                gemm(wcc, tcc_w, range(16), lambda k: hh[:, k, :], ["hh"], dest=(psH[:, 0:2], "psH"))
                P.add("act", lambda e: e.activation(out=tch[:, :], in_=psH[:, 0:2], func=AF.Copy),
                      reads=["psH"], writes=["tch"])
            pv, pvtok = gemm(wcv, tcv_w, range(16), lambda k: h[:, k, :], HT)
            if t == 0:
                gemm(wcv, tcv_w, range(16), lambda k: hh[:, k, :], ["hh"], dest=(psH[:, 2:4], "psH2"))
                P.add("dve", lambda e: e.tensor_tensor(out=usb[:, 0:2], in0=psH[:, 2:4], in1=tch[:, :], op=ALU.mult),
                      reads=["psH2", "tch"], writes=["usb"])
            else:
                P.add("pool", lambda e, j=j: e.tensor_copy(out=usb[:, 0:2], in_=uh[:, j, :]),
                      reads=[("uh", j)], writes=["usb"])
            P.add("dve", lambda e, pv=pv: e.tensor_tensor(out=usb[:, 2:T + 2], in0=pv, in1=tcc[:, :], op=ALU.mult),
                  reads=[pvtok, "tcc", "usb"], writes=["usb"])
            P.add("pool", lambda e, j=j: e.tensor_copy(out=uh[:, j, :], in_=usb[:, T:T + 2]),
                  reads=["usb"], writes=[("uh", j)])
            P.add("dve", lambda e, j=j: e.tensor_scalar(out=t1[:, :], in0=usb[:, 2:T + 2], scalar1=cw[:, 2, j:j + 1],
                                                         scalar2=None, op0=ALU.mult),
                  reads=["usb", "consts"], writes=["t1"])
            for tap, off in ((1, 1), (0, 0)):
                P.add("dve", lambda e, j=j, tap=tap, off=off: e.scalar_tensor_tensor(
                    out=t1[:, :], in0=usb[:, off:off + T], scalar=cw[:, tap, j:j + 1], in1=t1[:, :],
                    op0=ALU.mult, op1=ALU.add), reads=["usb", "t1", "consts"], writes=["t1"])
            pc, pctok = gemm(wcb, tcb_w, range(16), lambda k: h[:, k, :], HT)
            P.add("dve", lambda e, pc=pc, j=j: e.tensor_tensor(out=convy[:, j, :], in0=pc, in1=t1[:, :], op=ALU.mult),
                  reads=[pctok, "t1"], writes=[("cy", j)])
        CY = [("cy", k) for k in range(8)]
        for j in range(16):
            wgc, tgc = load_w(G_GC, j)
            wca, tca = load_w(G_CA, j)
            wga, tga = load_w(G_GA, j)
            p1, p1t = gemm(wgc, tgc, range(16), lambda k: h[:, k, :], HT)
            P.add("act", lambda e, p1=p1, j=j: e.activation(out=sg[:, :], in_=p1, func=AF.Sigmoid,
                                                             bias=bgc[:, j:j + 1], scale=1.0),
                  reads=[p1t, "consts"], writes=["sg"])
            p2, p2t = gemm(wca, tca, range(8), lambda k: convy[:, k, :], CY)
            P.add("dve", lambda e, p2=p2: e.tensor_tensor(out=m1[:, :], in0=p2, in1=sg[:, :], op=ALU.mult),
                  reads=[p2t, "sg"], writes=["m1"])
            p3, p3t = gemm(wga, tga, range(16), lambda k: h[:, k, :], HT)
            P.add("act", lambda e, p3=p3, j=j: e.activation(out=sa[:, :], in_=p3, func=AF.Sigmoid,
                                                             bias=bga[:, j:j + 1], scale=1.0),
                  reads=[p3t, "consts"], writes=["sa"])
            p4, p4t = gemm(wca, tca, range(8), lambda k: at_sb[:, k, :], ["at"], koff=8)
            P.add("dve", lambda e, p4=p4: e.tensor_tensor(out=m2[:, :], in0=p4, in1=sa[:, :], op=ALU.mult),
                  reads=[p4t, "sa"], writes=["m2"])
            P.add("pool", lambda e, j=j: e.tensor_tensor(out=merged[:, j, :], in0=m1[:, :], in1=m2[:, :], op=ALU.add),
                  reads=["m1", "m2"], writes=[("mg", j)])
        MG = [("mg", k) for k in range(16)]
        for j in range(16):
            wm, tm = load_w(G_MIX, j)
            p5, p5t = gemm(wm, tm, range(16), lambda k: merged[:, k, :], MG)
            P.add("dve", lambda e, p5=p5, j=j: e.tensor_tensor(out=xt[:, j, :], in0=p5, in1=xt[:, j, :], op=ALU.add),
                  reads=[p5t, ("x", j)], writes=[("x", j)])
        norm_main(gmlp)
        for half in range(2):
            for f in range(32):
                fg = half * 32 + f
                w1, t1w = load_w(G_FF1 + fg // 16, fg % 16)
                p6, p6t = gemm(w1, t1w, range(16), lambda k: h[:, k, :], HT)
                ri = st["nr"] % 2
                st["nr"] += 1
                rrs = rr[ri]
                P.add("act", lambda e, p6=p6, rrs=rrs: e.activation(out=rrs[:, :], in_=p6, func=AF.Relu),
                      reads=[p6t], writes=[("rr", ri)])
                P.add("dve", lambda e, p6=p6, rrs=rrs, f=f: e.tensor_tensor(out=u[:, f, :], in0=p6, in1=rrs[:, :], op=ALU.mult),
                      reads=[p6t, ("rr", ri)], writes=[("u", f)])
            UT = [("u", f) for f in range(32)]
            for j in range(16):
                s2 = st["nw2"] % 2
                st["nw2"] += 1
                w2 = w2t[s2]
                toks = []
                for q in range(2):
                    tok = ("w2", s2, q)
                    P.dma("sp", w2[:, 16 * q:16 * (q + 1), :].rearrange("p k c -> p (k c)"),
                          wb[G_FF2 + 2 * half + q, j], writes=[tok], key=tok)
                    toks.append(tok)
                b, btok = bank()
                for f in range(32):
                    P.add("pe", lambda e, b=b, w2=w2, f=f: e.matmul(b[:, :], lhsT=w2[:, f, :], rhs=u[:, f, :],
                                                                     start=(f == 0), stop=(f == 31)),
                          reads=toks + UT, writes=[btok], is_mm=True)
                P.add("dve", lambda e, b=b, j=j: e.tensor_tensor(out=xt[:, j, :], in0=b[:, :], in1=xt[:, j, :], op=ALU.add),
                      reads=[btok, ("x", j)], writes=[("x", j)])
        if final:
            norm_main(gfin, out_h=False)
        P.dma("pool", xo.rearrange("(k p) t -> p k t", p=128)[:, :, cs], xt[:, :, :], reads=XT, key="xo")

    for t in range(NT):
        tile(t)
    return c.finish()


class FCtx:
    def __init__(self):
        self.nc = bass.Bass("TRN2", target_bir_lowering=False)
        self.P = Prog()
        self.off = 0
        self.n = 0
        self.banks = [self.nc.alloc_psum_tensor("bank%d" % i, [128, 512], F32) for i in range(8)]
        self.ncc = 0

    def dram_in(self, name, shape, dt):
        return self.nc.dram_tensor(name, list(shape), dt, kind="ExternalInput").ap()

    def dram_out(self, name, shape, dt):
        return self.nc.dram_tensor(name, list(shape), dt, kind="ExternalOutput").ap()

    def dram(self, name, shape, dt):
        return self.nc.dram_tensor(name, list(shape), dt)

    def phase(self):
        self.off = 0
        self.P.barrier()

    def sb(self, name, shape, dt):
        nbytes = int(np.prod(shape[1:])) * (4 if dt == F32 else 2)
        off = (self.off + 63) // 64 * 64
        self.n += 1
        t = self.nc.alloc_sbuf_tensor_at("%s_%d" % (name, self.n), list(shape), dt, offset=off)
        self.off = off + nbytes
        assert self.off <= 206 * 1024, (name, self.off)
        return t

    def allgather(self, src_t, dst_t, rtok, wtok):
        self.ncc += 1
        return self.P.collective(src_t.ap().opt(), dst_t.ap().opt(), reads=[rtok], writes=[wtok],
                                 key=("cc", self.ncc))

    def finish(self):
        P = self.P
        P.finalize()
        print("fused: sems", len(P.sem_names()), "milestones", P.max_counts,
              "ops", {e: len(P.ops[e]) for e in P.ENGS}, flush=True)
        sems = {}
        with contextlib.ExitStack() as st:
            for i, n in enumerate(P.sem_names()):
                sems[n] = st.enter_context(self.nc.semaphore("s%d" % i))
            P.emit(self.nc, sems)
        return self.nc


def f_norm(c, xt, h, gcol, ones_bf, sq, ps_stat, rt, rstd, ncol=T, xtoks=None, htoks=None, inplace=False):
    P = c.P
    for k in range(16):
        P.add("act", lambda e, k=k: e.activation(out=sq[:, k, 0:ncol], in_=xt[:, k, 0:ncol], func=AF.Square),
              reads=[xtoks[k]], writes=[("sq", k)])
    for k in range(16):
        P.add("pe", lambda e, k=k: e.matmul(ps_stat[:, 0:ncol], lhsT=ones_bf[:, :], rhs=sq[:, k, 0:ncol],
                                             start=(k == 0), stop=(k == 15)),
              reads=[("sq", k), "consts"], writes=["ps_stat"], is_mm=True)
    P.add("act", lambda e: e.activation(out=rt[:, 0:ncol], in_=ps_stat[:, 0:ncol], func=AF.Sqrt, bias=EPS, scale=1.0 / D),
          reads=["ps_stat"], writes=["rt"])
    P.add("dve", lambda e: e.reciprocal(out=rstd[:, 0:ncol], in_=rt[:, 0:ncol]), reads=["rt"], writes=["rstd"])
    for k in range(16):
        dst = xt if inplace else h
        P.add("dve", lambda e, k=k, dst=dst: e.scalar_tensor_tensor(out=dst[:, k, 0:ncol], in0=xt[:, k, 0:ncol],
                                                                    scalar=gcol[:, k:k + 1], in1=rstd[:, 0:ncol],
                                                                    op0=ALU.mult, op1=ALU.mult),
              reads=[xtoks[k], "rstd", "consts"], writes=[xtoks[k] if inplace else htoks[k]])


def f_phase_W(c, wsrc, wmy, wall):
    P = c.P
    c.phase()
    stage = [c.sb("stage", [128, 2048], F32) for i in range(3)]
    wt = [c.sb("wt", [128, 16, 16, 128], BF16) for i in range(2)]
    n = 0
    for g in range(GPC):
        w = wt[g % 2]
        for k in range(16):
            st = stage[n % 3]
            tok = ("stage", n % 3)
            P.dma("sp", st[:, :], wsrc[g, k * 128:(k + 1) * 128, :], writes=[tok], key=tok)
            src = st[:, :].rearrange("p (m c) -> p m c", m=16)
            dst = w[:, :, k, :]
            eng = "dve" if n % 2 == 0 else "pool"
            P.add(eng, lambda e, dst=dst, src=src: e.tensor_copy(out=dst, in_=src),
                  reads=[tok], writes=[("wt", g % 2, k)])
            n += 1
        for half in range(2):
            dtok = ("d", "wmy", g, half)
            P.dma("sp", wmy[g][half].ap().rearrange("m p x -> p m x"),
                  w[:, 8 * half:8 * half + 8, :, :].rearrange("p m k c -> p m (k c)"),
                  reads=[("wt", g % 2, k) for k in range(16)], writes=[dtok], key=("wtst", g % 2, half))
            c.allgather(wmy[g][half], wall[g][half], dtok, ("d", "wall", g, half))


def wsl(wall, l, gi, m):
    G = l * NGRP + gi
    r, j = G // GPC, G % GPC
    half, mm = m // 8, m % 8
    return wall[j][half].ap()[r * 8 + mm], ("d", "wall", j, half)


def f_phase_A(c, l, x_ap, xtok_d, wall, wf_in, vecs_in, bfb_in, ones_in, q_my, k_my, v_my, lp_my,
              q_all, k_all, v_all, lp_all):
    P = c.P
    c.phase()
    g_sb = c.sb("g_sb", [128, 16], F32)
    ones_bf = c.sb("ones", [128, 128], BF16)
    w_sb = {nm: c.sb("w" + nm, [128, 8, 16, 128], BF16) for nm in ("q", "k", "v")}
    wf32 = c.sb("wf32", [128, 16, 8], F32)
    wf_sb = c.sb("wf_sb", [128, 16, 8], BF16)
    bf_sb = c.sb("bf_sb", [128, 32], F32)
    xt = [c.sb("xt", [128, 16, T], F32) for i in range(2)]
    sq = c.sb("sq", [128, 16, T], BF16)
    h = c.sb("h", [128, 16, T], BF16)
    rt = c.sb("rt", [128, T], F32)
    rstd = c.sb("rstd", [128, T], F32)
    ev = [c.sb("ev", [128, T], BF16) for i in range(4)]
    lpt = c.sb("lpt", [128, 32], F32)
    lpe = c.sb("lpe", [128, 32], F32)
    lps = [c.sb("lps", [128, 32], F32) for i in range(2)]
    ps_stat, ps_f = c.banks[0], c.banks[1]
    psb = c.banks[2:6]

    P.dma("sp", g_sb[:, :], vecs_in[:, 5 * l + 0, :], writes=["consts"], key="c0")
    P.dma("sp", ones_bf[:, :], ones_in[:, :], writes=["consts"], key="c0")
    P.dma("sp", bf_sb[:, :], bfb_in[l], writes=["consts"], key="c0")
    P.dma("sp", wf32[:, :, :], wf_in[l], writes=["wf32"], key="c1")
    P.add("dve", lambda e: e.tensor_copy(out=wf_sb[:, :, :], in_=wf32[:, :, :]), reads=["wf32"], writes=["wf"])
    xv = x_ap.rearrange("(k p) t -> p k t", p=128)
    XT = [[("xt", i, k) for k in range(16)] for i in range(2)]
    HT = [("h", k) for k in range(16)]
    P.dma("sp", xt[0][:, :, :], xv[:, :, 0:T], reads=[xtok_d], writes=XT[0], key=("xt", 0))
    for nm, gi, m0 in (("k", G_KV, 0), ("v", G_KV, 8), ("q", G_CVQ, 8)):
        for m in range(8):
            src, dtok = wsl(wall, l, gi, m0 + m)
            P.dma("pool", w_sb[nm][:, m, :, :].rearrange("p k c -> p (k c)"), src, reads=[dtok],
                  writes=["w" + nm], key="w" + nm)
    nev = 0
    for t in range(NT):
        if t + 1 < NT:
            P.dma("sp", xt[(t + 1) % 2][:, :, :], xv[:, :, (t + 1) * T:(t + 2) * T], reads=[xtok_d],
                  writes=XT[(t + 1) % 2], key=("xt", (t + 1) % 2))
        f_norm(c, xt[t % 2], h, g_sb, ones_bf, sq, ps_stat, rt, rstd, xtoks=XT[t % 2], htoks=HT)
        for nm, dst, sc in (("k", k_my, 1.0), ("q", q_my, SCALE), ("v", v_my, 1.0)):
            wsb = w_sb[nm]
            for m in range(NH):
                pb = psb[nev % 4]
                ptok = ("psb", nev % 4)
                for k in range(16):
                    P.add("pe", lambda e, pb=pb, wsb=wsb, m=m, k=k: e.matmul(
                        pb[:, :], lhsT=wsb[:, m, k, :], rhs=h[:, k, :], start=(k == 0), stop=(k == 15)),
                        reads=["w" + nm, HT[k]], writes=[ptok], is_mm=True)
                es = ev[nev % 4]
                etok = ("ev", nev % 4)
                P.add("act", lambda e, es=es, pb=pb, sc=sc: e.activation(
                    out=es[:, :], in_=pb[:, :], func=AF.Copy, scale=sc), reads=[ptok], writes=[etok])
                P.dma("pool", dst.ap()[m, :, t * T:(t + 1) * T], es[:, :], reads=[etok],
                      writes=[("d", nm + "_my")], key=etok)
                nev += 1
        for tb in range(4):
            for k in range(16):
                P.add("pe", lambda e, tb=tb, k=k: e.matmul(
                    ps_f[:, tb * 8:(tb + 1) * 8], lhsT=h[:, k, tb * 128:(tb + 1) * 128],
                    rhs=wf_sb[:, k, :], start=(k == 0), stop=(k == 15)),
                    reads=["wf", HT[k]], writes=["ps_f"], is_mm=True)
        ls = lps[t % 2]
        ltok = ("lps", t % 2)
        P.add("dve", lambda e: e.tensor_tensor(out=lpt[:, :], in0=ps_f[:, 0:32], in1=bf_sb[:, :], op=ALU.add),
              reads=["ps_f", "consts"], writes=["lpt"])
        P.add("act", lambda e: e.activation(out=lpe[:, :], in_=lpt[:, :], func=AF.Exp, scale=-1.0),
              reads=["lpt"], writes=["lpe"])
        P.add("act", lambda e, ls=ls: e.activation(out=ls[:, :], in_=lpe[:, :], func=AF.Ln, bias=1.0),
              reads=["lpe"], writes=[ltok])
        P.dma("pool", lp_my.ap()[t * T:(t + 1) * T, :].rearrange("(b p) h -> p b h", p=128),
              ls[:, :].rearrange("p (b h) -> p b h", b=4), reads=[ltok], writes=[("d", "lp_my")], key=ltok)
    for nm, my, al in (("k", k_my, k_all), ("v", v_my, v_all), ("q", q_my, q_all), ("lp", lp_my, lp_all)):
        c.allgather(my, al, ("d", nm + "_my"), ("d", nm + "_all"))


def f_phase_ATT(c, q_all, k_all, v_all, lp_all, cf32_in, cbf_in, oh_in, ohd_in, o_my, o_all):
    P = c.P
    c.phase()
    k_sb = c.sb("k_sb", [128, S], BF16)
    v_sb = c.sb("v_sb", [128, NKB, VW], BF16)
    lp8 = c.sb("lp8", [128, NKB, 8], F32)
    lp_sb = c.sb("lp_sb", [128, NKB], F32)
    cf = c.sb("cf", [128, 4, 128], F32)
    cbf = c.sb("cbf", [128, 2, 128], BF16)
    oh = c.sb("oh", [128, 8], F32)
    ohd = c.sb("ohd", [128, 8, 128], BF16)
    Tsb = c.sb("Tsb", [128, 2], F32)
    Dm = c.sb("Dm", [128, 128], F32)
    Cb = c.sb("Cb", [128, NKB], F32)
    Zf = c.sb("Zf", [128, T], F32)
    aug = [c.sb("aug", [128, T], F32) for i in range(2)]
    sel_in = [c.sb("sel_in", [128, 8, T], BF16) for i in range(2)]
    q_sb = [c.sb("q_sb", [128, T], BF16) for i in range(2)]
    e_sb = [c.sb("e_sb", [128, T], F32) for i in range(3)]
    p_sb = [c.sb("p_sb", [128, T], BF16) for i in range(4)]
    rl_sb = c.sb("rl_sb", [128, 4], F32)
    o_sb = [c.sb("o_sb", [128, 4, 128], BF16) for i in range(2)]
    oT_sb = [c.sb("oT_sb", [128, T], BF16) for i in range(2)]
    ps_s = c.banks[0:3]
    ps_o = [c.banks[3:5], c.banks[5:7]]
    ps_aug = c.banks[7]
    tri, tris, ones_f, ident_f = cf[:, 0, :], cf[:, 1, :], cf[:, 2, :], cf[:, 3, :]
    ident_b = cbf[:, 1, :]
    cnt = {"s": 0, "p": 0, "sel": 0}

    P.dma("sp", cf[:, :, :], cf32_in.rearrange("n p c -> p n c"), writes=["cf"], key="cf")
    P.dma("sp", cbf[:, :, :], cbf_in.rearrange("n p c -> p n c"), writes=["cbf"], key="cf")
    P.dma("sp", oh[:, :], oh_in[:, :], writes=["oh"], key="cf")
    P.dma("sp", ohd[:, :, :], ohd_in.rearrange("n p c -> p n c"), writes=["ohd"], key="cf")
    for i in range(4):
        P.dma("pool", lp8[:, 32 * i:32 * (i + 1), :],
              lp_all.ap()[4096 * i:4096 * (i + 1), :].rearrange("(b p) h -> p b h", p=128),
              reads=[("d", "lp_all")], writes=[("lp8", i)], key=("lp8", i))
    P.add("dve", lambda e: e.memset(v_sb[:, :, 128:VW], 1.0), writes=["v1"])
    for hh in range(8):
        if hh == 0:
            P.add("dve", lambda e: e.tensor_scalar(out=lp_sb[:, :], in0=lp8[:, :, 0], scalar1=oh[:, 0:1], scalar2=None,
                                                   op0=ALU.mult), reads=[("lp8", i) for i in range(4)] + ["oh"],
                  writes=["lp"])
        else:
            P.add("dve", lambda e, hh=hh: e.scalar_tensor_tensor(out=lp_sb[:, :], in0=lp8[:, :, hh],
                                                                  scalar=oh[:, hh:hh + 1], in1=lp_sb[:, :],
                                                                  op0=ALU.mult, op1=ALU.add),
                  reads=["lp", "oh"], writes=["lp"])

    def load_sel(src_all, dtok, r, piece):
        si = cnt["sel"] % 2
        cnt["sel"] += 1
        tok = ("sel", si)
        P.dma("sp", sel_in[si][:, :, :],
              src_all.ap()[r * 8:(r + 1) * 8, :, piece * T:(piece + 1) * T].rearrange("h p t -> p h t"),
              reads=[dtok], writes=[tok], key=tok)
        return sel_in[si], tok

    def next_bank():
        si = cnt["s"] % 3
        cnt["s"] += 1
        cnt["last_si"] = si
        return ps_s[si], ("ps_s", si)

    for r in range(8):
        for piece in range(4):
            sin, stok = load_sel(k_all, ("d", "k_all"), r, piece)
            pb, ptok = next_bank()
            for hh in range(8):
                P.add("pe", lambda e, pb=pb, sin=sin, hh=hh: e.matmul(pb[:, :], lhsT=ohd[:, hh, :], rhs=sin[:, hh, :],
                                                                       start=(hh == 0), stop=(hh == 7)),
                      reads=["ohd", stok], writes=[ptok], is_mm=True)
            c0 = r * TL + piece * T
            P.add("act", lambda e, pb=pb, c0=c0: e.activation(out=k_sb[:, c0:c0 + T], in_=pb[:, :], func=AF.Copy),
                  reads=[ptok], writes=[("k", c0 // T)])
    for r in range(8):
        for piece in range(4):
            sin, stok = load_sel(v_all, ("d", "v_all"), r, piece)
            pb, ptok = next_bank()
            for j in range(4):
                for hh in range(8):
                    P.add("pe", lambda e, pb=pb, sin=sin, hh=hh, j=j: e.matmul(
                        pb[:, j * 128:(j + 1) * 128], lhsT=sin[:, hh, j * 128:(j + 1) * 128], rhs=ohd[:, hh, :],
                        start=(j == 0 and hh == 0), stop=(j == 3 and hh == 7)),
                        reads=["ohd", stok], writes=[ptok], is_mm=True)
            b0 = (r * TL + piece * T) // 128
            P.add("dve", lambda e, pb=pb, b0=b0: e.tensor_copy(
                out=v_sb[:, b0:b0 + 4, 0:128], in_=pb[:, :].rearrange("p (j d) -> p j d", j=4)),
                reads=[ptok], writes=[("v", b0 // 4)])

    P.add("pe", lambda e: e.matmul(ps_aug[:, 0:2], lhsT=lp_sb[:, :], rhs=ones_f[:, 0:2], start=True, stop=True),
          reads=["lp", "cf"], writes=["ps_aug"], is_mm=True)
    P.add("dve", lambda e: e.tensor_copy(out=Tsb[:, :], in_=ps_aug[:, 0:2]), reads=["ps_aug"], writes=["Tsb"])
    P.add("dve", lambda e: e.tensor_scalar(out=Dm[:, :], in0=tris, scalar1=Tsb[:, 0:1], scalar2=None, op0=ALU.mult),
          reads=["Tsb", "cf"], writes=["Dm"])
    pbc, pbctok = next_bank()
    P.add("pe", lambda e: e.matmul(pbc[:, 0:128], lhsT=tri, rhs=lp_sb[:, :], start=True, stop=False),
          reads=["lp", "cf"], writes=[pbctok], is_mm=True)
    P.add("pe", lambda e: e.matmul(pbc[:, 0:128], lhsT=ones_f, rhs=Dm[:, :], start=False, stop=True),
          reads=["Dm", "cf"], writes=[pbctok], is_mm=True)
    P.add("dve", lambda e: e.tensor_copy(out=Cb[:, :], in_=pbc[:, 0:128]), reads=[pbctok], writes=["Cb"])

    def prep_tile(g):
        r, piece = g // 4, g % 4
        sin, stok = load_sel(q_all, ("d", "q_all"), r, piece)
        pb, ptok = next_bank()
        for hh in range(8):
            P.add("pe", lambda e, hh=hh: e.matmul(pb[:, :], lhsT=ohd[:, hh, :], rhs=sin[:, hh, :],
                                                   start=(hh == 0), stop=(hh == 7)),
                  reads=["ohd", stok], writes=[ptok], is_mm=True)
        qs = q_sb[g % 2]
        P.add("dve", lambda e: e.tensor_copy(out=qs[:, :], in_=pb[:, :]), reads=[ptok], writes=[("q", g % 2)])
        in0 = bass.AP(cf, 3 * 128, [[4 * 128, 128], [0, 4], [1, 128]])
        in1 = bass.AP(Cb, 4 * g, [[NKB, 128], [1, 4], [0, 128]])
        outz = Zf[:, :].rearrange("p (j t) -> p j t", j=4)
        P.add("dve", lambda e: e.tensor_tensor(out=outz, in0=in0, in1=in1, op=ALU.mult),
              reads=["cf", "Cb"], writes=["Zf"])
        P.add("pe", lambda e: e.matmul(ps_aug[:, :], lhsT=ones_f, rhs=Zf[:, :], start=True, stop=True),
              reads=["cf", "Zf"], writes=["ps_aug"], is_mm=True)
        a = aug[g % 2]
        P.add("act", lambda e: e.activation(out=a[:, :], in_=ps_aug[:, :], func=AF.Copy, scale=-1.0),
              reads=["ps_aug"], writes=[("aug", g % 2)])

    def tile(g):
        qs = q_sb[g % 2]
        qtok = ("q", g % 2)
        ag = aug[g % 2]
        agtok = ("aug", g % 2)
        nkb = 4 * g + 4
        po = ps_o[g % 2]
        potok = [("ps_o", g % 2, 0), ("ps_o", g % 2, 1)]
        slots = []
        started = [False, False]

        def emit_st(kb):
            j = kb - 4 * g
            c0 = 128 * j if j > 0 else 0
            pss, stok = next_bank()
            si = cnt["last_si"]
            P.add("pe", lambda e: e.matmul(pss[:, c0:T], lhsT=k_sb[:, kb * 128:(kb + 1) * 128], rhs=qs[:, c0:T],
                                           start=True, stop=True), reads=[("k", kb // 4), qtok], writes=[stok],
                  is_mm=True)
            es = e_sb[si]
            etok = ("e", si)
            P.add("dve", lambda e: e.tensor_tensor(out=es[:, c0:T], in0=pss[:, c0:T], in1=ag[:, c0:T], op=ALU.add),
                  reads=[stok, agtok], writes=[etok])
            pi = cnt["p"] % 4
            cnt["p"] += 1
            pt = p_sb[pi]
            ptok = ("p", pi)
            P.add("act", lambda e: e.activation(out=pt[:, c0:T], in_=es[:, c0:T], func=AF.Exp,
                                                bias=Cb[:, kb:kb + 1], scale=1.0),
                  reads=[etok, "Cb"], writes=[ptok])
            if j >= 0:
                P.add("pool", lambda e: e.affine_select(out=pt[:, c0:c0 + 128], in_=pt[:, c0:c0 + 128],
                                                        pattern=[[1, 128]], compare_op=ALU.is_ge, fill=0.0,
                                                        base=0, channel_multiplier=-1),
                      reads=[ptok], writes=[ptok])
            slots.append((pt, ptok, j))

        def emit_pv(kb):
            pt, ptok, j = slots[kb]
            for cc in range(max(j, 0), 4):
                bnk = cc // 2
                first = not started[bnk]
                started[bnk] = True
                last = (kb == 4 * g + 2 * bnk + 1) and (cc == 2 * bnk + 1)
                out = po[bnk][:, (cc % 2) * VW:(cc % 2) * VW + 129]
                P.add("pe", lambda e, out=out, cc=cc, first=first, last=last: e.matmul(
                    out, lhsT=pt[:, cc * 128:(cc + 1) * 128], rhs=v_sb[:, kb, 0:129], start=first, stop=last),
                    reads=[("v", kb // 4), "v1", ptok], writes=[potok[bnk]], is_mm=True)

        for i in range(nkb + 2):
            if i < nkb:
                emit_st(i)
            if i == min(8, nkb - 1) and g + 1 < NQT:
                prep_tile(g + 1)
            if i >= 2:
                emit_pv(i - 2)
        os_ = o_sb[g % 2]
        otok = ("o", g % 2)
        for bnk in range(2):
            src = bass.AP(po[bnk], 128, [[512, 128], [VW, 2]])
            P.add("dve", lambda e, src=src, bnk=bnk: e.reciprocal(out=rl_sb[:, 2 * bnk:2 * bnk + 2], in_=src),
                  reads=[potok[bnk]], writes=[("rl", bnk)])
            for h2 in range(2):
                cc = 2 * bnk + h2
                P.add("dve", lambda e, bnk=bnk, h2=h2, cc=cc: e.tensor_scalar(
                    out=os_[:, cc, :], in0=po[bnk][:, h2 * VW:h2 * VW + 128], scalar1=rl_sb[:, cc:cc + 1],
                    scalar2=None, op0=ALU.mult), reads=[potok[bnk], ("rl", bnk)], writes=[otok])
        pb, ptok = next_bank()
        for cc in range(4):
            P.add("pe", lambda e, cc=cc: e.matmul(pb[:, cc * 128:(cc + 1) * 128], lhsT=os_[:, cc, :], rhs=ident_b,
                                                   start=(cc == 0), stop=(cc == 3)),
                  reads=[otok, "cbf"], writes=[ptok], is_mm=True)
        ots = oT_sb[g % 2]
        P.add("act", lambda e: e.activation(out=ots[:, :], in_=pb[:, :], func=AF.Copy), reads=[ptok],
              writes=[("oT", g % 2)])
        P.dma("pool", o_my.ap()[:, g * T:(g + 1) * T], ots[:, :], reads=[("oT", g % 2)], writes=[("d", "o_my")],
              key=("oT", g % 2))

    prep_tile(0)
    for g in range(NQT):
        tile(g)
    c.allgather(o_my, o_all, ("d", "o_my"), ("d", "o_all"))


def f_phase_B(c, l, final, x_ap, xtok_d, x_dst_ap, xh_in, xh_all, xh_my, o_all, wall, vecs_in, cw_in, ones_in,
              oh_in, ohp_in):
    P = c.P
    c.phase()
    vec = c.sb("vec", [128, 5, 16], F32)
    cw = c.sb("cw_sb", [128, 3, 8], F32)
    ones_bf = c.sb("ones", [128, 128], BF16)
    oh = c.sb("oh", [128, 8], F32)
    ohp = c.sb("ohp", [128, 8], F32)
    xt = c.sb("xt", [128, 16, T], F32)
    sq = c.sb("sq", [128, 16, T], BF16)
    h = c.sb("h", [128, 16, T], BF16)
    rt = c.sb("rt", [128, T], F32)
    rstd = c.sb("rstd", [128, T], F32)
    xh_sb = c.sb("xh_sb", [128, 16, 2], F32)
    xh8 = c.sb("xh8", [128, 8, 32], F32)
    hh = c.sb("hh", [128, 16, 2], BF16)
    convy = c.sb("convy", [128, 8, T], BF16)
    at_sb = c.sb("at_sb", [128, 8, T], BF16)
    merged = c.sb("merged", [128, 16, T], BF16)
    u = c.sb("u", [128, 32, T], BF16)
    wt = [c.sb("wt", [128, 16, 128], BF16) for i in range(NW)]
    w2t = [c.sb("w2t", [128, 32, 128], BF16) for i in range(2)]
    tcc = c.sb("tcc", [128, T], F32)
    tch = c.sb("tch", [128, 2], F32)
    usb = c.sb("usb", [128, T + 2], F32)
    uh = c.sb("uh", [128, 8, 2], F32)
    t1 = c.sb("t1", [128, T], F32)
    sg = c.sb("sg", [128, T], F32)
    sa = c.sb("sa", [128, T], F32)
    m1 = c.sb("m1", [128, T], F32)
    m2 = c.sb("m2", [128, T], F32)
    rr = [c.sb("rr", [128, T], BF16) for i in range(2)]
    ps_stat, psH = c.banks[0], c.banks[1]
    psg = c.banks[2:8]
    st = {"nw": 0, "nw2": 0, "np": 0, "nr": 0}
    gmix, gmlp, gfin, bgc, bga = (vec[:, i, :] for i in range(5))
    XT = [("x", k) for k in range(16)]
    HT = [("h", k) for k in range(16)]
    MG = [("mg", k) for k in range(16)]

    P.dma("sp", vec[:, :, :], vecs_in[:, 5 * l:5 * l + 5, :], writes=["consts"], key="c0")
    P.dma("sp", cw[:, :, :], cw_in[l], writes=["consts"], key="c0")
    P.dma("sp", ones_bf[:, :], ones_in[:, :], writes=["consts"], key="c0")
    P.dma("sp", oh[:, :], oh_in[:, :], writes=["consts"], key="c0")
    P.dma("sp", ohp[:, :], ohp_in[:, :], writes=["consts"], key="c0")
    if l == 0:
        P.dma("sp", xh_sb[:, :, :], xh_in[:, :, :], writes=["xh"], key="xh")
    else:
        P.dma("sp", xh8[:, :, :], xh_all.ap().rearrange("(j p) x -> p j x", p=128), reads=[("d", "xh_all")],
              writes=["xh8"], key="xh")
        xhf = xh_sb[:, :, :].rearrange("p k i -> p (k i)")
        for j in range(8):
            if j == 0:
                P.add("dve", lambda e: e.tensor_scalar(out=xhf, in0=xh8[:, 0, :], scalar1=ohp[:, 0:1], scalar2=None,
                                                       op0=ALU.mult), reads=["xh8", "consts"], writes=["xh"])
            else:
                P.add("dve", lambda e, j=j: e.scalar_tensor_tensor(out=xhf, in0=xh8[:, j, :], scalar=ohp[:, j:j + 1],
                                                                    in1=xhf, op0=ALU.mult, op1=ALU.add),
                      reads=["xh8", "xh", "consts"], writes=["xh"])

    def load_w(g, m):
        sl = st["nw"] % NW
        st["nw"] += 1
        tok = ("w", sl)
        src, dtok = wsl(wall, l, g, m)
        P.dma("sp", wt[sl][:, :, :].rearrange("p k c -> p (k c)"), src, reads=[dtok], writes=[tok], key=tok)
        return wt[sl], tok

    def bank():
        i = st["np"] % 6
        st["np"] += 1
        return psg[i], ("psg", i)

    def gemm(w, wtok, ks, rhs_of, rtoks, ncols=T, dest=None, koff=0):
        if dest is None:
            b, btok = bank()
            out = b[:, 0:ncols]
        else:
            out, btok = dest
        n = len(ks)
        for i, k in enumerate(ks):
            P.add("pe", lambda e, out=out, k=k, i=i: e.matmul(out, lhsT=w[:, koff + k, :], rhs=rhs_of(k),
                                                              start=(i == 0), stop=(i == n - 1)),
                  reads=[wtok] + rtoks, writes=[btok], is_mm=True)
        return out, btok

    xv = x_ap.rearrange("(k p) t -> p k t", p=128)
    xov = x_dst_ap.rearrange("(k p) t -> p k t", p=128)

    def tile(t):
        cs = slice(t * T, (t + 1) * T)
        P.dma("sp", xt[:, :, :], xv[:, :, cs], reads=[xtok_d], writes=XT, key="xt")
        for hd in range(8):
            si = hd % 2
            cand = merged[:, 8 * si:8 * si + 8, :]
            ctoks = MG[8 * si:8 * si + 8]
            P.dma("pool", cand, o_all.ap()[hd * 128:(hd + 1) * 128, :].rearrange("p (r x) -> p r x", r=8)[:, :, cs],
                  reads=[("d", "o_all")], writes=ctoks, key=("cand", si))
            for r in range(8):
                if r == 0:
                    P.add("dve", lambda e, hd=hd, cand=cand: e.tensor_scalar(
                        out=at_sb[:, hd, :], in0=cand[:, 0, :], scalar1=oh[:, 0:1], scalar2=None, op0=ALU.mult),
                        reads=ctoks + ["consts"], writes=[("at", hd)])
                else:
                    P.add("dve", lambda e, hd=hd, cand=cand, r=r: e.scalar_tensor_tensor(
                        out=at_sb[:, hd, :], in0=cand[:, r, :], scalar=oh[:, r:r + 1], in1=at_sb[:, hd, :],
                        op0=ALU.mult, op1=ALU.add), reads=ctoks + ["consts", ("at", hd)], writes=[("at", hd)])
        AT = [("at", k) for k in range(8)]
        if t == 0:
            f_norm(c, xh_sb, hh, gmix, ones_bf, sq, ps_stat, rt, rstd, ncol=2, xtoks=["xh"] * 16, htoks=["hh"] * 16)
        f_norm(c, xt, h, gmix, ones_bf, sq, ps_stat, rt, rstd, xtoks=XT, htoks=HT)
        for j in range(8):
            wcc, tcc_w = load_w(G_CBCC, 8 + j)
            wcv, tcv_w = load_w(G_CVQ, j)
            wcb, tcb_w = load_w(G_CBCC, j)
            pa, patok = gemm(wcc, tcc_w, range(16), lambda k: h[:, k, :], HT)
            P.add("act", lambda e, pa=pa: e.activation(out=tcc[:, :], in_=pa, func=AF.Copy),
                  reads=[patok], writes=["tcc"])
            if t == 0:
                gemm(wcc, tcc_w, range(16), lambda k: hh[:, k, :], ["hh"], dest=(psH[:, 0:2], "psH"))
                P.add("act", lambda e: e.activation(out=tch[:, :], in_=psH[:, 0:2], func=AF.Copy),
                      reads=["psH"], writes=["tch"])
            pv, pvtok = gemm(wcv, tcv_w, range(16), lambda k: h[:, k, :], HT)
            if t == 0:
                gemm(wcv, tcv_w, range(16), lambda k: hh[:, k, :], ["hh"], dest=(psH[:, 2:4], "psH2"))
                P.add("dve", lambda e: e.tensor_tensor(out=usb[:, 0:2], in0=psH[:, 2:4], in1=tch[:, :], op=ALU.mult),
                      reads=["psH2", "tch"], writes=["usb"])
            else:
                P.add("pool", lambda e, j=j: e.tensor_copy(out=usb[:, 0:2], in_=uh[:, j, :]),
                      reads=[("uh", j)], writes=["usb"])
            P.add("dve", lambda e, pv=pv: e.tensor_tensor(out=usb[:, 2:T + 2], in0=pv, in1=tcc[:, :], op=ALU.mult),
                  reads=[pvtok, "tcc", "usb"], writes=["usb"])
            P.add("pool", lambda e, j=j: e.tensor_copy(out=uh[:, j, :], in_=usb[:, T:T + 2]),
                  reads=["usb"], writes=[("uh", j)])
            P.add("dve", lambda e, j=j: e.tensor_scalar(out=t1[:, :], in0=usb[:, 2:T + 2], scalar1=cw[:, 2, j:j + 1],
                                                         scalar2=None, op0=ALU.mult),
                  reads=["usb", "consts"], writes=["t1"])
            for tap, off in ((1, 1), (0, 0)):
                P.add("dve", lambda e, j=j, tap=tap, off=off: e.scalar_tensor_tensor(
                    out=t1[:, :], in0=usb[:, off:off + T], scalar=cw[:, tap, j:j + 1], in1=t1[:, :],
                    op0=ALU.mult, op1=ALU.add), reads=["usb", "t1", "consts"], writes=["t1"])
            pc, pctok = gemm(wcb, tcb_w, range(16), lambda k: h[:, k, :], HT)
            P.add("dve", lambda e, pc=pc, j=j: e.tensor_tensor(out=convy[:, j, :], in0=pc, in1=t1[:, :], op=ALU.mult),
                  reads=[pctok, "t1"], writes=[("cy", j)])
        CY = [("cy", k) for k in range(8)]
        for j in range(16):
            wgc, tgc = load_w(G_GC, j)
            wca, tca = load_w(G_CA, j)
            wga, tga = load_w(G_GA, j)
            p1, p1t = gemm(wgc, tgc, range(16), lambda k: h[:, k, :], HT)
            P.add("act", lambda e, p1=p1, j=j: e.activation(out=sg[:, :], in_=p1, func=AF.Sigmoid,
                                                             bias=bgc[:, j:j + 1], scale=1.0),
                  reads=[p1t, "consts"], writes=["sg"])
            p2, p2t = gemm(wca, tca, range(8), lambda k: convy[:, k, :], CY)
            P.add("dve", lambda e, p2=p2: e.tensor_tensor(out=m1[:, :], in0=p2, in1=sg[:, :], op=ALU.mult),
                  reads=[p2t, "sg"], writes=["m1"])
            p3, p3t = gemm(wga, tga, range(16), lambda k: h[:, k, :], HT)
            P.add("act", lambda e, p3=p3, j=j: e.activation(out=sa[:, :], in_=p3, func=AF.Sigmoid,
                                                             bias=bga[:, j:j + 1], scale=1.0),
                  reads=[p3t, "consts"], writes=["sa"])
            p4, p4t = gemm(wca, tca, range(8), lambda k: at_sb[:, k, :], AT, koff=8)
            P.add("dve", lambda e, p4=p4: e.tensor_tensor(out=m2[:, :], in0=p4, in1=sa[:, :], op=ALU.mult),
                  reads=[p4t, "sa"], writes=["m2"])
            P.add("pool", lambda e, j=j: e.tensor_tensor(out=merged[:, j, :], in0=m1[:, :], in1=m2[:, :], op=ALU.add),
                  reads=["m1", "m2"], writes=[("mg", j)])
        for j in range(16):
            wm, tm = load_w(G_MIX, j)
            p5, p5t = gemm(wm, tm, range(16), lambda k: merged[:, k, :], MG)
            P.add("dve", lambda e, p5=p5, j=j: e.tensor_tensor(out=xt[:, j, :], in0=p5, in1=xt[:, j, :], op=ALU.add),
                  reads=[p5t, ("x", j)], writes=[("x", j)])
        f_norm(c, xt, h, gmlp, ones_bf, sq, ps_stat, rt, rstd, xtoks=XT, htoks=HT)
        for half in range(2):
            for f in range(32):
                fg = half * 32 + f
                w1, t1w = load_w(G_FF1 + fg // 16, fg % 16)
                p6, p6t = gemm(w1, t1w, range(16), lambda k: h[:, k, :], HT)
                ri = st["nr"] % 2
                st["nr"] += 1
                rrs = rr[ri]
                P.add("act", lambda e, p6=p6, rrs=rrs: e.activation(out=rrs[:, :], in_=p6, func=AF.Relu),
                      reads=[p6t], writes=[("rr", ri)])
                P.add("dve", lambda e, p6=p6, rrs=rrs, f=f: e.tensor_tensor(out=u[:, f, :], in0=p6, in1=rrs[:, :],
                                                                            op=ALU.mult),
                      reads=[p6t, ("rr", ri)], writes=[("u", f)])
            UT = [("u", f) for f in range(32)]
            for j in range(16):
                s2 = st["nw2"] % 2
                st["nw2"] += 1
                w2 = w2t[s2]
                toks = []
                for q in range(2):
                    tok = ("w2", s2, q)
                    src, dtok = wsl(wall, l, G_FF2 + 2 * half + q, j)
                    P.dma("sp", w2[:, 16 * q:16 * (q + 1), :].rearrange("p k c -> p (k c)"), src, reads=[dtok],
                          writes=[tok], key=tok)
                    toks.append(tok)
                b, btok = bank()
                for f in range(32):
                    P.add("pe", lambda e, b=b, w2=w2, f=f: e.matmul(b[:, :], lhsT=w2[:, f, :], rhs=u[:, f, :],
                                                                     start=(f == 0), stop=(f == 31)),
                          reads=toks + UT, writes=[btok], is_mm=True)
                P.add("dve", lambda e, b=b, j=j: e.tensor_tensor(out=xt[:, j, :], in0=b[:, :], in1=xt[:, j, :], op=ALU.add),
                      reads=[btok, ("x", j)], writes=[("x", j)])
        if not final and t == NT - 1:
            P.dma("pool", xh_my.ap().rearrange("p (k i) -> p k i", i=2), xt[:, :, T - 2:T], reads=XT,
                  writes=[("d", "xh_my")], key="xhst")
        if final:
            f_norm(c, xt, h, gfin, ones_bf, sq, ps_stat, rt, rstd, xtoks=XT, htoks=HT, inplace=True)
        P.dma("pool", xov[:, :, cs], xt[:, :, :], reads=XT, writes=[("d", "xdst")], key="xo")

    for t in range(NT):
        tile(t)
    if not final:
        c.allgather(xh_my, xh_all, ("d", "xh_my"), ("d", "xh_all"))


def build_fused(depth=2):
    c = FCtx()
    wsrc = c.dram_in("wsrc", [GPC, 2048, 2048], F32)
    xT = c.dram_in("xT", [D, TL], F32)
    xh0 = c.dram_in("xh0", [128, 16, 2], F32)
    wf_in = c.dram_in("wf", [depth, 128, 16, 8], F32)
    bfb_in = c.dram_in("bfb", [depth, 128, 32], F32)
    vecs_in = c.dram_in("vecs", [128, 5 * depth, 16], F32)
    cw_in = c.dram_in("cw", [depth, 128, 3, 8], F32)
    ones_in = c.dram_in("ones_bf", [128, 128], BF16)
    cf32_in = c.dram_in("cf32", [4, 128, 128], F32)
    cbf_in = c.dram_in("cbf", [2, 128, 128], BF16)
    oh_in = c.dram_in("oh", [128, 8], F32)
    ohp_in = c.dram_in("ohp", [128, 8], F32)
    ohd_in = c.dram_in("ohd", [8, 128, 128], BF16)
    xo = c.dram_out("xo", [D, TL], F32)
    wmy = [[c.dram("wmy%d_%d" % (g, hf), [8, 128, 2048], BF16) for hf in range(2)] for g in range(GPC)]
    wall = [[c.dram("wall%d_%d" % (g, hf), [64, 128, 2048], BF16) for hf in range(2)] for g in range(GPC)]
    qkv_my = {nm: c.dram(nm + "_my", [8, 128, TL], BF16) for nm in ("q", "k", "v")}
    qkv_all = {nm: c.dram(nm + "_all", [64, 128, TL], BF16) for nm in ("q", "k", "v")}
    lp_my = c.dram("lp_my", [TL, 8], F32)
    lp_all = c.dram("lp_all", [S, 8], F32)
    o_my = c.dram("o_my", [128, S], BF16)
    o_all = c.dram("o_all", [1024, S], BF16)
    x_cur = c.dram("x_cur", [D, TL], F32)
    xh_my = c.dram("xh_my", [128, 32], F32)
    xh_all = c.dram("xh_all", [1024, 32], F32)

    f_phase_W(c, wsrc, wmy, wall)
    for l in range(depth):
        final = (l == depth - 1)
        x_ap = xT if l == 0 else x_cur.ap()
        xtok = ("d", "xin") if l == 0 else ("d", "xdst")
        f_phase_A(c, l, x_ap, xtok, wall, wf_in, vecs_in, bfb_in, ones_in, qkv_my["q"], qkv_my["k"], qkv_my["v"],
                  lp_my, qkv_all["q"], qkv_all["k"], qkv_all["v"], lp_all)
        f_phase_ATT(c, qkv_all["q"], qkv_all["k"], qkv_all["v"], lp_all, cf32_in, cbf_in, oh_in, ohd_in, o_my, o_all)
        f_phase_B(c, l, final, x_ap, xtok, xo if final else x_cur.ap(), xh0, xh_all, xh_my, o_all, wall, vecs_in,
                  cw_in, ones_in, oh_in, ohp_in)
    return c.finish()


def kernel_fused(x, g_mix, w_in, b_f, b_gate, conv_w, w_conv_out, w_attn_out, w_mix_out, g_mlp, w_ff1, w_ff2, g_final):
    f32 = np.float32
    bf = ml_dtypes.bfloat16
    x = np.asarray(x, f32)
    depth = w_in.shape[0]
    groups = []
    for l in range(depth):
        groups += [np.asarray(g, f32) for g in weight_groups(w_in, w_conv_out, w_attn_out, w_mix_out, w_ff1, w_ff2, l)]
    ng = len(groups)
    wf = np.ascontiguousarray(np.stack([np.asarray(w_in[l][:, 6144:6152], f32).reshape(16, 128, 8).transpose(1, 0, 2)
                                        for l in range(depth)]))
    bfb = np.ascontiguousarray(np.stack([np.tile(np.asarray(b_f[l], f32), (128, 4)) for l in range(depth)]))
    vl = []
    for l in range(depth):
        vl += [colvec(np.asarray(g_mix[l], f32)), colvec(np.asarray(g_mlp[l], f32)), colvec(np.asarray(g_final, f32)),
               colvec(np.asarray(b_gate[l][:D], f32)), colvec(np.asarray(b_gate[l][D:], f32))]
    vecs = np.ascontiguousarray(np.stack(vl, axis=1))
    cw = np.ascontiguousarray(np.stack([np.asarray(conv_w[l], f32).reshape(3, 8, 128).transpose(2, 0, 1)
                                        for l in range(depth)]))
    cf32 = att_consts()
    cbf = np.stack([np.ones((128, 128), f32), np.eye(128, dtype=f32)]).astype(bf)
    ones_bf = np.ones((128, 128), bf)
    ims = []
    for r in range(NCORES):
        ids = [(r * GPC + j) % ng for j in range(GPC)]
        oh = np.zeros((128, 8), f32)
        oh[:, r] = 1
        ohp = np.zeros((128, 8), f32)
        if r > 0:
            ohp[:, r - 1] = 1
        ohd = np.zeros((8, 128, 128), f32)
        ohd[r] = np.eye(128, dtype=f32)
        halo = np.zeros((D, 2), f32) if r == 0 else x[0, r * TL - 2:r * TL, :].T
        ims.append({"wsrc": np.ascontiguousarray(np.stack([groups[i] for i in ids])),
                    "xT": np.ascontiguousarray(x[0, r * TL:(r + 1) * TL, :].T),
                    "xh0": np.ascontiguousarray(halo.reshape(16, 128, 2).transpose(1, 0, 2)),
                    "wf": wf, "bfb": bfb, "vecs": vecs, "cw": cw, "ones_bf": ones_bf, "cf32": cf32, "cbf": cbf,
                    "oh": oh, "ohp": ohp, "ohd": ohd.astype(bf)})
    if "F" not in _CACHE:
        _CACHE["F"] = build_fused(depth)
    res = run(_CACHE["F"], ims)
    out = np.concatenate([res[r]["xo"].T for r in range(NCORES)], axis=0)[None]
    return np.ascontiguousarray(out.astype(f32))


_CACHE = {}


def get_prog(name):
    if name not in _CACHE:
        _CACHE[name] = {"W": build_W, "A": build_A, "ATT": build_ATT, "B0": lambda: build_B(False), "B1": lambda: build_B(True)}[name]()
    return _CACHE[name]


def run(nc, in_maps):
    res = run_bass_kernel_spmd(nc, in_maps, core_ids=list(range(NCORES)))
    return res.results


def weight_groups(w_in, w_conv_out, w_attn_out, w_mix_out, w_ff1, w_ff2, l):
    gs = [w_in[l][:, 0:2048], w_in[l][:, 2048:4096], w_in[l][:, 4096:6144],
          w_in[l][:, 6152:8200], w_in[l][:, 8200:10248], w_mix_out[l],
          np.concatenate([w_conv_out[l], w_attn_out[l]], axis=0)]
    gs += [w_ff1[l][:, 2048 * i:2048 * (i + 1)] for i in range(4)]
    gs += [w_ff2[l][2048 * i:2048 * (i + 1), :] for i in range(4)]
    return gs


def convert_weights(groups):
    ng = len(groups)
    in_maps = []
    for cidx in range(NCORES):
        ids = [(cidx * GPC + j) % ng for j in range(GPC)]
        in_maps.append({"wsrc": np.ascontiguousarray(np.stack([groups[i] for i in ids]))})
    res = run(get_prog("W"), in_maps)
    out = [None] * ng
    for cidx in range(NCORES):
        for j in range(GPC):
            gi = cidx * GPC + j
            if gi < ng:
                out[gi] = res[cidx]["wb"][j]
    return out


def colvec(v):
    return np.ascontiguousarray(v.reshape(16, 128).T)


def att_consts():
    tri = np.triu(np.ones((128, 128), np.float32))
    tris = np.triu(np.ones((128, 128), np.float32), 1)
    cf32 = np.stack([tri, tris, np.ones((128, 128), np.float32), np.eye(128, dtype=np.float32)])
    return cf32


def kernel(x, g_mix, w_in, b_f, b_gate, conv_w, w_conv_out, w_attn_out, w_mix_out, g_mlp, w_ff1, w_ff2, g_final):
    f32 = np.float32
    x = np.asarray(x, f32)
    depth = w_in.shape[0]
    groups = []
    for l in range(depth):
        groups += weight_groups(w_in, w_conv_out, w_attn_out, w_mix_out, w_ff1, w_ff2, l)
    conv = convert_weights([np.asarray(g, f32) for g in groups])
    ones_bf = np.ones((128, 128), ml_dtypes.bfloat16)
    cf32 = att_consts()
    xT = [np.ascontiguousarray(x[0, r * TL:(r + 1) * TL, :].T) for r in range(NCORES)]
    for l in range(depth):
        wbl = np.ascontiguousarray(np.stack(conv[l * NGRP:(l + 1) * NGRP]))
        wf = np.ascontiguousarray(np.asarray(w_in[l][:, 6144:6152], f32).reshape(16, 128, 8).transpose(1, 0, 2))
        bfb = np.ascontiguousarray(np.tile(np.asarray(b_f[l], f32), (128, 4)))
        gm = colvec(np.asarray(g_mix[l], f32))
        ims = [{"xT": xT[r], "gmix": gm, "wq": np.ascontiguousarray(wbl[G_CVQ][8:16]),
                "wk": np.ascontiguousarray(wbl[G_KV][0:8]), "wv": np.ascontiguousarray(wbl[G_KV][8:16]),
                "wf": wf, "bfb": bfb, "ones_bf": ones_bf} for r in range(NCORES)]
        ra = run(get_prog("A"), ims)
        v_all = np.concatenate([ra[r]["v"] for r in range(NCORES)], axis=0)
        lp_all = np.concatenate([ra[r]["lp"] for r in range(NCORES)], axis=0)
        ims = []
        for hh in range(NH):
            qT = np.concatenate([ra[r]["qT"][hh] for r in range(NCORES)], axis=1)
            kT = np.concatenate([ra[r]["kT"][hh] for r in range(NCORES)], axis=1)
            vB = v_all[:, hh * 128:(hh + 1) * 128].reshape(NKB, 128, 128).transpose(1, 0, 2)
            lp = lp_all[:, hh].reshape(NKB, 128).T
            ims.append({"qT": np.ascontiguousarray(qT), "kT": np.ascontiguousarray(kT),
                        "vB": np.ascontiguousarray(vB), "lp": np.ascontiguousarray(lp),
                        "cf32": cf32})
        rt_ = run(get_prog("ATT"), ims)
        vecs = np.ascontiguousarray(np.stack([
            colvec(np.asarray(g_mix[l], f32)), colvec(np.asarray(g_mlp[l], f32)), colvec(np.asarray(g_final, f32)),
            colvec(np.asarray(b_gate[l][:D], f32)), colvec(np.asarray(b_gate[l][D:], f32))], axis=1))
        cw = np.ascontiguousarray(np.asarray(conv_w[l], f32).reshape(3, 8, 128).transpose(2, 0, 1))
        ims = []
        for r in range(NCORES):
            attnT = np.concatenate([rt_[hh]["oTok"][r * TL:(r + 1) * TL, :].T for hh in range(NH)], axis=0)
            if r == 0:
                halo = np.zeros((D, 2), f32)
            else:
                halo = xT[r - 1][:, TL - 2:TL]
            xh = np.ascontiguousarray(halo.reshape(16, 128, 2).transpose(1, 0, 2))
            ims.append({"xT": xT[r], "xh": xh, "attnT": np.ascontiguousarray(attnT), "wb": wbl,
                        "vecs": vecs, "cw": cw, "ones_bf": ones_bf})
        rb = run(get_prog("B1" if l == depth - 1 else "B0"), ims)
        xT = [rb[r]["xo"] for r in range(NCORES)]
    out = np.concatenate([xT[r].T for r in range(NCORES)], axis=0)[None]
    return np.ascontiguousarray(out.astype(f32))
```

```python
import contextlib
import numpy as np
import ml_dtypes
import concourse.bass as bass
import concourse.mybir as mybir
from concourse.bass_utils import run_bass_kernel_spmd

F32 = mybir.dt.float32
BF16 = mybir.dt.bfloat16
AF = mybir.ActivationFunctionType
ALU = mybir.AluOpType

NCORES = 8
D = 2048
S = 16384
TL = S // NCORES
T = 512
NT = TL // T
NH = 8
HD = 128
DFF = 8192
EPS = 1e-6
SCALE = float(HD) ** -0.5
NGRP = 15
G_CBCC, G_CVQ, G_KV, G_GC, G_GA, G_MIX, G_CA = 0, 1, 2, 3, 4, 5, 6
G_FF1 = 7
G_FF2 = 11


class Op:
    __slots__ = ("eng", "fn", "deps", "dma_key", "ms", "is_mm", "waits", "cc", "idx")

    def __init__(self, eng, fn, dma_key=None, is_mm=False):
        self.eng, self.fn, self.dma_key, self.is_mm = eng, fn, dma_key, is_mm
        self.cc = False
        self.deps = []
        self.ms = None
        self.waits = []


class Prog:
    ENGS = ("pe", "act", "dve", "pool", "sp")

    def __init__(self):
        self.ops = {e: [] for e in self.ENGS}
        self.last_w = {}
        self.readers = {}
        self.all_ops = []
        self.dma_keys = []
        self.last_dma = {}
        self.bar_deps = []
        self.bar_pending = set()

    def add(self, eng, fn, reads=(), writes=(), dma_key=None, is_mm=False):
        op = Op(eng, fn, dma_key, is_mm)
        deps = []
        for t in reads:
            w = self.last_w.get(t)
            if w is not None:
                deps.append(w)
        for t in writes:
            w = self.last_w.get(t)
            if w is not None:
                deps.append(w)
            deps.extend(self.readers.get(t, ()))
        seen = set()
        for d in deps:
            if d is op or id(d) in seen:
                continue
            if is_mm and d.is_mm and d.dma_key is None:
                continue
            seen.add(id(d))
            op.deps.append(d)
        if eng in self.bar_pending:
            self.bar_pending.discard(eng)
            for d in self.bar_deps:
                if d is not op and id(d) not in seen and not (d.eng == eng and d.dma_key is None):
                    seen.add(id(d))
                    op.deps.append(d)
        for t in reads:
            self.readers.setdefault(t, []).append(op)
        for t in writes:
            self.last_w[t] = op
            self.readers[t] = []
        if dma_key is not None:
            self.last_dma[dma_key] = op
        best = {}
        kept = []
        for d in op.deps:
            if d.dma_key is not None:
                kept.append(d)
            elif d.eng not in best or d.idx > best[d.eng].idx:
                best[d.eng] = d
        op.deps = kept + list(best.values())
        op.idx = len(self.ops[eng])
        self.ops[eng].append(op)
        self.all_ops.append(op)
        if dma_key is not None and dma_key not in self.dma_keys:
            self.dma_keys.append(dma_key)
        return op

    def dma(self, queue, out, in_, reads=(), writes=(), key=None):
        assert key is not None
        return self.add(queue, lambda e: e.dma_start(out=out, in_=in_), reads, writes, dma_key=key)

    def barrier(self):
        deps = [self.ops[e][-1] for e in self.ENGS if self.ops[e] and self.ops[e][-1].dma_key is None]
        deps += list(self.last_dma.values())
        self.bar_deps = deps
        self.bar_pending = set(self.ENGS)

    def collective(self, in_ap, out_ap, reads=(), writes=(), key=None):
        op = self.add("pool", lambda e: e.collective_compute(
            "AllGather", ALU.bypass, replica_groups=[list(range(NCORES))], ins=[in_ap], outs=[out_ap]),
            reads, writes, dma_key=key)
        op.cc = True
        return op

    def finalize(self):
        needed = set()
        for op in self.all_ops:
            for d in op.deps:
                needed.add(id(d))
        cnt = {e: 0 for e in self.ENGS}
        dcnt = {}
        final_dma = {}
        for op in self.all_ops:
            if op.dma_key is not None:
                dcnt[op.dma_key] = dcnt.get(op.dma_key, 0) + (1 if op.cc else 16)
                op.ms = ("dma:%s" % (op.dma_key,), dcnt[op.dma_key])
                final_dma[op.dma_key] = dcnt[op.dma_key]
            elif id(op) in needed:
                cnt[op.eng] += 1
                op.ms = ("eng:" + op.eng, cnt[op.eng])
        waited = {e: {} for e in self.ENGS}
        for e in self.ENGS:
            for op in self.ops[e]:
                w = {}
                for d in op.deps:
                    s, v = d.ms
                    if waited[e].get(s, 0) >= v:
                        continue
                    w[s] = max(w.get(s, 0), v)
                for s, v in w.items():
                    waited[e][s] = v
                op.waits = list(w.items())
        self.final_dma = final_dma
        self.max_counts = dict(cnt)

    def sem_names(self):
        return ["eng:" + e for e in self.ENGS] + ["dma:%s" % (k,) for k in self.dma_keys]

    def emit(self, nc, sems, final_wait_eng="sp"):
        engmap = {"pe": "tensor", "act": "scalar", "dve": "vector", "pool": "gpsimd", "sp": "sync"}
        with nc.Block() as block:
            for e in self.ENGS:
                ops = self.ops[e]
                fw = self.final_dma if e == final_wait_eng else {}

                def body(engine, ops=ops, fw=fw):
                    for op in ops:
                        for s, v in op.waits:
                            engine.wait_ge(sems[s], v)
                        ins = op.fn(engine)
                        if op.cc:
                            ins.then_inc(sems[op.ms[0]])
                        elif op.ms is not None:
                            ins.then_inc(sems[op.ms[0]], 16 if op.dma_key is not None else 1)
                    for k, v in fw.items():
                        engine.wait_ge(sems["dma:%s" % (k,)], v)

                getattr(block, engmap[e])(body)


class Ctx:
    def __init__(self):
        self.nc = bass.Bass("TRN2", target_bir_lowering=False)
        self.P = Prog()
        self.stack = contextlib.ExitStack()
        self.nsb = 0

    def dram_in(self, name, shape, dt):
        return self.nc.dram_tensor(name, list(shape), dt, kind="ExternalInput").ap()

    def dram_out(self, name, shape, dt):
        return self.nc.dram_tensor(name, list(shape), dt, kind="ExternalOutput").ap()

    def sb(self, name, shape, dt):
        return self.stack.enter_context(self.nc.sbuf_tensor(name, list(shape), dt))

    def ps(self, name):
        return self.stack.enter_context(self.nc.psum_tensor(name, [128, 512], F32))

    def finish(self):
        P = self.P
        P.finalize()
        sems = {}
        for i, n in enumerate(P.sem_names()):
            sems[n] = self.stack.enter_context(self.nc.semaphore("s%d" % i))
        P.emit(self.nc, sems)
        self.stack.close()
        return self.nc


GPC = 4


def build_W():
    c = Ctx()
    P = c.P
    wsrc = c.dram_in("wsrc", [GPC, 2048, 2048], F32)
    wb = c.dram_out("wb", [GPC, 16, 128, 2048], BF16)
    stage = [c.sb("stage%d" % i, [128, 2048], F32) for i in range(3)]
    wt = [c.sb("wt%d" % i, [128, 16, 16, 128], BF16) for i in range(2)]
    n = 0
    for g in range(GPC):
        w = wt[g % 2]
        for k in range(16):
            st = stage[n % 3]
            tok = ("stage", n % 3)
            P.dma("sp", st[:, :], wsrc[g, k * 128:(k + 1) * 128, :], writes=[tok], key=tok)
            src = st[:, :].rearrange("p (m c) -> p m c", m=16)
            dst = w[:, :, k, :]
            eng = "dve" if n % 2 == 0 else "pool"
            P.add(eng, lambda e, dst=dst, src=src: e.tensor_copy(out=dst, in_=src),
                  reads=[tok], writes=[("wt", g % 2, k)])
            n += 1
        P.dma("pool", wb[g].rearrange("m p x -> p m x"),
              w[:, :, :, :].rearrange("p m k c -> p m (k c)"),
              reads=[("wt", g % 2, k) for k in range(16)], key=("wtst", g % 2))
    return c.finish()


def emit_norm(c, xt, xtok, h, htok, gcol, ones_bf, sq, ps_stat, rt, rstd, uid):
    P = c.P
    for k in range(16):
        P.add("act", lambda e, k=k: e.activation(out=sq[:, k, :], in_=xt[:, k, :], func=AF.Square),
              reads=[xtok], writes=[("sq", k)])
    for k in range(16):
        P.add("pe", lambda e, k=k: e.matmul(ps_stat[:, :], lhsT=ones_bf[:, :], rhs=sq[:, k, :],
                                             start=(k == 0), stop=(k == 15)),
              reads=[("sq", k), "consts"], writes=[("ps", id(ps_stat))], is_mm=True)
    P.add("act", lambda e: e.activation(out=rt[:, :], in_=ps_stat[:, :], func=AF.Sqrt,
                                        bias=EPS, scale=1.0 / D),
          reads=[("ps", id(ps_stat))], writes=["rt"])
    P.add("dve", lambda e: e.reciprocal(out=rstd[:, :], in_=rt[:, :]), reads=["rt"], writes=["rstd"])
    for k in range(16):
        P.add("dve", lambda e, k=k: e.scalar_tensor_tensor(out=h[:, k, :], in0=xt[:, k, :],
                                                            scalar=gcol[:, k:k + 1], in1=rstd[:, :],
                                                            op0=ALU.mult, op1=ALU.mult),
              reads=[xtok, "rstd", "consts"], writes=[(htok, k)])


def build_A():
    c = Ctx()
    P = c.P
    xT = c.dram_in("xT", [D, TL], F32)
    gmix = c.dram_in("gmix", [128, 16], F32)
    wq = c.dram_in("wq", [8, 128, 2048], BF16)
    wk = c.dram_in("wk", [8, 128, 2048], BF16)
    wv = c.dram_in("wv", [8, 128, 2048], BF16)
    wf = c.dram_in("wf", [128, 16, 8], F32)
    bfb = c.dram_in("bfb", [128, 32], F32)
    ones_in = c.dram_in("ones_bf", [128, 128], BF16)
    qT_o = c.dram_out("qT", [NH, 128, TL], BF16)
    kT_o = c.dram_out("kT", [NH, 128, TL], BF16)
    v_o = c.dram_out("v", [TL, 1024], BF16)
    lp_o = c.dram_out("lp", [TL, 8], F32)

    g_sb = c.sb("g_sb", [128, 16], F32)
    ones_bf = c.sb("ones", [128, 128], BF16)
    wq_sb = c.sb("wq_sb", [128, 8, 16, 128], BF16)
    wk_sb = c.sb("wk_sb", [128, 8, 16, 128], BF16)
    wv_sb = c.sb("wv_sb", [128, 8, 16, 128], BF16)
    wf32 = c.sb("wf32", [128, 16, 8], F32)
    wf_sb = c.sb("wf_sb", [128, 16, 8], BF16)
    bf_sb = c.sb("bf_sb", [128, 32], F32)
    xt = [c.sb("xt%d" % i, [128, 16, T], F32) for i in range(2)]
    sq = c.sb("sq", [128, 16, T], BF16)
    h = c.sb("h", [128, 16, T], BF16)
    rt = c.sb("rt", [128, T], F32)
    rstd = c.sb("rstd", [128, T], F32)
    ev = [c.sb("ev%d" % i, [128, T], BF16) for i in range(4)]
    lpt = c.sb("lpt", [128, 32], F32)
    lpe = c.sb("lpe", [128, 32], F32)
    lps = [c.sb("lps%d" % i, [128, 32], F32) for i in range(2)]
    ps_stat = c.ps("ps_stat")
    psb = [c.ps("psb%d" % i) for i in range(4)]
    ps_f = c.ps("ps_f")

    P.dma("sp", g_sb[:, :], gmix[:, :], writes=["consts"], key="c0")
    P.dma("sp", ones_bf[:, :], ones_in[:, :], writes=["consts"], key="c0")
    P.dma("sp", bf_sb[:, :], bfb[:, :], writes=["consts"], key="c0")
    P.dma("sp", wf32[:, :, :], wf[:, :, :], writes=["wf32"], key="c1")
    P.add("dve", lambda e: e.tensor_copy(out=wf_sb[:, :, :], in_=wf32[:, :, :]), reads=["wf32"], writes=["wf"])
    P.dma("sp", xt[0][:, :, :], xT.rearrange("(k p) t -> p k t", p=128)[:, :, 0:T],
          writes=[("xt", 0)], key=("xt", 0))
    for nm, src, dst in (("wk", wk, wk_sb), ("wv", wv, wv_sb), ("wq", wq, wq_sb)):
        P.dma("pool", dst[:, :, :, :].rearrange("p m k c -> p m (k c)"),
              src.rearrange("m p x -> p m x"), writes=[nm], key=nm)

    nev = 0
    for t in range(NT):
        xs = xt[t % 2]
        xtok = ("xt", t % 2)
        if t + 1 < NT:
            P.dma("sp", xt[(t + 1) % 2][:, :, :],
                  xT.rearrange("(k p) t -> p k t", p=128)[:, :, (t + 1) * T:(t + 2) * T],
                  writes=[("xt", (t + 1) % 2)], key=("xt", (t + 1) % 2))
        emit_norm(c, xs, xtok, h, "h", g_sb, ones_bf, sq, ps_stat, rt, rstd, t)
        hreads = [("h", k) for k in range(16)]
        for nm, wsb, dst, sc in (("wk", wk_sb, kT_o, 1.0), ("wq", wq_sb, qT_o, SCALE)):
            for m in range(NH):
                pb = psb[nev % 4]
                ptok = ("ps", id(pb))
                for k in range(16):
                    P.add("pe", lambda e, pb=pb, wsb=wsb, m=m, k=k: e.matmul(
                        pb[:, :], lhsT=wsb[:, m, k, :], rhs=h[:, k, :], start=(k == 0), stop=(k == 15)),
                        reads=[nm] + hreads, writes=[ptok], is_mm=True)
                es = ev[nev % 4]
                etok = ("ev", nev % 4)
                P.add("act", lambda e, es=es, pb=pb, sc=sc: e.activation(
                    out=es[:, :], in_=pb[:, :], func=AF.Copy, scale=sc), reads=[ptok], writes=[etok])
                P.dma("pool", dst[m, :, t * T:(t + 1) * T], es[:, :], reads=[etok], key=etok)
                nev += 1
        for tb in range(4):
            for half in range(2):
                pb = psb[nev % 4]
                ptok = ("ps", id(pb))
                for k in range(16):
                    P.add("pe", lambda e, pb=pb, tb=tb, half=half, k=k: e.matmul(
                        pb[:, :], lhsT=h[:, k, tb * 128:(tb + 1) * 128],
                        rhs=wv_sb[:, 4 * half:4 * half + 4, k, :], start=(k == 0), stop=(k == 15)),
                        reads=["wv"] + hreads, writes=[ptok], is_mm=True)
                es = ev[nev % 4]
                etok = ("ev", nev % 4)
                P.add("dve", lambda e, es=es, pb=pb: e.tensor_copy(out=es[:, :], in_=pb[:, :]),
                      reads=[ptok], writes=[etok])
                r0 = t * T + tb * 128
                P.dma("pool", v_o[r0:r0 + 128, half * 512:(half + 1) * 512], es[:, :], reads=[etok], key=etok)
                nev += 1
        ftok = ("ps", id(ps_f))
        for tb in range(4):
            for k in range(16):
                P.add("pe", lambda e, tb=tb, k=k: e.matmul(
                    ps_f[:, tb * 8:(tb + 1) * 8], lhsT=h[:, k, tb * 128:(tb + 1) * 128],
                    rhs=wf_sb[:, k, :], start=(k == 0), stop=(k == 15)),
                    reads=["wf"] + hreads, writes=[ftok], is_mm=True)
        ls = lps[t % 2]
        ltok = ("lps", t % 2)
        P.add("dve", lambda e: e.tensor_tensor(out=lpt[:, :], in0=ps_f[:, 0:32], in1=bf_sb[:, :], op=ALU.add),
              reads=[ftok, "consts"], writes=["lpt"])
        P.add("act", lambda e: e.activation(out=lpe[:, :], in_=lpt[:, :], func=AF.Exp, scale=-1.0),
              reads=["lpt"], writes=["lpe"])
        P.add("act", lambda e, ls=ls: e.activation(out=ls[:, :], in_=lpe[:, :], func=AF.Ln, bias=1.0),
              reads=["lpe"], writes=[ltok])
        P.dma("pool", lp_o[t * T:(t + 1) * T, :].rearrange("(b p) h -> p b h", p=128),
              ls[:, :].rearrange("p (b h) -> p b h", b=4), reads=[ltok], key=ltok)
    return c.finish()


NQT = S // T
NKB = S // 128
VW = 130


def build_ATT():
    c = Ctx()
    P = c.P
    qT = c.dram_in("qT", [128, S], BF16)
    kT = c.dram_in("kT", [128, S], BF16)
    vB = c.dram_in("vB", [128, NKB, 128], BF16)
    lp = c.dram_in("lp", [128, NKB], F32)
    cf32 = c.dram_in("cf32", [4, 128, 128], F32)
    oTok = c.dram_out("oTok", [S, 128], BF16)

    k_sb = c.sb("k_sb", [128, S], BF16)
    v_sb = c.sb("v_sb", [128, NKB, VW], BF16)
    lp_sb = c.sb("lp_sb", [128, NKB], F32)
    cf = c.sb("cf", [128, 4, 128], F32)
    Tsb = c.sb("Tsb", [128, 2], F32)
    Dm = c.sb("Dm", [128, 128], F32)
    Cb = c.sb("Cb", [128, NKB], F32)
    Zf = c.sb("Zf", [128, T], F32)
    aug = [c.sb("aug%d" % i, [128, T], F32) for i in range(2)]
    q_sb = [c.sb("q_sb%d" % i, [128, T], BF16) for i in range(2)]
    e_sb = [c.sb("e_sb%d" % i, [128, T], F32) for i in range(4)]
    p_sb = [c.sb("p_sb%d" % i, [128, T], BF16) for i in range(5)]
    rl_sb = c.sb("rl_sb", [128, 4], F32)
    o_sb = [c.sb("o_sb%d" % i, [128, 4, 128], BF16) for i in range(2)]
    ps_s = [c.ps("ps_s%d" % i) for i in range(4)]
    ps_o = [[c.ps("ps_o%d_%d" % (i, j)) for j in range(2)] for i in range(2)]
    tri, tris, ones_f, ident_f = cf[:, 0, :], cf[:, 1, :], cf[:, 2, :], cf[:, 3, :]

    P.dma("sp", cf[:, :, :], cf32.rearrange("n p c -> p n c"), writes=["cf"], key="cf")
    P.dma("sp", lp_sb[:, :], lp[:, :], writes=["lp"], key="lp")
    P.dma("sp", q_sb[0][:, :], qT[:, 0:T], writes=[("q", 0)], key=("q", 0))
    P.dma("sp", k_sb[:, :], kT[:, :], writes=["k"], key="k")
    for hv in range(4):
        P.dma("pool", v_sb[:, 32 * hv:32 * (hv + 1), 0:128], vB[:, 32 * hv:32 * (hv + 1), :],
              writes=[("v", hv)], key=("v", hv))
    P.add("dve", lambda e: e.memset(v_sb[:, :, 128:VW], 1.0), writes=["v1"])

    P.add("pe", lambda e: e.matmul(ps_s[1][:, 0:2], lhsT=lp_sb[:, :], rhs=ones_f[:, 0:2], start=True, stop=True),
          reads=["lp", "cf"], writes=[("ps_s", 1)], is_mm=True)
    P.add("dve", lambda e: e.tensor_copy(out=Tsb[:, :], in_=ps_s[1][:, 0:2]), reads=[("ps_s", 1)], writes=["Tsb"])
    P.add("dve", lambda e: e.tensor_scalar(out=Dm[:, :], in0=tris, scalar1=Tsb[:, 0:1], scalar2=None, op0=ALU.mult),
          reads=["Tsb", "cf"], writes=["Dm"])
    P.add("pe", lambda e: e.matmul(ps_s[0][:, 0:128], lhsT=tri, rhs=lp_sb[:, :], start=True, stop=False),
          reads=["lp", "cf"], writes=[("ps_s", 0)], is_mm=True)
    P.add("pe", lambda e: e.matmul(ps_s[0][:, 0:128], lhsT=ones_f, rhs=Dm[:, :], start=False, stop=True),
          reads=["Dm", "cf"], writes=[("ps_s", 0)], is_mm=True)
    P.add("dve", lambda e: e.tensor_copy(out=Cb[:, :], in_=ps_s[0][:, 0:128]), reads=[("ps_s", 0)], writes=["Cb"])

    cnt = {"s": 0, "p": 0}

    def prep_aug(g):
        in0 = bass.AP(cf, 3 * 128, [[4 * 128, 128], [0, 4], [1, 128]])
        in1 = bass.AP(Cb, 4 * g, [[NKB, 128], [1, 4], [0, 128]])
        outz = Zf[:, :].rearrange("p (j t) -> p j t", j=4)
        P.add("dve", lambda e: e.tensor_tensor(out=outz, in0=in0, in1=in1, op=ALU.mult),
              reads=["cf", "Cb"], writes=["Zf"])
        si = cnt["s"] % 4
        cnt["s"] += 1
        pa = ps_s[si]
        P.add("pe", lambda e: e.matmul(pa[:, :], lhsT=ones_f, rhs=Zf[:, :], start=True, stop=True),
              reads=["cf", "Zf"], writes=[("ps_s", si)], is_mm=True)
        a = aug[g % 2]
        P.add("act", lambda e: e.activation(out=a[:, :], in_=pa[:, :], func=AF.Copy, scale=-1.0),
              reads=[("ps_s", si)], writes=[("aug", g % 2)])

    def tile(g):
        qs = q_sb[g % 2]
        qtok = ("q", g % 2)
        ag = aug[g % 2]
        agtok = ("aug", g % 2)
        if g + 1 < NQT:
            P.dma("sp", q_sb[(g + 1) % 2][:, :], qT[:, (g + 1) * T:(g + 2) * T],
                  writes=[("q", (g + 1) % 2)], key=("q", (g + 1) % 2))
        nkb = 4 * g + 4
        po = ps_o[g % 2]
        potok = [("ps_o", g % 2, 0), ("ps_o", g % 2, 1)]
        slots = []
        started = [False, False]
        last_c = {}
        def emit_st(kb):
            j = kb - 4 * g
            c0 = 128 * j if j > 0 else 0
            si = cnt["s"] % 4
            cnt["s"] += 1
            pss = ps_s[si]
            stok = ("ps_s", si)
            P.add("pe", lambda e: e.matmul(pss[:, c0:T], lhsT=k_sb[:, kb * 128:(kb + 1) * 128], rhs=qs[:, c0:T],
                                           start=True, stop=True), reads=["k", qtok], writes=[stok], is_mm=True)
            es = e_sb[si]
            etok = ("e", si)
            P.add("dve", lambda e: e.tensor_tensor(out=es[:, c0:T], in0=pss[:, c0:T], in1=ag[:, c0:T], op=ALU.add),
                  reads=[stok, agtok], writes=[etok])
            pi = cnt["p"] % 5
            cnt["p"] += 1
            pt = p_sb[pi]
            ptok = ("p", pi)
            P.add("act", lambda e: e.activation(out=pt[:, c0:T], in_=es[:, c0:T], func=AF.Exp,
                                                bias=Cb[:, kb:kb + 1], scale=1.0),
                  reads=[etok, "Cb"], writes=[ptok])
            if j >= 0:
                P.add("pool", lambda e: e.affine_select(out=pt[:, c0:c0 + 128], in_=pt[:, c0:c0 + 128],
                                                        pattern=[[1, 128]], compare_op=ALU.is_ge, fill=0.0,
                                                        base=0, channel_multiplier=-1),
                      reads=[ptok], writes=[ptok])
            slots.append((pt, ptok, j))

        def emit_pv(kb):
            pt, ptok, j = slots[kb]
            for cc in range(max(j, 0), 4):
                bnk = cc // 2
                first = not started[bnk]
                started[bnk] = True
                last = (kb == 4 * g + 2 * bnk + 1) and (cc == 2 * bnk + 1)
                out = po[bnk][:, (cc % 2) * VW:(cc % 2) * VW + 129]
                P.add("pe", lambda e, out=out, cc=cc, first=first, last=last: e.matmul(
                    out, lhsT=pt[:, cc * 128:(cc + 1) * 128], rhs=v_sb[:, kb, 0:129], start=first, stop=last),
                    reads=[("v", kb // 32), "v1", ptok], writes=[potok[bnk]], is_mm=True)

        for i in range(nkb + 3):
            if i < nkb:
                emit_st(i)
            if i == min(8, nkb - 1) and g + 1 < NQT:
                prep_aug(g + 1)
            if i >= 3:
                emit_pv(i - 3)
        os_ = o_sb[g % 2]
        otok = ("o", g % 2)
        for bnk in range(2):
            src = bass.AP(po[bnk], 128, [[512, 128], [VW, 2]])
            P.add("dve", lambda e, src=src, bnk=bnk: e.reciprocal(out=rl_sb[:, 2 * bnk:2 * bnk + 2], in_=src),
                  reads=[potok[bnk]], writes=[("rl", bnk)])
            for h2 in range(2):
                cc = 2 * bnk + h2
                P.add("dve", lambda e, bnk=bnk, h2=h2, cc=cc: e.tensor_scalar(
                    out=os_[:, cc, :], in0=po[bnk][:, h2 * VW:h2 * VW + 128], scalar1=rl_sb[:, cc:cc + 1],
                    scalar2=None, op0=ALU.mult), reads=[potok[bnk], ("rl", bnk)], writes=[otok])
        P.dma("pool", oTok[g * T:(g + 1) * T, :].rearrange("(c p) d -> p c d", p=128), os_[:, :, :],
              reads=[otok], key=otok)

    prep_aug(0)
    for g in range(NQT):
        tile(g)
    return c.finish()


NW = 5


def build_B(final):
    c = Ctx()
    P = c.P
    xT = c.dram_in("xT", [D, TL], F32)
    xh = c.dram_in("xh", [128, 16, 2], F32)
    attnT = c.dram_in("attnT", [1024, TL], BF16)
    wb = c.dram_in("wb", [NGRP, 16, 128, 2048], BF16)
    vecs = c.dram_in("vecs", [128, 5, 16], F32)
    cwd = c.dram_in("cw", [128, 3, 8], F32)
    ones_in = c.dram_in("ones_bf", [128, 128], BF16)
    xo = c.dram_out("xo", [D, TL], F32)

    vec = c.sb("vec", [128, 5, 16], F32)
    cw = c.sb("cw_sb", [128, 3, 8], F32)
    ones_bf = c.sb("ones", [128, 128], BF16)
    xt = c.sb("xt", [128, 16, T], F32)
    sq = c.sb("sq", [128, 16, T], BF16)
    h = c.sb("h", [128, 16, T], BF16)
    rt = c.sb("rt", [128, T], F32)
    rstd = c.sb("rstd", [128, T], F32)
    xh_sb = c.sb("xh_sb", [128, 16, 2], F32)
    hh = c.sb("hh", [128, 16, 2], BF16)
    convy = c.sb("convy", [128, 8, T], BF16)
    at_sb = c.sb("at_sb", [128, 8, T], BF16)
    merged = c.sb("merged", [128, 16, T], BF16)
    u = c.sb("u", [128, 32, T], BF16)
    wt = [c.sb("wt%d" % i, [128, 16, 128], BF16) for i in range(NW)]
    w2t = [c.sb("w2t%d" % i, [128, 32, 128], BF16) for i in range(2)]
    tcc = c.sb("tcc", [128, T], F32)
    tch = c.sb("tch", [128, 2], F32)
    usb = c.sb("usb", [128, T + 2], F32)
    uh = c.sb("uh", [128, 8, 2], F32)
    t1 = c.sb("t1", [128, T], F32)
    sg = c.sb("sg", [128, T], F32)
    sa = c.sb("sa", [128, T], F32)
    m1 = c.sb("m1", [128, T], F32)
    m2 = c.sb("m2", [128, T], F32)
    rr = [c.sb("rr%d" % i, [128, T], BF16) for i in range(2)]
    ps_stat = c.ps("ps_stat")
    psH = c.ps("psH")
    psg = [c.ps("psg%d" % i) for i in range(6)]
    st = {"nw": 0, "nw2": 0, "np": 0, "nr": 0}
    gmix, gmlp, gfin, bgc, bga = (vec[:, i, :] for i in range(5))

    P.dma("sp", vec[:, :, :], vecs[:, :, :], writes=["consts"], key="c0")
    P.dma("sp", cw[:, :, :], cwd[:, :, :], writes=["consts"], key="c0")
    P.dma("sp", ones_bf[:, :], ones_in[:, :], writes=["consts"], key="c0")
    P.dma("sp", xh_sb[:, :, :], xh[:, :, :], writes=["xh"], key="xh")

    XT = [("x", k) for k in range(16)]
    HT = [("h", k) for k in range(16)]

    def load_w(g, m):
        sl = st["nw"] % NW
        st["nw"] += 1
        tok = ("w", sl)
        P.dma("sp", wt[sl][:, :, :].rearrange("p k c -> p (k c)"), wb[g, m], writes=[tok], key=tok)
        return wt[sl], tok

    def bank():
        b = psg[st["np"] % 6]
        st["np"] += 1
        return b, ("psg", id(b))

    def gemm(w, wtok, ks, rhs_of, rtoks, ncols=T, dest=None, koff=0):
        if dest is None:
            b, btok = bank()
            out = b[:, 0:ncols]
        else:
            out, btok = dest
        n = len(ks)
        for i, k in enumerate(ks):
            P.add("pe", lambda e, out=out, k=k, i=i: e.matmul(out, lhsT=w[:, koff + k, :], rhs=rhs_of(k),
                                                              start=(i == 0), stop=(i == n - 1)),
                  reads=[wtok] + rtoks, writes=[btok], is_mm=True)
        return out, btok

    def norm_small():
        for k in range(16):
            P.add("act", lambda e, k=k: e.activation(out=sq[:, k, 0:2], in_=xh_sb[:, k, :], func=AF.Square),
                  reads=["xh"], writes=[("sq", k)])
        for k in range(16):
            P.add("pe", lambda e, k=k: e.matmul(ps_stat[:, 0:2], lhsT=ones_bf[:, :], rhs=sq[:, k, 0:2],
                                                 start=(k == 0), stop=(k == 15)),
                  reads=[("sq", k), "consts"], writes=[("ps", id(ps_stat))], is_mm=True)
        P.add("act", lambda e: e.activation(out=rt[:, 0:2], in_=ps_stat[:, 0:2], func=AF.Sqrt, bias=EPS, scale=1.0 / D),
              reads=[("ps", id(ps_stat))], writes=["rt"])
        P.add("dve", lambda e: e.reciprocal(out=rstd[:, 0:2], in_=rt[:, 0:2]), reads=["rt"], writes=["rstd"])
        for k in range(16):
            P.add("dve", lambda e, k=k: e.scalar_tensor_tensor(out=hh[:, k, :], in0=xh_sb[:, k, :],
                                                                scalar=gmix[:, k:k + 1], in1=rstd[:, 0:2],
                                                                op0=ALU.mult, op1=ALU.mult),
                  reads=["xh", "rstd", "consts"], writes=["hh"])

    def norm_main(gcol, out_h=True):
        for k in range(16):
            P.add("act", lambda e, k=k: e.activation(out=sq[:, k, :], in_=xt[:, k, :], func=AF.Square),
                  reads=[("x", k)], writes=[("sq", k)])
        for k in range(16):
            P.add("pe", lambda e, k=k: e.matmul(ps_stat[:, :], lhsT=ones_bf[:, :], rhs=sq[:, k, :],
                                                 start=(k == 0), stop=(k == 15)),
                  reads=[("sq", k), "consts"], writes=[("ps", id(ps_stat))], is_mm=True)
        P.add("act", lambda e: e.activation(out=rt[:, :], in_=ps_stat[:, :], func=AF.Sqrt, bias=EPS, scale=1.0 / D),
              reads=[("ps", id(ps_stat))], writes=["rt"])
        P.add("dve", lambda e: e.reciprocal(out=rstd[:, :], in_=rt[:, :]), reads=["rt"], writes=["rstd"])
        for k in range(16):
            if out_h:
                P.add("dve", lambda e, k=k: e.scalar_tensor_tensor(out=h[:, k, :], in0=xt[:, k, :],
                                                                    scalar=gcol[:, k:k + 1], in1=rstd[:, :],
                                                                    op0=ALU.mult, op1=ALU.mult),
                      reads=[("x", k), "rstd", "consts"], writes=[("h", k)])
            else:
                P.add("dve", lambda e, k=k: e.scalar_tensor_tensor(out=xt[:, k, :], in0=xt[:, k, :],
                                                                    scalar=gcol[:, k:k + 1], in1=rstd[:, :],
                                                                    op0=ALU.mult, op1=ALU.mult),
                      reads=[("x", k), "rstd", "consts"], writes=[("x", k)])

    def tile(t):
        cs = slice(t * T, (t + 1) * T)
        P.dma("sp", xt[:, :, :], xT.rearrange("(k p) t -> p k t", p=128)[:, :, cs], writes=XT, key="xt")
        P.dma("pool", at_sb[:, :, :], attnT.rearrange("(k p) t -> p k t", p=128)[:, :, cs],
              writes=["at"], key="at")
        if t == 0:
            norm_small()
        norm_main(gmix)
        for j in range(8):
            wcc, tcc_w = load_w(G_CBCC, 8 + j)
            wcv, tcv_w = load_w(G_CVQ, j)
            wcb, tcb_w = load_w(G_CBCC, j)
            pa, patok = gemm(wcc, tcc_w, range(16), lambda k: h[:, k, :], HT)
            P.add("act", lambda e, pa=pa: e.activation(out=tcc[:, :], in_=pa, func=AF.Copy),
                  reads=[patok], writes=["tcc"])
            if t == 0:
                gemm(wcc, tcc_w, range(16), lambda k: hh[:, k, :], ["hh"], dest=(psH[:, 0:2], "psH"))
                P.add("act", lambda e: e.activation(out=tch[:, :], in_=psH[:, 0:2], func=AF.Copy),
                      reads=["psH"], writes=["tch"])
            pv, pvtok = gemm(wcv, tcv_w, range(16), lambda k: h[:, k, :], HT)
            if t == 0:
                gemm(wcv, tcv_w, range(16), lambda k: hh[:, k, :], ["hh"], dest=(psH[:, 2:4], "psH2"))
                P.add("dve", lambda e: e.tensor_tensor(out=usb[:, 0:2], in0=psH[:, 2:4], in1=tch[:, :], op=ALU.mult),
                      reads=["psH2", "tch"], writes=["usb"])
            else:
                P.add("pool", lambda e, j=j: e.tensor_copy(out=usb[:, 0:2], in_=uh[:, j, :]),
                      reads=[("uh", j)], writes=["usb"])
            P.add("dve", lambda e, pv=pv: e.tensor_tensor(out=usb[:, 2:T + 2], in0=pv, in1=tcc[:, :], op=ALU.mult),
                  reads=[pvtok, "tcc", "usb"], writes=["usb"])
            P.add("pool", lambda e, j=j: e.tensor_copy(out=uh[:, j, :], in_=usb[:, T:T + 2]),
                  reads=["usb"], writes=[("uh", j)])
            P.add("dve", lambda e, j=j: e.tensor_scalar(out=t1[:, :], in0=usb[:, 2:T + 2], scalar1=cw[:, 2, j:j + 1],
                                                         scalar2=None, op0=ALU.mult),
                  reads=["usb", "consts"], writes=["t1"])
            for tap, off in ((1, 1), (0, 0)):
                P.add("dve", lambda e, j=j, tap=tap, off=off: e.scalar_tensor_tensor(
                    out=t1[:, :], in0=usb[:, off:off + T], scalar=cw[:, tap, j:j + 1], in1=t1[:, :],
                    op0=ALU.mult, op1=ALU.add), reads=["usb", "t1", "consts"], writes=["t1"])
            pc, pctok = gemm(wcb, tcb_w, range(16), lambda k: h[:, k, :], HT)
            P.add("dve", lambda e, pc=pc, j=j: e.tensor_tensor(out=convy[:, j, :], in0=pc, in1=t1[:, :], op=ALU.mult),
                  reads=[pctok, "t1"], writes=[("cy", j)])
        CY = [("cy", k) for k in range(8)]
        for j in range(16):
            wgc, tgc = load_w(G_GC, j)
            wca, tca = load_w(G_CA, j)
            wga, tga = load_w(G_GA, j)
            p1, p1t = gemm(wgc, tgc, range(16), lambda k: h[:, k, :], HT)
            P.add("act", lambda e, p1=p1, j=j: e.activation(out=sg[:, :], in_=p1, func=AF.Sigmoid,
                                                             bias=bgc[:, j:j + 1], scale=1.0),
                  reads=[p1t, "consts"], writes=["sg"])
            p2, p2t = gemm(wca, tca, range(8), lambda k: convy[:, k, :], CY)
            P.add("dve", lambda e, p2=p2: e.tensor_tensor(out=m1[:, :], in0=p2, in1=sg[:, :], op=ALU.mult),
                  reads=[p2t, "sg"], writes=["m1"])
            p3, p3t = gemm(wga, tga, range(16), lambda k: h[:, k, :], HT)
            P.add("act", lambda e, p3=p3, j=j: e.activation(out=sa[:, :], in_=p3, func=AF.Sigmoid,
                                                             bias=bga[:, j:j + 1], scale=1.0),
                  reads=[p3t, "consts"], writes=["sa"])
            p4, p4t = gemm(wca, tca, range(8), lambda k: at_sb[:, k, :], ["at"], koff=8)
            P.add("dve", lambda e, p4=p4: e.tensor_tensor(out=m2[:, :], in0=p4, in1=sa[:, :], op=ALU.mult),
                  reads=[p4t, "sa"], writes=["m2"])
            P.add("pool", lambda e, j=j: e.tensor_tensor(out=merged[:, j, :], in0=m1[:, :], in1=m2[:, :], op=ALU.add),
                  reads=["m1", "m2"], writes=[("mg", j)])
        MG = [("mg", k) for k in range(16)]
        for j in range(16):
            wm, tm = load_w(G_MIX, j)
            p5, p5t = gemm(wm, tm, range(16), lambda k: merged[:, k, :], MG)
            P.add("dve", lambda e, p5=p5, j=j: e.tensor_tensor(out=xt[:, j, :], in0=p5, in1=xt[:, j, :], op=ALU.add),
                  reads=[p5t, ("x", j)], writes=[("x", j)])
        norm_main(gmlp)
        for half in range(2):
            for f in range(32):
                fg = half * 32 + f
                w1, t1w = load_w(G_FF1 + fg // 16, fg % 16)
                p6, p6t = gemm(w1, t1w, range(16), lambda k: h[:, k, :], HT)
                ri = st["nr"] % 2
                st["nr"] += 1
                rrs = rr[ri]
                P.add("act", lambda e, p6=p6, rrs=rrs: e.activation(out=rrs[:, :], in_=p6, func=AF.Relu),
                      reads=[p6t], writes=[("rr", ri)])
                P.add("dve", lambda e, p6=p6, rrs=rrs, f=f: e.tensor_tensor(out=u[:, f, :], in0=p6, in1=rrs[:, :], op=ALU.mult),
                      reads=[p6t, ("rr", ri)], writes=[("u", f)])
            UT = [("u", f) for f in range(32)]
            for j in range(16):
                s2 = st["nw2"] % 2
                st["nw2"] += 1
                w2 = w2t[s2]
                toks = []
                for q in range(2):
                    tok = ("w2", s2, q)
                    P.dma("sp", w2[:, 16 * q:16 * (q + 1), :].rearrange("p k c -> p (k c)"),
                          wb[G_FF2 + 2 * half + q, j], writes=[tok], key=tok)
                    toks.append(tok)
                b, btok = bank()
                for f in range(32):
                    P.add("pe", lambda e, b=b, w2=w2, f=f: e.matmul(b[:, :], lhsT=w2[:, f, :], rhs=u[:, f, :],
                                                                     start=(f == 0), stop=(f == 31)),
                          reads=toks + UT, writes=[btok], is_mm=True)
                P.add("dve", lambda e, b=b, j=j: e.tensor_tensor(out=xt[:, j, :], in0=b[:, :], in1=xt[:, j, :], op=ALU.add),
                      reads=[btok, ("x", j)], writes=[("x", j)])
        if final:
            norm_main(gfin, out_h=False)
        P.dma("pool", xo.rearrange("(k p) t -> p k t", p=128)[:, :, cs], xt[:, :, :], reads=XT, key="xo")

    for t in range(NT):
        tile(t)
    return c.finish()


class FCtx:
    def __init__(self):
        self.nc = bass.Bass("TRN2", target_bir_lowering=False)
        self.P = Prog()
        self.off = 0
        self.n = 0
        self.banks = [self.nc.alloc_psum_tensor("bank%d" % i, [128, 512], F32) for i in range(8)]
        self.ncc = 0

    def dram_in(self, name, shape, dt):
        return self.nc.dram_tensor(name, list(shape), dt, kind="ExternalInput").ap()

    def dram_out(self, name, shape, dt):
        return self.nc.dram_tensor(name, list(shape), dt, kind="ExternalOutput").ap()

    def dram(self, name, shape, dt):
        return self.nc.dram_tensor(name, list(shape), dt)

    def phase(self):
        self.off = 0
        self.P.barrier()

    def sb(self, name, shape, dt):
        nbytes = int(np.prod(shape[1:])) * (4 if dt == F32 else 2)
        off = (self.off + 63) // 64 * 64
        self.n += 1
        t = self.nc.alloc_sbuf_tensor_at("%s_%d" % (name, self.n), list(shape), dt, offset=off)
        self.off = off + nbytes
        assert self.off <= 206 * 1024, (name, self.off)
        return t

    def allgather(self, src_t, dst_t, rtok, wtok):
        self.ncc += 1
        return self.P.collective(src_t.ap().opt(), dst_t.ap().opt(), reads=[rtok], writes=[wtok],
                                 key=("cc", self.ncc))

    def finish(self):
        P = self.P
        P.finalize()
        print("fused: sems", len(P.sem_names()), "milestones", P.max_counts,
              "ops", {e: len(P.ops[e]) for e in P.ENGS}, flush=True)
        sems = {}
        with contextlib.ExitStack() as st:
            for i, n in enumerate(P.sem_names()):
                sems[n] = st.enter_context(self.nc.semaphore("s%d" % i))
            P.emit(self.nc, sems)
        return self.nc


def f_norm(c, xt, h, gcol, ones_bf, sq, ps_stat, rt, rstd, ncol=T, xtoks=None, htoks=None, inplace=False):
    P = c.P
    for k in range(16):
        P.add("act", lambda e, k=k: e.activation(out=sq[:, k, 0:ncol], in_=xt[:, k, 0:ncol], func=AF.Square),
              reads=[xtoks[k]], writes=[("sq", k)])
    for k in range(16):
        P.add("pe", lambda e, k=k: e.matmul(ps_stat[:, 0:ncol], lhsT=ones_bf[:, :], rhs=sq[:, k, 0:ncol],
                                             start=(k == 0), stop=(k == 15)),
              reads=[("sq", k), "consts"], writes=["ps_stat"], is_mm=True)
    P.add("act", lambda e: e.activation(out=rt[:, 0:ncol], in_=ps_stat[:, 0:ncol], func=AF.Sqrt, bias=EPS, scale=1.0 / D),
          reads=["ps_stat"], writes=["rt"])
    P.add("dve", lambda e: e.reciprocal(out=rstd[:, 0:ncol], in_=rt[:, 0:ncol]), reads=["rt"], writes=["rstd"])
    for k in range(16):
        dst = xt if inplace else h
        P.add("dve", lambda e, k=k, dst=dst: e.scalar_tensor_tensor(out=dst[:, k, 0:ncol], in0=xt[:, k, 0:ncol],
                                                                    scalar=gcol[:, k:k + 1], in1=rstd[:, 0:ncol],
                                                                    op0=ALU.mult, op1=ALU.mult),
              reads=[xtoks[k], "rstd", "consts"], writes=[xtoks[k] if inplace else htoks[k]])


def f_phase_W(c, wsrc, wmy, wall):
    P = c.P
    c.phase()
    stage = [c.sb("stage", [128, 2048], F32) for i in range(3)]
    wt = [c.sb("wt", [128, 16, 16, 128], BF16) for i in range(2)]
    n = 0
    for g in range(GPC):
        w = wt[g % 2]
        for k in range(16):
            st = stage[n % 3]
            tok = ("stage", n % 3)
            P.dma("sp", st[:, :], wsrc[g, k * 128:(k + 1) * 128, :], writes=[tok], key=tok)
            src = st[:, :].rearrange("p (m c) -> p m c", m=16)
            dst = w[:, :, k, :]
            eng = "dve" if n % 2 == 0 else "pool"
            P.add(eng, lambda e, dst=dst, src=src: e.tensor_copy(out=dst, in_=src),
                  reads=[tok], writes=[("wt", g % 2, k)])
            n += 1
        for half in range(2):
            dtok = ("d", "wmy", g, half)
            P.dma("sp", wmy[g][half].ap().rearrange("m p x -> p m x"),
                  w[:, 8 * half:8 * half + 8, :, :].rearrange("p m k c -> p m (k c)"),
                  reads=[("wt", g % 2, k) for k in range(16)], writes=[dtok], key=("wtst", g % 2, half))
            c.allgather(wmy[g][half], wall[g][half], dtok, ("d", "wall", g, half))


def wsl(wall, l, gi, m):
    G = l * NGRP + gi
    r, j = G // GPC, G % GPC
    half, mm = m // 8, m % 8
    return wall[j][half].ap()[r * 8 + mm], ("d", "wall", j, half)


def f_phase_A(c, l, x_ap, xtok_d, wall, wf_in, vecs_in, bfb_in, ones_in, q_my, k_my, v_my, lp_my,
              q_all, k_all, v_all, lp_all):
    P = c.P
    c.phase()
    g_sb = c.sb("g_sb", [128, 16], F32)
    ones_bf = c.sb("ones", [128, 128], BF16)
    w_sb = {nm: c.sb("w" + nm, [128, 8, 16, 128], BF16) for nm in ("q", "k", "v")}
    wf32 = c.sb("wf32", [128, 16, 8], F32)
    wf_sb = c.sb("wf_sb", [128, 16, 8], BF16)
    bf_sb = c.sb("bf_sb", [128, 32], F32)
    xt = [c.sb("xt", [128, 16, T], F32) for i in range(2)]
    sq = c.sb("sq", [128, 16, T], BF16)
    h = c.sb("h", [128, 16, T], BF16)
    rt = c.sb("rt", [128, T], F32)
    rstd = c.sb("rstd", [128, T], F32)
    ev = [c.sb("ev", [128, T], BF16) for i in range(4)]
    lpt = c.sb("lpt", [128, 32], F32)
    lpe = c.sb("lpe", [128, 32], F32)
    lps = [c.sb("lps", [128, 32], F32) for i in range(2)]
    ps_stat, ps_f = c.banks[0], c.banks[1]
    psb = c.banks[2:6]

    P.dma("sp", g_sb[:, :], vecs_in[:, 5 * l + 0, :], writes=["consts"], key="c0")
    P.dma("sp", ones_bf[:, :], ones_in[:, :], writes=["consts"], key="c0")
    P.dma("sp", bf_sb[:, :], bfb_in[l], writes=["consts"], key="c0")
    P.dma("sp", wf32[:, :, :], wf_in[l], writes=["wf32"], key="c1")
    P.add("dve", lambda e: e.tensor_copy(out=wf_sb[:, :, :], in_=wf32[:, :, :]), reads=["wf32"], writes=["wf"])
    xv = x_ap.rearrange("(k p) t -> p k t", p=128)
    XT = [[("xt", i, k) for k in range(16)] for i in range(2)]
    HT = [("h", k) for k in range(16)]
    P.dma("sp", xt[0][:, :, :], xv[:, :, 0:T], reads=[xtok_d], writes=XT[0], key=("xt", 0))
    for nm, gi, m0 in (("k", G_KV, 0), ("v", G_KV, 8), ("q", G_CVQ, 8)):
        for m in range(8):
            src, dtok = wsl(wall, l, gi, m0 + m)
            P.dma("pool", w_sb[nm][:, m, :, :].rearrange("p k c -> p (k c)"), src, reads=[dtok],
                  writes=["w" + nm], key="w" + nm)
    nev = 0
    for t in range(NT):
        if t + 1 < NT:
            P.dma("sp", xt[(t + 1) % 2][:, :, :], xv[:, :, (t + 1) * T:(t + 2) * T], reads=[xtok_d],
                  writes=XT[(t + 1) % 2], key=("xt", (t + 1) % 2))
        f_norm(c, xt[t % 2], h, g_sb, ones_bf, sq, ps_stat, rt, rstd, xtoks=XT[t % 2], htoks=HT)
        for nm, dst, sc in (("k", k_my, 1.0), ("q", q_my, SCALE), ("v", v_my, 1.0)):
            wsb = w_sb[nm]
            for m in range(NH):
                pb = psb[nev % 4]
                ptok = ("psb", nev % 4)
                for k in range(16):
                    P.add("pe", lambda e, pb=pb, wsb=wsb, m=m, k=k: e.matmul(
                        pb[:, :], lhsT=wsb[:, m, k, :], rhs=h[:, k, :], start=(k == 0), stop=(k == 15)),
                        reads=["w" + nm, HT[k]], writes=[ptok], is_mm=True)
                es = ev[nev % 4]
                etok = ("ev", nev % 4)
                P.add("act", lambda e, es=es, pb=pb, sc=sc: e.activation(
                    out=es[:, :], in_=pb[:, :], func=AF.Copy, scale=sc), reads=[ptok], writes=[etok])
                P.dma("pool", dst.ap()[m, :, t * T:(t + 1) * T], es[:, :], reads=[etok],
                      writes=[("d", nm + "_my")], key=etok)
                nev += 1
        for tb in range(4):
            for k in range(16):
                P.add("pe", lambda e, tb=tb, k=k: e.matmul(
                    ps_f[:, tb * 8:(tb + 1) * 8], lhsT=h[:, k, tb * 128:(tb + 1) * 128],
                    rhs=wf_sb[:, k, :], start=(k == 0), stop=(k == 15)),
                    reads=["wf", HT[k]], writes=["ps_f"], is_mm=True)
        ls = lps[t % 2]
        ltok = ("lps", t % 2)
        P.add("dve", lambda e: e.tensor_tensor(out=lpt[:, :], in0=ps_f[:, 0:32], in1=bf_sb[:, :], op=ALU.add),
              reads=["ps_f", "consts"], writes=["lpt"])
        P.add("act", lambda e: e.activation(out=lpe[:, :], in_=lpt[:, :], func=AF.Exp, scale=-1.0),
              reads=["lpt"], writes=["lpe"])
        P.add("act", lambda e, ls=ls: e.activation(out=ls[:, :], in_=lpe[:, :], func=AF.Ln, bias=1.0),
              reads=["lpe"], writes=[ltok])
        P.dma("pool", lp_my.ap()[t * T:(t + 1) * T, :].rearrange("(b p) h -> p b h", p=128),
              ls[:, :].rearrange("p (b h) -> p b h", b=4), reads=[ltok], writes=[("d", "lp_my")], key=ltok)
    P.barrier()
    for nm, my, al in (("k", k_my, k_all), ("v", v_my, v_all), ("q", q_my, q_all), ("lp", lp_my, lp_all)):
        c.allgather(my, al, ("d", nm + "_my"), ("d", nm + "_all"))


def f_phase_ATT(c, q_all, k_all, v_all, lp_all, cf32_in, cbf_in, oh_in, ohd_in, o_my, o_all):
    P = c.P
    c.phase()
    k_sb = c.sb("k_sb", [128, S], BF16)
    v_sb = c.sb("v_sb", [128, NKB, VW], BF16)
    lp8 = c.sb("lp8", [128, NKB, 8], F32)
    lp_sb = c.sb("lp_sb", [128, NKB], F32)
    cf = c.sb("cf", [128, 4, 128], F32)
    cbf = c.sb("cbf", [128, 2, 128], BF16)
    oh = c.sb("oh", [128, 8], F32)
    ohd = c.sb("ohd", [128, 8, 128], BF16)
    Tsb = c.sb("Tsb", [128, 2], F32)
    Dm = c.sb("Dm", [128, 128], F32)
    Cb = c.sb("Cb", [128, NKB], F32)
    Zf = c.sb("Zf", [128, T], F32)
    aug = [c.sb("aug", [128, T], F32) for i in range(2)]
    sel_in = [c.sb("sel_in", [128, 8, T], BF16) for i in range(2)]
    q_sb = [c.sb("q_sb", [128, T], BF16) for i in range(2)]
    e_sb = [c.sb("e_sb", [128, T], F32) for i in range(4)]
    p_sb = [c.sb("p_sb", [128, T], BF16) for i in range(5)]
    rl_sb = c.sb("rl_sb", [128, 4], F32)
    o_sb = [c.sb("o_sb", [128, 4, 128], BF16) for i in range(2)]
    oT_sb = [c.sb("oT_sb", [128, T], BF16) for i in range(2)]
    ps_s = c.banks[0:4]
    ps_o = [c.banks[4:6], c.banks[6:8]]
    tri, tris, ones_f, ident_f = cf[:, 0, :], cf[:, 1, :], cf[:, 2, :], cf[:, 3, :]
    ident_b = cbf[:, 1, :]
    cnt = {"s": 0, "p": 0, "sel": 0}

    P.dma("sp", cf[:, :, :], cf32_in.rearrange("n p c -> p n c"), writes=["cf"], key="cf")
    P.dma("sp", cbf[:, :, :], cbf_in.rearrange("n p c -> p n c"), writes=["cbf"], key="cf")
    P.dma("sp", oh[:, :], oh_in[:, :], writes=["oh"], key="cf")
    P.dma("sp", ohd[:, :, :], ohd_in.rearrange("n p c -> p n c"), writes=["ohd"], key="cf")
    for i in range(4):
        P.dma("pool", lp8[:, 32 * i:32 * (i + 1), :],
              lp_all.ap()[4096 * i:4096 * (i + 1), :].rearrange("(b p) h -> p b h", p=128),
              reads=[("d", "lp_all")], writes=[("lp8", i)], key=("lp8", i))
    P.add("dve", lambda e: e.memset(v_sb[:, :, 128:VW], 1.0), writes=["v1"])
    for hh in range(8):
        if hh == 0:
            P.add("dve", lambda e: e.tensor_scalar(out=lp_sb[:, :], in0=lp8[:, :, 0], scalar1=oh[:, 0:1], scalar2=None,
                                                   op0=ALU.mult), reads=[("lp8", i) for i in range(4)] + ["oh"],
                  writes=["lp"])
        else:
            P.add("dve", lambda e, hh=hh: e.scalar_tensor_tensor(out=lp_sb[:, :], in0=lp8[:, :, hh],
                                                                  scalar=oh[:, hh:hh + 1], in1=lp_sb[:, :],
                                                                  op0=ALU.mult, op1=ALU.add),
                  reads=["lp", "oh"], writes=["lp"])

    def load_sel(src_all, dtok, r, piece):
        si = cnt["sel"] % 2
        cnt["sel"] += 1
        tok = ("sel", si)
        P.dma("sp", sel_in[si][:, :, :],
              src_all.ap()[r * 8:(r + 1) * 8, :, piece * T:(piece + 1) * T].rearrange("h p t -> p h t"),
              reads=[dtok], writes=[tok], key=tok)
        return sel_in[si], tok

    def next_bank():
        si = cnt["s"] % 4
        cnt["s"] += 1
        cnt["last_si"] = si
        return ps_s[si], ("ps_s", si)

    for r in range(8):
        for piece in range(4):
            sin, stok = load_sel(k_all, ("d", "k_all"), r, piece)
            pb, ptok = next_bank()
            for hh in range(8):
                P.add("pe", lambda e, pb=pb, sin=sin, hh=hh: e.matmul(pb[:, :], lhsT=ohd[:, hh, :], rhs=sin[:, hh, :],
                                                                       start=(hh == 0), stop=(hh == 7)),
                      reads=["ohd", stok], writes=[ptok], is_mm=True)
            c0 = r * TL + piece * T
            P.add("act", lambda e, pb=pb, c0=c0: e.activation(out=k_sb[:, c0:c0 + T], in_=pb[:, :], func=AF.Copy),
                  reads=[ptok], writes=[("k", c0 // T)])
    for r in range(8):
        for piece in range(4):
            sin, stok = load_sel(v_all, ("d", "v_all"), r, piece)
            pb, ptok = next_bank()
            for j in range(4):
                for hh in range(8):
                    P.add("pe", lambda e, pb=pb, sin=sin, hh=hh, j=j: e.matmul(
                        pb[:, j * 128:(j + 1) * 128], lhsT=sin[:, hh, j * 128:(j + 1) * 128], rhs=ohd[:, hh, :],
                        start=(j == 0 and hh == 0), stop=(j == 3 and hh == 7)),
                        reads=["ohd", stok], writes=[ptok], is_mm=True)
            b0 = (r * TL + piece * T) // 128
            P.add("dve", lambda e, pb=pb, b0=b0: e.tensor_copy(
                out=v_sb[:, b0:b0 + 4, 0:128], in_=pb[:, :].rearrange("p (j d) -> p j d", j=4)),
                reads=[ptok], writes=[("v", b0 // 4)])

    pbt, pbttok = next_bank()
    P.add("pe", lambda e: e.matmul(pbt[:, 0:2], lhsT=lp_sb[:, :], rhs=ones_f[:, 0:2], start=True, stop=True),
          reads=["lp", "cf"], writes=[pbttok], is_mm=True)
    P.add("dve", lambda e: e.tensor_copy(out=Tsb[:, :], in_=pbt[:, 0:2]), reads=[pbttok], writes=["Tsb"])
    P.add("dve", lambda e: e.tensor_scalar(out=Dm[:, :], in0=tris, scalar1=Tsb[:, 0:1], scalar2=None, op0=ALU.mult),
          reads=["Tsb", "cf"], writes=["Dm"])
    pbc, pbctok = next_bank()
    P.add("pe", lambda e: e.matmul(pbc[:, 0:128], lhsT=tri, rhs=lp_sb[:, :], start=True, stop=False),
          reads=["lp", "cf"], writes=[pbctok], is_mm=True)
    P.add("pe", lambda e: e.matmul(pbc[:, 0:128], lhsT=ones_f, rhs=Dm[:, :], start=False, stop=True),
          reads=["Dm", "cf"], writes=[pbctok], is_mm=True)
    P.add("dve", lambda e: e.tensor_copy(out=Cb[:, :], in_=pbc[:, 0:128]), reads=[pbctok], writes=["Cb"])

    def prep_tile(g):
        r, piece = g // 4, g % 4
        sin, stok = load_sel(q_all, ("d", "q_all"), r, piece)
        pb, ptok = next_bank()
        for hh in range(8):
            P.add("pe", lambda e, hh=hh: e.matmul(pb[:, :], lhsT=ohd[:, hh, :], rhs=sin[:, hh, :],
                                                   start=(hh == 0), stop=(hh == 7)),
                  reads=["ohd", stok], writes=[ptok], is_mm=True)
        qs = q_sb[g % 2]
        P.add("dve", lambda e: e.tensor_copy(out=qs[:, :], in_=pb[:, :]), reads=[ptok], writes=[("q", g % 2)])
        in0 = bass.AP(cf, 3 * 128, [[4 * 128, 128], [0, 4], [1, 128]])
        in1 = bass.AP(Cb, 4 * g, [[NKB, 128], [1, 4], [0, 128]])
        outz = Zf[:, :].rearrange("p (j t) -> p j t", j=4)
        P.add("dve", lambda e: e.tensor_tensor(out=outz, in0=in0, in1=in1, op=ALU.mult),
              reads=["cf", "Cb"], writes=["Zf"])
        pa, patok = next_bank()
        P.add("pe", lambda e: e.matmul(pa[:, :], lhsT=ones_f, rhs=Zf[:, :], start=True, stop=True),
              reads=["cf", "Zf"], writes=[patok], is_mm=True)
        a = aug[g % 2]
        P.add("act", lambda e: e.activation(out=a[:, :], in_=pa[:, :], func=AF.Copy, scale=-1.0),
              reads=[patok], writes=[("aug", g % 2)])

    def tile(g):
        qs = q_sb[g % 2]
        qtok = ("q", g % 2)
        ag = aug[g % 2]
        agtok = ("aug", g % 2)
        nkb = 4 * g + 4
        po = ps_o[g % 2]
        potok = [("ps_o", g % 2, 0), ("ps_o", g % 2, 1)]
        slots = []
        started = [False, False]

        def emit_st(kb):
            j = kb - 4 * g
            c0 = 128 * j if j > 0 else 0
            pss, stok = next_bank()
            si = cnt["last_si"]
            P.add("pe", lambda e: e.matmul(pss[:, c0:T], lhsT=k_sb[:, kb * 128:(kb + 1) * 128], rhs=qs[:, c0:T],
                                           start=True, stop=True), reads=[("k", kb // 4), qtok], writes=[stok],
                  is_mm=True)
            es = e_sb[si]
            etok = ("e", si)
            P.add("dve", lambda e: e.tensor_tensor(out=es[:, c0:T], in0=pss[:, c0:T], in1=ag[:, c0:T], op=ALU.add),
                  reads=[stok, agtok], writes=[etok])
            pi = cnt["p"] % 5
            cnt["p"] += 1
            pt = p_sb[pi]
            ptok = ("p", pi)
            P.add("act", lambda e: e.activation(out=pt[:, c0:T], in_=es[:, c0:T], func=AF.Exp,
                                                bias=Cb[:, kb:kb + 1], scale=1.0),
                  reads=[etok, "Cb"], writes=[ptok])
            if j >= 0:
                P.add("pool", lambda e: e.affine_select(out=pt[:, c0:c0 + 128], in_=pt[:, c0:c0 + 128],
                                                        pattern=[[1, 128]], compare_op=ALU.is_ge, fill=0.0,
                                                        base=0, channel_multiplier=-1),
                      reads=[ptok], writes=[ptok])
            slots.append((pt, ptok, j))

        def emit_pv(kb):
            pt, ptok, j = slots[kb]
            for cc in range(max(j, 0), 4):
                bnk = cc // 2
                first = not started[bnk]
                started[bnk] = True
                last = (kb == 4 * g + 2 * bnk + 1) and (cc == 2 * bnk + 1)
                out = po[bnk][:, (cc % 2) * VW:(cc % 2) * VW + 129]
                P.add("pe", lambda e, out=out, cc=cc, first=first, last=last: e.matmul(
                    out, lhsT=pt[:, cc * 128:(cc + 1) * 128], rhs=v_sb[:, kb, 0:129], start=first, stop=last),
                    reads=[("v", kb // 4), "v1", ptok], writes=[potok[bnk]], is_mm=True)

        for i in range(nkb + 3):
            if i < nkb:
                emit_st(i)
            if i == min(8, nkb - 1) and g + 1 < NQT:
                prep_tile(g + 1)
            if i >= 3:
                emit_pv(i - 3)
        os_ = o_sb[g % 2]
        otok = ("o", g % 2)
        for bnk in range(2):
            src = bass.AP(po[bnk], 128, [[512, 128], [VW, 2]])
            P.add("dve", lambda e, src=src, bnk=bnk: e.reciprocal(out=rl_sb[:, 2 * bnk:2 * bnk + 2], in_=src),
                  reads=[potok[bnk]], writes=[("rl", bnk)])
            for h2 in range(2):
                cc = 2 * bnk + h2
                P.add("dve", lambda e, bnk=bnk, h2=h2, cc=cc: e.tensor_scalar(
                    out=os_[:, cc, :], in0=po[bnk][:, h2 * VW:h2 * VW + 128], scalar1=rl_sb[:, cc:cc + 1],
                    scalar2=None, op0=ALU.mult), reads=[potok[bnk], ("rl", bnk)], writes=[otok])
        pb, ptok = next_bank()
        for cc in range(4):
            P.add("pe", lambda e, cc=cc: e.matmul(pb[:, cc * 128:(cc + 1) * 128], lhsT=os_[:, cc, :], rhs=ident_b,
                                                   start=(cc == 0), stop=(cc == 3)),
                  reads=[otok, "cbf"], writes=[ptok], is_mm=True)
        ots = oT_sb[g % 2]
        P.add("act", lambda e: e.activation(out=ots[:, :], in_=pb[:, :], func=AF.Copy), reads=[ptok],
              writes=[("oT", g % 2)])
        P.dma("pool", o_my.ap()[:, g * T:(g + 1) * T], ots[:, :], reads=[("oT", g % 2)], writes=[("d", "o_my")],
              key=("oT", g % 2))

    prep_tile(0)
    for g in range(NQT):
        tile(g)
    P.barrier()
    c.allgather(o_my, o_all, ("d", "o_my"), ("d", "o_all"))


def f_phase_B(c, l, final, x_ap, xtok_d, x_dst_ap, xh_in, xh_all, xh_my, o_all, wall, vecs_in, cw_in, ones_in,
              oh_in, ohp_in):
    P = c.P
    c.phase()
    vec = c.sb("vec", [128, 5, 16], F32)
    cw = c.sb("cw_sb", [128, 3, 8], F32)
    ones_bf = c.sb("ones", [128, 128], BF16)
    oh = c.sb("oh", [128, 8], F32)
    ohp = c.sb("ohp", [128, 8], F32)
    xt = c.sb("xt", [128, 16, T], F32)
    sq = c.sb("sq", [128, 16, T], BF16)
    h = c.sb("h", [128, 16, T], BF16)
    rt = c.sb("rt", [128, T], F32)
    rstd = c.sb("rstd", [128, T], F32)
    xh_sb = c.sb("xh_sb", [128, 16, 2], F32)
    xh8 = c.sb("xh8", [128, 8, 32], F32)
    hh = c.sb("hh", [128, 16, 2], BF16)
    convy = c.sb("convy", [128, 8, T], BF16)
    at_sb = c.sb("at_sb", [128, 8, T], BF16)
    merged = c.sb("merged", [128, 16, T], BF16)
    u = c.sb("u", [128, 32, T], BF16)
    wt = [c.sb("wt", [128, 16, 128], BF16) for i in range(NW)]
    w2t = [c.sb("w2t", [128, 32, 128], BF16) for i in range(2)]
    tcc = c.sb("tcc", [128, T], F32)
    tch = c.sb("tch", [128, 2], F32)
    usb = c.sb("usb", [128, T + 2], F32)
    uh = c.sb("uh", [128, 8, 2], F32)
    t1 = c.sb("t1", [128, T], F32)
    sg = c.sb("sg", [128, T], F32)
    sa = c.sb("sa", [128, T], F32)
    m1 = c.sb("m1", [128, T], F32)
    m2 = c.sb("m2", [128, T], F32)
    rr = [c.sb("rr", [128, T], BF16) for i in range(2)]
    ps_stat, psH = c.banks[0], c.banks[1]
    psg = c.banks[2:8]
    st = {"nw": 0, "nw2": 0, "np": 0, "nr": 0}
    gmix, gmlp, gfin, bgc, bga = (vec[:, i, :] for i in range(5))
    XT = [("x", k) for k in range(16)]
    HT = [("h", k) for k in range(16)]
    MG = [("mg", k) for k in range(16)]

    P.dma("sp", vec[:, :, :], vecs_in[:, 5 * l:5 * l + 5, :], writes=["consts"], key="c0")
    P.dma("sp", cw[:, :, :], cw_in[l], writes=["consts"], key="c0")
    P.dma("sp", ones_bf[:, :], ones_in[:, :], writes=["consts"], key="c0")
    P.dma("sp", oh[:, :], oh_in[:, :], writes=["consts"], key="c0")
    P.dma("sp", ohp[:, :], ohp_in[:, :], writes=["consts"], key="c0")
    if l == 0:
        P.dma("sp", xh_sb[:, :, :], xh_in[:, :, :], writes=["xh"], key="xh")
    else:
        P.dma("sp", xh8[:, :, :], xh_all.ap().rearrange("(j p) x -> p j x", p=128), reads=[("d", "xh_all")],
              writes=["xh8"], key="xh")
        xhf = xh_sb[:, :, :].rearrange("p k i -> p (k i)")
        for j in range(8):
            if j == 0:
                P.add("dve", lambda e: e.tensor_scalar(out=xhf, in0=xh8[:, 0, :], scalar1=ohp[:, 0:1], scalar2=None,
                                                       op0=ALU.mult), reads=["xh8", "consts"], writes=["xh"])
            else:
                P.add("dve", lambda e, j=j: e.scalar_tensor_tensor(out=xhf, in0=xh8[:, j, :], scalar=ohp[:, j:j + 1],
                                                                    in1=xhf, op0=ALU.mult, op1=ALU.add),
                      reads=["xh8", "xh", "consts"], writes=["xh"])

    def load_w(g, m):
        sl = st["nw"] % NW
        st["nw"] += 1
        tok = ("w", sl)
        src, dtok = wsl(wall, l, g, m)
        P.dma("sp", wt[sl][:, :, :].rearrange("p k c -> p (k c)"), src, reads=[dtok], writes=[tok], key=tok)
        return wt[sl], tok

    def bank():
        i = st["np"] % 6
        st["np"] += 1
        return psg[i], ("psg", i)

    def gemm(w, wtok, ks, rhs_of, rtoks, ncols=T, dest=None, koff=0):
        if dest is None:
            b, btok = bank()
            out = b[:, 0:ncols]
        else:
            out, btok = dest
        n = len(ks)
        for i, k in enumerate(ks):
            P.add("pe", lambda e, out=out, k=k, i=i: e.matmul(out, lhsT=w[:, koff + k, :], rhs=rhs_of(k),
                                                              start=(i == 0), stop=(i == n - 1)),
                  reads=[wtok] + rtoks, writes=[btok], is_mm=True)
        return out, btok

    xv = x_ap.rearrange("(k p) t -> p k t", p=128)
    xov = x_dst_ap.rearrange("(k p) t -> p k t", p=128)

    def tile(t):
        cs = slice(t * T, (t + 1) * T)
        P.dma("sp", xt[:, :, :], xv[:, :, cs], reads=[xtok_d], writes=XT, key="xt")
        for hd in range(8):
            si = hd % 2
            cand = merged[:, 8 * si:8 * si + 8, :]
            ctoks = MG[8 * si:8 * si + 8]
            P.dma("pool", cand, o_all.ap()[hd * 128:(hd + 1) * 128, :].rearrange("p (r x) -> p r x", r=8)[:, :, cs],
                  reads=[("d", "o_all")], writes=ctoks, key=("cand", si))
            for r in range(8):
                if r == 0:
                    P.add("dve", lambda e, hd=hd, cand=cand: e.tensor_scalar(
                        out=at_sb[:, hd, :], in0=cand[:, 0, :], scalar1=oh[:, 0:1], scalar2=None, op0=ALU.mult),
                        reads=ctoks + ["consts"], writes=[("at", hd)])
                else:
                    P.add("dve", lambda e, hd=hd, cand=cand, r=r: e.scalar_tensor_tensor(
                        out=at_sb[:, hd, :], in0=cand[:, r, :], scalar=oh[:, r:r + 1], in1=at_sb[:, hd, :],
                        op0=ALU.mult, op1=ALU.add), reads=ctoks + ["consts", ("at", hd)], writes=[("at", hd)])
        AT = [("at", k) for k in range(8)]
        if t == 0:
            f_norm(c, xh_sb, hh, gmix, ones_bf, sq, ps_stat, rt, rstd, ncol=2, xtoks=["xh"] * 16, htoks=["hh"] * 16)
        f_norm(c, xt, h, gmix, ones_bf, sq, ps_stat, rt, rstd, xtoks=XT, htoks=HT)
        for j in range(8):
            wcc, tcc_w = load_w(G_CBCC, 8 + j)
            wcv, tcv_w = load_w(G_CVQ, j)
            wcb, tcb_w = load_w(G_CBCC, j)
            pa, patok = gemm(wcc, tcc_w, range(16), lambda k: h[:, k, :], HT)
            P.add("act", lambda e, pa=pa: e.activation(out=tcc[:, :], in_=pa, func=AF.Copy),
                  reads=[patok], writes=["tcc"])
            if t == 0:
                gemm(wcc, tcc_w, range(16), lambda k: hh[:, k, :], ["hh"], dest=(psH[:, 0:2], "psH"))
                P.add("act", lambda e: e.activation(out=tch[:, :], in_=psH[:, 0:2], func=AF.Copy),
                      reads=["psH"], writes=["tch"])
            pv, pvtok = gemm(wcv, tcv_w, range(16), lambda k: h[:, k, :], HT)
            if t == 0:
                gemm(wcv, tcv_w, range(16), lambda k: hh[:, k, :], ["hh"], dest=(psH[:, 2:4], "psH2"))
                P.add("dve", lambda e: e.tensor_tensor(out=usb[:, 0:2], in0=psH[:, 2:4], in1=tch[:, :], op=ALU.mult),
                      reads=["psH2", "tch"], writes=["usb"])
            else:
                P.add("pool", lambda e, j=j: e.tensor_copy(out=usb[:, 0:2], in_=uh[:, j, :]),
                      reads=[("uh", j)], writes=["usb"])
            P.add("dve", lambda e, pv=pv: e.tensor_tensor(out=usb[:, 2:T + 2], in0=pv, in1=tcc[:, :], op=ALU.mult),
                  reads=[pvtok, "tcc", "usb"], writes=["usb"])
            P.add("pool", lambda e, j=j: e.tensor_copy(out=uh[:, j, :], in_=usb[:, T:T + 2]),
                  reads=["usb"], writes=[("uh", j)])
            P.add("dve", lambda e, j=j: e.tensor_scalar(out=t1[:, :], in0=usb[:, 2:T + 2], scalar1=cw[:, 2, j:j + 1],
                                                         scalar2=None, op0=ALU.mult),
                  reads=["usb", "consts"], writes=["t1"])
            for tap, off in ((1, 1), (0, 0)):
                P.add("dve", lambda e, j=j, tap=tap, off=off: e.scalar_tensor_tensor(
                    out=t1[:, :], in0=usb[:, off:off + T], scalar=cw[:, tap, j:j + 1], in1=t1[:, :],
                    op0=ALU.mult, op1=ALU.add), reads=["usb", "t1", "consts"], writes=["t1"])
            pc, pctok = gemm(wcb, tcb_w, range(16), lambda k: h[:, k, :], HT)
            P.add("dve", lambda e, pc=pc, j=j: e.tensor_tensor(out=convy[:, j, :], in0=pc, in1=t1[:, :], op=ALU.mult),
                  reads=[pctok, "t1"], writes=[("cy", j)])
        CY = [("cy", k) for k in range(8)]
        for j in range(16):
            wgc, tgc = load_w(G_GC, j)
            wca, tca = load_w(G_CA, j)
            wga, tga = load_w(G_GA, j)
            p1, p1t = gemm(wgc, tgc, range(16), lambda k: h[:, k, :], HT)
            P.add("act", lambda e, p1=p1, j=j: e.activation(out=sg[:, :], in_=p1, func=AF.Sigmoid,
                                                             bias=bgc[:, j:j + 1], scale=1.0),
                  reads=[p1t, "consts"], writes=["sg"])
            p2, p2t = gemm(wca, tca, range(8), lambda k: convy[:, k, :], CY)
            P.add("dve", lambda e, p2=p2: e.tensor_tensor(out=m1[:, :], in0=p2, in1=sg[:, :], op=ALU.mult),
                  reads=[p2t, "sg"], writes=["m1"])
            p3, p3t = gemm(wga, tga, range(16), lambda k: h[:, k, :], HT)
            P.add("act", lambda e, p3=p3, j=j: e.activation(out=sa[:, :], in_=p3, func=AF.Sigmoid,
                                                             bias=bga[:, j:j + 1], scale=1.0),
                  reads=[p3t, "consts"], writes=["sa"])
            p4, p4t = gemm(wca, tca, range(8), lambda k: at_sb[:, k, :], AT, koff=8)
            P.add("dve", lambda e, p4=p4: e.tensor_tensor(out=m2[:, :], in0=p4, in1=sa[:, :], op=ALU.mult),
                  reads=[p4t, "sa"], writes=["m2"])
            P.add("pool", lambda e, j=j: e.tensor_tensor(out=merged[:, j, :], in0=m1[:, :], in1=m2[:, :], op=ALU.add),
                  reads=["m1", "m2"], writes=[("mg", j)])
        for j in range(16):
            wm, tm = load_w(G_MIX, j)
            p5, p5t = gemm(wm, tm, range(16), lambda k: merged[:, k, :], MG)
            P.add("dve", lambda e, p5=p5, j=j: e.tensor_tensor(out=xt[:, j, :], in0=p5, in1=xt[:, j, :], op=ALU.add),
                  reads=[p5t, ("x", j)], writes=[("x", j)])
        f_norm(c, xt, h, gmlp, ones_bf, sq, ps_stat, rt, rstd, xtoks=XT, htoks=HT)
        for half in range(2):
            for f in range(32):
                fg = half * 32 + f
                w1, t1w = load_w(G_FF1 + fg // 16, fg % 16)
                p6, p6t = gemm(w1, t1w, range(16), lambda k: h[:, k, :], HT)
                ri = st["nr"] % 2
                st["nr"] += 1
                rrs = rr[ri]
                P.add("act", lambda e, p6=p6, rrs=rrs: e.activation(out=rrs[:, :], in_=p6, func=AF.Relu),
                      reads=[p6t], writes=[("rr", ri)])
                P.add("dve", lambda e, p6=p6, rrs=rrs, f=f: e.tensor_tensor(out=u[:, f, :], in0=p6, in1=rrs[:, :],
                                                                            op=ALU.mult),
                      reads=[p6t, ("rr", ri)], writes=[("u", f)])
            UT = [("u", f) for f in range(32)]
            for j in range(16):
                s2 = st["nw2"] % 2
                st["nw2"] += 1
                w2 = w2t[s2]
                toks = []
                for q in range(2):
                    tok = ("w2", s2, q)
                    src, dtok = wsl(wall, l, G_FF2 + 2 * half + q, j)
                    P.dma("sp", w2[:, 16 * q:16 * (q + 1), :].rearrange("p k c -> p (k c)"), src, reads=[dtok],
                          writes=[tok], key=tok)
                    toks.append(tok)
                b, btok = bank()
                for f in range(32):
                    P.add("pe", lambda e, b=b, w2=w2, f=f: e.matmul(b[:, :], lhsT=w2[:, f, :], rhs=u[:, f, :],
                                                                     start=(f == 0), stop=(f == 31)),
                          reads=toks + UT, writes=[btok], is_mm=True)
                P.add("dve", lambda e, b=b, j=j: e.tensor_tensor(out=xt[:, j, :], in0=b[:, :], in1=xt[:, j, :], op=ALU.add),
                      reads=[btok, ("x", j)], writes=[("x", j)])
        if not final and t == NT - 1:
            P.dma("pool", xh_my.ap().rearrange("p (k i) -> p k i", i=2), xt[:, :, T - 2:T], reads=XT,
                  writes=[("d", "xh_my")], key="xhst")
        if final:
            f_norm(c, xt, h, gfin, ones_bf, sq, ps_stat, rt, rstd, xtoks=XT, htoks=HT, inplace=True)
        P.dma("pool", xov[:, :, cs], xt[:, :, :], reads=XT, writes=[("d", "xdst")], key="xo")

    for t in range(NT):
        tile(t)
    if not final:
        P.barrier()
        c.allgather(xh_my, xh_all, ("d", "xh_my"), ("d", "xh_all"))


def build_fused(depth=2):
    c = FCtx()
    wsrc = c.dram_in("wsrc", [GPC, 2048, 2048], F32)
    xT = c.dram_in("xT", [D, TL], F32)
    xh0 = c.dram_in("xh0", [128, 16, 2], F32)
    wf_in = c.dram_in("wf", [depth, 128, 16, 8], F32)
    bfb_in = c.dram_in("bfb", [depth, 128, 32], F32)
    vecs_in = c.dram_in("vecs", [128, 5 * depth, 16], F32)
    cw_in = c.dram_in("cw", [depth, 128, 3, 8], F32)
    ones_in = c.dram_in("ones_bf", [128, 128], BF16)
    cf32_in = c.dram_in("cf32", [4, 128, 128], F32)
    cbf_in = c.dram_in("cbf", [2, 128, 128], BF16)
    oh_in = c.dram_in("oh", [128, 8], F32)
    ohp_in = c.dram_in("ohp", [128, 8], F32)
    ohd_in = c.dram_in("ohd", [8, 128, 128], BF16)
    xo = c.dram_out("xo", [D, TL], F32)
    wmy = [[c.dram("wmy%d_%d" % (g, hf), [8, 128, 2048], BF16) for hf in range(2)] for g in range(GPC)]
    wall = [[c.dram("wall%d_%d" % (g, hf), [64, 128, 2048], BF16) for hf in range(2)] for g in range(GPC)]
    qkv_my = {nm: c.dram(nm + "_my", [8, 128, TL], BF16) for nm in ("q", "k", "v")}
    qkv_all = {nm: c.dram(nm + "_all", [64, 128, TL], BF16) for nm in ("q", "k", "v")}
    lp_my = c.dram("lp_my", [TL, 8], F32)
    lp_all = c.dram("lp_all", [S, 8], F32)
    o_my = c.dram("o_my", [128, S], BF16)
    o_all = c.dram("o_all", [1024, S], BF16)
    x_cur = c.dram("x_cur", [D, TL], F32)
    xh_my = c.dram("xh_my", [128, 32], F32)
    xh_all = c.dram("xh_all", [1024, 32], F32)

    f_phase_W(c, wsrc, wmy, wall)
    for l in range(depth):
        final = (l == depth - 1)
        x_ap = xT if l == 0 else x_cur.ap()
        xtok = ("d", "xin") if l == 0 else ("d", "xdst")
        f_phase_A(c, l, x_ap, xtok, wall, wf_in, vecs_in, bfb_in, ones_in, qkv_my["q"], qkv_my["k"], qkv_my["v"],
                  lp_my, qkv_all["q"], qkv_all["k"], qkv_all["v"], lp_all)
        f_phase_ATT(c, qkv_all["q"], qkv_all["k"], qkv_all["v"], lp_all, cf32_in, cbf_in, oh_in, ohd_in, o_my, o_all)
        f_phase_B(c, l, final, x_ap, xtok, xo if final else x_cur.ap(), xh0, xh_all, xh_my, o_all, wall, vecs_in,
                  cw_in, ones_in, oh_in, ohp_in)
    return c.finish()


def kernel_fused(x, g_mix, w_in, b_f, b_gate, conv_w, w_conv_out, w_attn_out, w_mix_out, g_mlp, w_ff1, w_ff2, g_final):
    f32 = np.float32
    bf = ml_dtypes.bfloat16
    x = np.asarray(x, f32)
    depth = w_in.shape[0]
    groups = []
    for l in range(depth):
        groups += [np.asarray(g, f32) for g in weight_groups(w_in, w_conv_out, w_attn_out, w_mix_out, w_ff1, w_ff2, l)]
    ng = len(groups)
    wf = np.ascontiguousarray(np.stack([np.asarray(w_in[l][:, 6144:6152], f32).reshape(16, 128, 8).transpose(1, 0, 2)
                                        for l in range(depth)]))
    bfb = np.ascontiguousarray(np.stack([np.tile(np.asarray(b_f[l], f32), (128, 4)) for l in range(depth)]))
    vl = []
    for l in range(depth):
        vl += [colvec(np.asarray(g_mix[l], f32)), colvec(np.asarray(g_mlp[l], f32)), colvec(np.asarray(g_final, f32)),
               colvec(np.asarray(b_gate[l][:D], f32)), colvec(np.asarray(b_gate[l][D:], f32))]
    vecs = np.ascontiguousarray(np.stack(vl, axis=1))
    cw = np.ascontiguousarray(np.stack([np.asarray(conv_w[l], f32).reshape(3, 8, 128).transpose(2, 0, 1)
                                        for l in range(depth)]))
    cf32 = att_consts()
    cbf = np.stack([np.ones((128, 128), f32), np.eye(128, dtype=f32)]).astype(bf)
    ones_bf = np.ones((128, 128), bf)
    ims = []
    for r in range(NCORES):
        ids = [(r * GPC + j) % ng for j in range(GPC)]
        oh = np.zeros((128, 8), f32)
        oh[:, r] = 1
        ohp = np.zeros((128, 8), f32)
        if r > 0:
            ohp[:, r - 1] = 1
        ohd = np.zeros((8, 128, 128), f32)
        ohd[r] = np.eye(128, dtype=f32)
        halo = np.zeros((D, 2), f32) if r == 0 else x[0, r * TL - 2:r * TL, :].T
        ims.append({"wsrc": np.ascontiguousarray(np.stack([groups[i] for i in ids])),
                    "xT": np.ascontiguousarray(x[0, r * TL:(r + 1) * TL, :].T),
                    "xh0": np.ascontiguousarray(halo.reshape(16, 128, 2).transpose(1, 0, 2)),
                    "wf": wf, "bfb": bfb, "vecs": vecs, "cw": cw, "ones_bf": ones_bf, "cf32": cf32, "cbf": cbf,
                    "oh": oh, "ohp": ohp, "ohd": ohd.astype(bf)})
    if "F" not in _CACHE:
        _CACHE["F"] = build_fused(depth)
    res = run(_CACHE["F"], ims)
    out = np.concatenate([res[r]["xo"].T for r in range(NCORES)], axis=0)[None]
    return np.ascontiguousarray(out.astype(f32))


_CACHE = {}


def get_prog(name):
    if name not in _CACHE:
        _CACHE[name] = {"W": build_W, "A": build_A, "ATT": build_ATT, "B0": lambda: build_B(False), "B1": lambda: build_B(True)}[name]()
    return _CACHE[name]


def run(nc, in_maps):
    res = run_bass_kernel_spmd(nc, in_maps, core_ids=list(range(NCORES)))
    return res.results


def weight_groups(w_in, w_conv_out, w_attn_out, w_mix_out, w_ff1, w_ff2, l):
    gs = [w_in[l][:, 0:2048], w_in[l][:, 2048:4096], w_in[l][:, 4096:6144],
          w_in[l][:, 6152:8200], w_in[l][:, 8200:10248], w_mix_out[l],
          np.concatenate([w_conv_out[l], w_attn_out[l]], axis=0)]
    gs += [w_ff1[l][:, 2048 * i:2048 * (i + 1)] for i in range(4)]
    gs += [w_ff2[l][2048 * i:2048 * (i + 1), :] for i in range(4)]
    return gs


def convert_weights(groups):
    ng = len(groups)
    in_maps = []
    for cidx in range(NCORES):
        ids = [(cidx * GPC + j) % ng for j in range(GPC)]
        in_maps.append({"wsrc": np.ascontiguousarray(np.stack([groups[i] for i in ids]))})
    res = run(get_prog("W"), in_maps)
    out = [None] * ng
    for cidx in range(NCORES):
        for j in range(GPC):
            gi = cidx * GPC + j
            if gi < ng:
                out[gi] = res[cidx]["wb"][j]
    return out


def colvec(v):
    return np.ascontiguousarray(v.reshape(16, 128).T)


def att_consts():
    tri = np.triu(np.ones((128, 128), np.float32))
    tris = np.triu(np.ones((128, 128), np.float32), 1)
    cf32 = np.stack([tri, tris, np.ones((128, 128), np.float32), np.eye(128, dtype=np.float32)])
    return cf32


def kernel(x, g_mix, w_in, b_f, b_gate, conv_w, w_conv_out, w_attn_out, w_mix_out, g_mlp, w_ff1, w_ff2, g_final):
    f32 = np.float32
    x = np.asarray(x, f32)
    depth = w_in.shape[0]
    groups = []
    for l in range(depth):
        groups += weight_groups(w_in, w_conv_out, w_attn_out, w_mix_out, w_ff1, w_ff2, l)
    conv = convert_weights([np.asarray(g, f32) for g in groups])
    ones_bf = np.ones((128, 128), ml_dtypes.bfloat16)
    cf32 = att_consts()
    xT = [np.ascontiguousarray(x[0, r * TL:(r + 1) * TL, :].T) for r in range(NCORES)]
    for l in range(depth):
        wbl = np.ascontiguousarray(np.stack(conv[l * NGRP:(l + 1) * NGRP]))
        wf = np.ascontiguousarray(np.asarray(w_in[l][:, 6144:6152], f32).reshape(16, 128, 8).transpose(1, 0, 2))
        bfb = np.ascontiguousarray(np.tile(np.asarray(b_f[l], f32), (128, 4)))
        gm = colvec(np.asarray(g_mix[l], f32))
        ims = [{"xT": xT[r], "gmix": gm, "wq": np.ascontiguousarray(wbl[G_CVQ][8:16]),
                "wk": np.ascontiguousarray(wbl[G_KV][0:8]), "wv": np.ascontiguousarray(wbl[G_KV][8:16]),
                "wf": wf, "bfb": bfb, "ones_bf": ones_bf} for r in range(NCORES)]
        ra = run(get_prog("A"), ims)
        v_all = np.concatenate([ra[r]["v"] for r in range(NCORES)], axis=0)
        lp_all = np.concatenate([ra[r]["lp"] for r in range(NCORES)], axis=0)
        ims = []
        for hh in range(NH):
            qT = np.concatenate([ra[r]["qT"][hh] for r in range(NCORES)], axis=1)
            kT = np.concatenate([ra[r]["kT"][hh] for r in range(NCORES)], axis=1)
            vB = v_all[:, hh * 128:(hh + 1) * 128].reshape(NKB, 128, 128).transpose(1, 0, 2)
            lp = lp_all[:, hh].reshape(NKB, 128).T
            ims.append({"qT": np.ascontiguousarray(qT), "kT": np.ascontiguousarray(kT),
                        "vB": np.ascontiguousarray(vB), "lp": np.ascontiguousarray(lp),
                        "cf32": cf32})
        rt_ = run(get_prog("ATT"), ims)
        vecs = np.ascontiguousarray(np.stack([
            colvec(np.asarray(g_mix[l], f32)), colvec(np.asarray(g_mlp[l], f32)), colvec(np.asarray(g_final, f32)),
            colvec(np.asarray(b_gate[l][:D], f32)), colvec(np.asarray(b_gate[l][D:], f32))], axis=1))
        cw = np.ascontiguousarray(np.asarray(conv_w[l], f32).reshape(3, 8, 128).transpose(2, 0, 1))
        ims = []
        for r in range(NCORES):
            attnT = np.concatenate([rt_[hh]["oTok"][r * TL:(r + 1) * TL, :].T for hh in range(NH)], axis=0)
            if r == 0:
                halo = np.zeros((D, 2), f32)
            else:
                halo = xT[r - 1][:, TL - 2:TL]
            xh = np.ascontiguousarray(halo.reshape(16, 128, 2).transpose(1, 0, 2))
            ims.append({"xT": xT[r], "xh": xh, "attnT": np.ascontiguousarray(attnT), "wb": wbl,
                        "vecs": vecs, "cw": cw, "ones_bf": ones_bf})
        rb = run(get_prog("B1" if l == depth - 1 else "B0"), ims)
        xT = [rb[r]["xo"] for r in range(NCORES)]
    out = np.concatenate([xT[r].T for r in range(NCORES)], axis=0)[None]
    return np.ascontiguousarray(out.astype(f32))
```

```python
import contextlib
import numpy as np
import ml_dtypes
import concourse.bass as bass
import concourse.mybir as mybir
from concourse.bass_utils import run_bass_kernel_spmd

F32 = mybir.dt.float32
BF16 = mybir.dt.bfloat16
AF = mybir.ActivationFunctionType
ALU = mybir.AluOpType

NCORES = 8
D = 2048
S = 16384
TL = S // NCORES
T = 512
NT = TL // T
NH = 8
HD = 128
DFF = 8192
EPS = 1e-6
SCALE = float(HD) ** -0.5
NGRP = 15
G_CBCC, G_CVQ, G_KV, G_GC, G_GA, G_MIX, G_CA = 0, 1, 2, 3, 4, 5, 6
G_FF1 = 7
G_FF2 = 11


class Op:
    __slots__ = ("eng", "fn", "deps", "dma_key", "ms", "is_mm", "waits", "cc", "idx")

    def __init__(self, eng, fn, dma_key=None, is_mm=False):
        self.eng, self.fn, self.dma_key, self.is_mm = eng, fn, dma_key, is_mm
        self.cc = False
        self.deps = []
        self.ms = None
        self.waits = []


class Prog:
    ENGS = ("pe", "act", "dve", "pool", "sp")

    def __init__(self):
        self.ops = {e: [] for e in self.ENGS}
        self.last_w = {}
        self.readers = {}
        self.all_ops = []
        self.dma_keys = []
        self.last_dma = {}
        self.bar_deps = []
        self.bar_pending = set()

    def add(self, eng, fn, reads=(), writes=(), dma_key=None, is_mm=False):
        op = Op(eng, fn, dma_key, is_mm)
        deps = []
        for t in reads:
            w = self.last_w.get(t)
            if w is not None:
                deps.append(w)
        for t in writes:
            w = self.last_w.get(t)
            if w is not None:
                deps.append(w)
            deps.extend(self.readers.get(t, ()))
        seen = set()
        for d in deps:
            if d is op or id(d) in seen:
                continue
            if is_mm and d.is_mm and d.dma_key is None:
                continue
            seen.add(id(d))
            op.deps.append(d)
        if eng in self.bar_pending:
            self.bar_pending.discard(eng)
            for d in self.bar_deps:
                if d is not op and id(d) not in seen and not (d.eng == eng and d.dma_key is None):
                    seen.add(id(d))
                    op.deps.append(d)
        for t in reads:
            self.readers.setdefault(t, []).append(op)
        for t in writes:
            self.last_w[t] = op
            self.readers[t] = []
        if dma_key is not None:
            self.last_dma[dma_key] = op
        best = {}
        kept = []
        for d in op.deps:
            if d.dma_key is not None:
                kept.append(d)
            elif d.eng not in best or d.idx > best[d.eng].idx:
                best[d.eng] = d
        op.deps = kept + list(best.values())
        op.idx = len(self.ops[eng])
        self.ops[eng].append(op)
        self.all_ops.append(op)
        if dma_key is not None and dma_key not in self.dma_keys:
            self.dma_keys.append(dma_key)
        return op

    def dma(self, queue, out, in_, reads=(), writes=(), key=None):
        assert key is not None
        return self.add(queue, lambda e: e.dma_start(out=out, in_=in_), reads, writes, dma_key=key)

    def barrier(self):
        deps = [self.ops[e][-1] for e in self.ENGS if self.ops[e] and self.ops[e][-1].dma_key is None]
        deps += list(self.last_dma.values())
        self.bar_deps = deps
        self.bar_pending = set(self.ENGS)

    def collective(self, in_ap, out_ap, reads=(), writes=(), key=None):
        op = self.add("pool", lambda e: e.collective_compute(
            "AllGather", ALU.bypass, replica_groups=[list(range(NCORES))], ins=[in_ap], outs=[out_ap]),
            reads, writes, dma_key=key)
        op.cc = True
        return op

    def finalize(self):
        needed = set()
        for op in self.all_ops:
            for d in op.deps:
                needed.add(id(d))
        cnt = {e: 0 for e in self.ENGS}
        dcnt = {}
        final_dma = {}
        for op in self.all_ops:
            if op.dma_key is not None:
                dcnt[op.dma_key] = dcnt.get(op.dma_key, 0) + (1 if op.cc else 16)
                op.ms = ("dma:%s" % (op.dma_key,), dcnt[op.dma_key])
                final_dma[op.dma_key] = dcnt[op.dma_key]
            elif id(op) in needed:
                cnt[op.eng] += 1
                op.ms = ("eng:" + op.eng, cnt[op.eng])
        waited = {e: {} for e in self.ENGS}
        for e in self.ENGS:
            for op in self.ops[e]:
                w = {}
                for d in op.deps:
                    s, v = d.ms
                    if waited[e].get(s, 0) >= v:
                        continue
                    w[s] = max(w.get(s, 0), v)
                for s, v in w.items():
                    waited[e][s] = v
                op.waits = list(w.items())
        self.final_dma = final_dma
        self.max_counts = dict(cnt)

    def sem_names(self):
        return ["eng:" + e for e in self.ENGS] + ["dma:%s" % (k,) for k in self.dma_keys]

    def emit(self, nc, sems, final_wait_eng="sp"):
        engmap = {"pe": "tensor", "act": "scalar", "dve": "vector", "pool": "gpsimd", "sp": "sync"}
        with nc.Block() as block:
            for e in self.ENGS:
                ops = self.ops[e]
                fw = self.final_dma if e == final_wait_eng else {}

                def body(engine, ops=ops, fw=fw):
                    for op in ops:
                        for s, v in op.waits:
                            engine.wait_ge(sems[s], v)
                        ins = op.fn(engine)
                        if op.cc:
                            ins.then_inc(sems[op.ms[0]])
                        elif op.ms is not None:
                            ins.then_inc(sems[op.ms[0]], 16 if op.dma_key is not None else 1)
                    for k, v in fw.items():
                        engine.wait_ge(sems["dma:%s" % (k,)], v)

                getattr(block, engmap[e])(body)


class Ctx:
    def __init__(self):
        self.nc = bass.Bass("TRN2", target_bir_lowering=False)
        self.P = Prog()
        self.stack = contextlib.ExitStack()
        self.nsb = 0

    def dram_in(self, name, shape, dt):
        return self.nc.dram_tensor(name, list(shape), dt, kind="ExternalInput").ap()

    def dram_out(self, name, shape, dt):
        return self.nc.dram_tensor(name, list(shape), dt, kind="ExternalOutput").ap()

    def sb(self, name, shape, dt):
        return self.stack.enter_context(self.nc.sbuf_tensor(name, list(shape), dt))

    def ps(self, name):
        return self.stack.enter_context(self.nc.psum_tensor(name, [128, 512], F32))

    def finish(self):
        P = self.P
        P.finalize()
        sems = {}
        for i, n in enumerate(P.sem_names()):
            sems[n] = self.stack.enter_context(self.nc.semaphore("s%d" % i))
        P.emit(self.nc, sems)
        self.stack.close()
        return self.nc


GPC = 4


def build_W():
    c = Ctx()
    P = c.P
    wsrc = c.dram_in("wsrc", [GPC, 2048, 2048], F32)
    wb = c.dram_out("wb", [GPC, 16, 128, 2048], BF16)
    stage = [c.sb("stage%d" % i, [128, 2048], F32) for i in range(3)]
    wt = [c.sb("wt%d" % i, [128, 16, 16, 128], BF16) for i in range(2)]
    n = 0
    for g in range(GPC):
        w = wt[g % 2]
        for k in range(16):
            st = stage[n % 3]
            tok = ("stage", n % 3)
            P.dma("sp", st[:, :], wsrc[g, k * 128:(k + 1) * 128, :], writes=[tok], key=tok)
            src = st[:, :].rearrange("p (m c) -> p m c", m=16)
            dst = w[:, :, k, :]
            eng = "dve" if n % 2 == 0 else "pool"
            P.add(eng, lambda e, dst=dst, src=src: e.tensor_copy(out=dst, in_=src),
                  reads=[tok], writes=[("wt", g % 2, k)])
            n += 1
        P.dma("pool", wb[g].rearrange("m p x -> p m x"),
              w[:, :, :, :].rearrange("p m k c -> p m (k c)"),
              reads=[("wt", g % 2, k) for k in range(16)], key=("wtst", g % 2))
    return c.finish()


def emit_norm(c, xt, xtok, h, htok, gcol, ones_bf, sq, ps_stat, rt, rstd, uid):
    P = c.P
    for k in range(16):
        P.add("act", lambda e, k=k: e.activation(out=sq[:, k, :], in_=xt[:, k, :], func=AF.Square),
              reads=[xtok], writes=[("sq", k)])
    for k in range(16):
        P.add("pe", lambda e, k=k: e.matmul(ps_stat[:, :], lhsT=ones_bf[:, :], rhs=sq[:, k, :],
                                             start=(k == 0), stop=(k == 15)),
              reads=[("sq", k), "consts"], writes=[("ps", id(ps_stat))], is_mm=True)
    P.add("act", lambda e: e.activation(out=rt[:, :], in_=ps_stat[:, :], func=AF.Sqrt,
                                        bias=EPS, scale=1.0 / D),
          reads=[("ps", id(ps_stat))], writes=["rt"])
    P.add("dve", lambda e: e.reciprocal(out=rstd[:, :], in_=rt[:, :]), reads=["rt"], writes=["rstd"])
    for k in range(16):
        P.add("dve", lambda e, k=k: e.scalar_tensor_tensor(out=h[:, k, :], in0=xt[:, k, :],
                                                            scalar=gcol[:, k:k + 1], in1=rstd[:, :],
                                                            op0=ALU.mult, op1=ALU.mult),
              reads=[xtok, "rstd", "consts"], writes=[(htok, k)])


def build_A():
    c = Ctx()
    P = c.P
    xT = c.dram_in("xT", [D, TL], F32)
    gmix = c.dram_in("gmix", [128, 16], F32)
    wq = c.dram_in("wq", [8, 128, 2048], BF16)
    wk = c.dram_in("wk", [8, 128, 2048], BF16)
    wv = c.dram_in("wv", [8, 128, 2048], BF16)
    wf = c.dram_in("wf", [128, 16, 8], F32)
    bfb = c.dram_in("bfb", [128, 32], F32)
    ones_in = c.dram_in("ones_bf", [128, 128], BF16)
    qT_o = c.dram_out("qT", [NH, 128, TL], BF16)
    kT_o = c.dram_out("kT", [NH, 128, TL], BF16)
    v_o = c.dram_out("v", [TL, 1024], BF16)
    lp_o = c.dram_out("lp", [TL, 8], F32)

    g_sb = c.sb("g_sb", [128, 16], F32)
    ones_bf = c.sb("ones", [128, 128], BF16)
    wq_sb = c.sb("wq_sb", [128, 8, 16, 128], BF16)
    wk_sb = c.sb("wk_sb", [128, 8, 16, 128], BF16)
    wv_sb = c.sb("wv_sb", [128, 8, 16, 128], BF16)
    wf32 = c.sb("wf32", [128, 16, 8], F32)
    wf_sb = c.sb("wf_sb", [128, 16, 8], BF16)
    bf_sb = c.sb("bf_sb", [128, 32], F32)
    xt = [c.sb("xt%d" % i, [128, 16, T], F32) for i in range(2)]
    sq = c.sb("sq", [128, 16, T], BF16)
    h = c.sb("h", [128, 16, T], BF16)
    rt = c.sb("rt", [128, T], F32)
    rstd = c.sb("rstd", [128, T], F32)
    ev = [c.sb("ev%d" % i, [128, T], BF16) for i in range(4)]
    lpt = c.sb("lpt", [128, 32], F32)
    lpe = c.sb("lpe", [128, 32], F32)
    lps = [c.sb("lps%d" % i, [128, 32], F32) for i in range(2)]
    ps_stat = c.ps("ps_stat")
    psb = [c.ps("psb%d" % i) for i in range(4)]
    ps_f = c.ps("ps_f")

    P.dma("sp", g_sb[:, :], gmix[:, :], writes=["consts"], key="c0")
    P.dma("sp", ones_bf[:, :], ones_in[:, :], writes=["consts"], key="c0")
    P.dma("sp", bf_sb[:, :], bfb[:, :], writes=["consts"], key="c0")
    P.dma("sp", wf32[:, :, :], wf[:, :, :], writes=["wf32"], key="c1")
    P.add("dve", lambda e: e.tensor_copy(out=wf_sb[:, :, :], in_=wf32[:, :, :]), reads=["wf32"], writes=["wf"])
    P.dma("sp", xt[0][:, :, :], xT.rearrange("(k p) t -> p k t", p=128)[:, :, 0:T],
          writes=[("xt", 0)], key=("xt", 0))
    for nm, src, dst in (("wk", wk, wk_sb), ("wv", wv, wv_sb), ("wq", wq, wq_sb)):
        P.dma("pool", dst[:, :, :, :].rearrange("p m k c -> p m (k c)"),
              src.rearrange("m p x -> p m x"), writes=[nm], key=nm)

    nev = 0
    for t in range(NT):
        xs = xt[t % 2]
        xtok = ("xt", t % 2)
        if t + 1 < NT:
            P.dma("sp", xt[(t + 1) % 2][:, :, :],
                  xT.rearrange("(k p) t -> p k t", p=128)[:, :, (t + 1) * T:(t + 2) * T],
                  writes=[("xt", (t + 1) % 2)], key=("xt", (t + 1) % 2))
        emit_norm(c, xs, xtok, h, "h", g_sb, ones_bf, sq, ps_stat, rt, rstd, t)
        hreads = [("h", k) for k in range(16)]
        for nm, wsb, dst, sc in (("wk", wk_sb, kT_o, 1.0), ("wq", wq_sb, qT_o, SCALE)):
            for m in range(NH):
                pb = psb[nev % 4]
                ptok = ("ps", id(pb))
                for k in range(16):
                    P.add("pe", lambda e, pb=pb, wsb=wsb, m=m, k=k: e.matmul(
                        pb[:, :], lhsT=wsb[:, m, k, :], rhs=h[:, k, :], start=(k == 0), stop=(k == 15)),
                        reads=[nm] + hreads, writes=[ptok], is_mm=True)
                es = ev[nev % 4]
                etok = ("ev", nev % 4)
                P.add("act", lambda e, es=es, pb=pb, sc=sc: e.activation(
                    out=es[:, :], in_=pb[:, :], func=AF.Copy, scale=sc), reads=[ptok], writes=[etok])
                P.dma("pool", dst[m, :, t * T:(t + 1) * T], es[:, :], reads=[etok], key=etok)
                nev += 1
        for tb in range(4):
            for half in range(2):
                pb = psb[nev % 4]
                ptok = ("ps", id(pb))
                for k in range(16):
                    P.add("pe", lambda e, pb=pb, tb=tb, half=half, k=k: e.matmul(
                        pb[:, :], lhsT=h[:, k, tb * 128:(tb + 1) * 128],
                        rhs=wv_sb[:, 4 * half:4 * half + 4, k, :], start=(k == 0), stop=(k == 15)),
                        reads=["wv"] + hreads, writes=[ptok], is_mm=True)
                es = ev[nev % 4]
                etok = ("ev", nev % 4)
                P.add("dve", lambda e, es=es, pb=pb: e.tensor_copy(out=es[:, :], in_=pb[:, :]),
                      reads=[ptok], writes=[etok])
                r0 = t * T + tb * 128
                P.dma("pool", v_o[r0:r0 + 128, half * 512:(half + 1) * 512], es[:, :], reads=[etok], key=etok)
                nev += 1
        ftok = ("ps", id(ps_f))
        for tb in range(4):
            for k in range(16):
                P.add("pe", lambda e, tb=tb, k=k: e.matmul(
                    ps_f[:, tb * 8:(tb + 1) * 8], lhsT=h[:, k, tb * 128:(tb + 1) * 128],
                    rhs=wf_sb[:, k, :], start=(k == 0), stop=(k == 15)),
                    reads=["wf"] + hreads, writes=[ftok], is_mm=True)
        ls = lps[t % 2]
        ltok = ("lps", t % 2)
        P.add("dve", lambda e: e.tensor_tensor(out=lpt[:, :], in0=ps_f[:, 0:32], in1=bf_sb[:, :], op=ALU.add),
              reads=[ftok, "consts"], writes=["lpt"])
        P.add("act", lambda e: e.activation(out=lpe[:, :], in_=lpt[:, :], func=AF.Exp, scale=-1.0),
              reads=["lpt"], writes=["lpe"])
        P.add("act", lambda e, ls=ls: e.activation(out=ls[:, :], in_=lpe[:, :], func=AF.Ln, bias=1.0),
              reads=["lpe"], writes=[ltok])
        P.dma("pool", lp_o[t * T:(t + 1) * T, :].rearrange("(b p) h -> p b h", p=128),
              ls[:, :].rearrange("p (b h) -> p b h", b=4), reads=[ltok], key=ltok)
    return c.finish()


NQT = S // T
NKB = S // 128
VW = 130


def build_ATT():
    c = Ctx()
    P = c.P
    qT = c.dram_in("qT", [128, S], BF16)
    kT = c.dram_in("kT", [128, S], BF16)
    vB = c.dram_in("vB", [128, NKB, 128], BF16)
    lp = c.dram_in("lp", [128, NKB], F32)
    cf32 = c.dram_in("cf32", [4, 128, 128], F32)
    oTok = c.dram_out("oTok", [S, 128], BF16)

    k_sb = c.sb("k_sb", [128, S], BF16)
    v_sb = c.sb("v_sb", [128, NKB, VW], BF16)
    lp_sb = c.sb("lp_sb", [128, NKB], F32)
    cf = c.sb("cf", [128, 4, 128], F32)
    Tsb = c.sb("Tsb", [128, 2], F32)
    Dm = c.sb("Dm", [128, 128], F32)
    Cb = c.sb("Cb", [128, NKB], F32)
    Zf = c.sb("Zf", [128, T], F32)
    aug = [c.sb("aug%d" % i, [128, T], F32) for i in range(2)]
    q_sb = [c.sb("q_sb%d" % i, [128, T], BF16) for i in range(2)]
    e_sb = [c.sb("e_sb%d" % i, [128, T], F32) for i in range(4)]
    p_sb = [c.sb("p_sb%d" % i, [128, T], BF16) for i in range(5)]
    rl_sb = c.sb("rl_sb", [128, 4], F32)
    o_sb = [c.sb("o_sb%d" % i, [128, 4, 128], BF16) for i in range(2)]
    ps_s = [c.ps("ps_s%d" % i) for i in range(4)]
    ps_o = [[c.ps("ps_o%d_%d" % (i, j)) for j in range(2)] for i in range(2)]
    tri, tris, ones_f, ident_f = cf[:, 0, :], cf[:, 1, :], cf[:, 2, :], cf[:, 3, :]

    P.dma("sp", cf[:, :, :], cf32.rearrange("n p c -> p n c"), writes=["cf"], key="cf")
    P.dma("sp", lp_sb[:, :], lp[:, :], writes=["lp"], key="lp")
    P.dma("sp", q_sb[0][:, :], qT[:, 0:T], writes=[("q", 0)], key=("q", 0))
    P.dma("sp", k_sb[:, :], kT[:, :], writes=["k"], key="k")
    for hv in range(4):
        P.dma("pool", v_sb[:, 32 * hv:32 * (hv + 1), 0:128], vB[:, 32 * hv:32 * (hv + 1), :],
              writes=[("v", hv)], key=("v", hv))
    P.add("dve", lambda e: e.memset(v_sb[:, :, 128:VW], 1.0), writes=["v1"])

    P.add("pe", lambda e: e.matmul(ps_s[1][:, 0:2], lhsT=lp_sb[:, :], rhs=ones_f[:, 0:2], start=True, stop=True),
          reads=["lp", "cf"], writes=[("ps_s", 1)], is_mm=True)
    P.add("dve", lambda e: e.tensor_copy(out=Tsb[:, :], in_=ps_s[1][:, 0:2]), reads=[("ps_s", 1)], writes=["Tsb"])
    P.add("dve", lambda e: e.tensor_scalar(out=Dm[:, :], in0=tris, scalar1=Tsb[:, 0:1], scalar2=None, op0=ALU.mult),
          reads=["Tsb", "cf"], writes=["Dm"])
    P.add("pe", lambda e: e.matmul(ps_s[0][:, 0:128], lhsT=tri, rhs=lp_sb[:, :], start=True, stop=False),
          reads=["lp", "cf"], writes=[("ps_s", 0)], is_mm=True)
    P.add("pe", lambda e: e.matmul(ps_s[0][:, 0:128], lhsT=ones_f, rhs=Dm[:, :], start=False, stop=True),
          reads=["Dm", "cf"], writes=[("ps_s", 0)], is_mm=True)
    P.add("dve", lambda e: e.tensor_copy(out=Cb[:, :], in_=ps_s[0][:, 0:128]), reads=[("ps_s", 0)], writes=["Cb"])

    cnt = {"s": 0, "p": 0}

    def prep_aug(g):
        in0 = bass.AP(cf, 3 * 128, [[4 * 128, 128], [0, 4], [1, 128]])
        in1 = bass.AP(Cb, 4 * g, [[NKB, 128], [1, 4], [0, 128]])
        outz = Zf[:, :].rearrange("p (j t) -> p j t", j=4)
        P.add("dve", lambda e: e.tensor_tensor(out=outz, in0=in0, in1=in1, op=ALU.mult),
              reads=["cf", "Cb"], writes=["Zf"])
        si = cnt["s"] % 4
        cnt["s"] += 1
        pa = ps_s[si]
        P.add("pe", lambda e: e.matmul(pa[:, :], lhsT=ones_f, rhs=Zf[:, :], start=True, stop=True),
              reads=["cf", "Zf"], writes=[("ps_s", si)], is_mm=True)
        a = aug[g % 2]
        P.add("act", lambda e: e.activation(out=a[:, :], in_=pa[:, :], func=AF.Copy, scale=-1.0),
              reads=[("ps_s", si)], writes=[("aug", g % 2)])

    def tile(g):
        qs = q_sb[g % 2]
        qtok = ("q", g % 2)
        ag = aug[g % 2]
        agtok = ("aug", g % 2)
        if g + 1 < NQT:
            P.dma("sp", q_sb[(g + 1) % 2][:, :], qT[:, (g + 1) * T:(g + 2) * T],
                  writes=[("q", (g + 1) % 2)], key=("q", (g + 1) % 2))
        nkb = 4 * g + 4
        po = ps_o[g % 2]
        potok = [("ps_o", g % 2, 0), ("ps_o", g % 2, 1)]
        slots = []
        started = [False, False]
        last_c = {}
        def emit_st(kb):
            j = kb - 4 * g
            c0 = 128 * j if j > 0 else 0
            si = cnt["s"] % 4
            cnt["s"] += 1
            pss = ps_s[si]
            stok = ("ps_s", si)
            P.add("pe", lambda e: e.matmul(pss[:, c0:T], lhsT=k_sb[:, kb * 128:(kb + 1) * 128], rhs=qs[:, c0:T],
                                           start=True, stop=True), reads=["k", qtok], writes=[stok], is_mm=True)
            es = e_sb[si]
            etok = ("e", si)
            P.add("dve", lambda e: e.tensor_tensor(out=es[:, c0:T], in0=pss[:, c0:T], in1=ag[:, c0:T], op=ALU.add),
                  reads=[stok, agtok], writes=[etok])
            pi = cnt["p"] % 5
            cnt["p"] += 1
            pt = p_sb[pi]
            ptok = ("p", pi)
            P.add("act", lambda e: e.activation(out=pt[:, c0:T], in_=es[:, c0:T], func=AF.Exp,
                                                bias=Cb[:, kb:kb + 1], scale=1.0),
                  reads=[etok, "Cb"], writes=[ptok])
            if j >= 0:
                P.add("pool", lambda e: e.affine_select(out=pt[:, c0:c0 + 128], in_=pt[:, c0:c0 + 128],
                                                        pattern=[[1, 128]], compare_op=ALU.is_ge, fill=0.0,
                                                        base=0, channel_multiplier=-1),
                      reads=[ptok], writes=[ptok])
            slots.append((pt, ptok, j))

        def emit_pv(kb):
            pt, ptok, j = slots[kb]
            for cc in range(max(j, 0), 4):
                bnk = cc // 2
                first = not started[bnk]
                started[bnk] = True
                last = (kb == 4 * g + 2 * bnk + 1) and (cc == 2 * bnk + 1)
                out = po[bnk][:, (cc % 2) * VW:(cc % 2) * VW + 129]
                P.add("pe", lambda e, out=out, cc=cc, first=first, last=last: e.matmul(
                    out, lhsT=pt[:, cc * 128:(cc + 1) * 128], rhs=v_sb[:, kb, 0:129], start=first, stop=last),
                    reads=[("v", kb // 32), "v1", ptok], writes=[potok[bnk]], is_mm=True)

        for i in range(nkb + 3):
            if i < nkb:
                emit_st(i)
            if i == min(8, nkb - 1) and g + 1 < NQT:
                prep_aug(g + 1)
            if i >= 3:
                emit_pv(i - 3)
        os_ = o_sb[g % 2]
        otok = ("o", g % 2)
        for bnk in range(2):
            src = bass.AP(po[bnk], 128, [[512, 128], [VW, 2]])
            P.add("dve", lambda e, src=src, bnk=bnk: e.reciprocal(out=rl_sb[:, 2 * bnk:2 * bnk + 2], in_=src),
                  reads=[potok[bnk]], writes=[("rl", bnk)])
            for h2 in range(2):
                cc = 2 * bnk + h2
                P.add("dve", lambda e, bnk=bnk, h2=h2, cc=cc: e.tensor_scalar(
                    out=os_[:, cc, :], in0=po[bnk][:, h2 * VW:h2 * VW + 128], scalar1=rl_sb[:, cc:cc + 1],
                    scalar2=None, op0=ALU.mult), reads=[potok[bnk], ("rl", bnk)], writes=[otok])
        P.dma("pool", oTok[g * T:(g + 1) * T, :].rearrange("(c p) d -> p c d", p=128), os_[:, :, :],
              reads=[otok], key=otok)

    prep_aug(0)
    for g in range(NQT):
        tile(g)
    return c.finish()


NW = 5


def build_B(final):
    c = Ctx()
    P = c.P
    xT = c.dram_in("xT", [D, TL], F32)
    xh = c.dram_in("xh", [128, 16, 2], F32)
    attnT = c.dram_in("attnT", [1024, TL], BF16)
    wb = c.dram_in("wb", [NGRP, 16, 128, 2048], BF16)
    vecs = c.dram_in("vecs", [128, 5, 16], F32)
    cwd = c.dram_in("cw", [128, 3, 8], F32)
    ones_in = c.dram_in("ones_bf", [128, 128], BF16)
    xo = c.dram_out("xo", [D, TL], F32)

    vec = c.sb("vec", [128, 5, 16], F32)
    cw = c.sb("cw_sb", [128, 3, 8], F32)
    ones_bf = c.sb("ones", [128, 128], BF16)
    xt = c.sb("xt", [128, 16, T], F32)
    sq = c.sb("sq", [128, 16, T], BF16)
    h = c.sb("h", [128, 16, T], BF16)
    rt = c.sb("rt", [128, T], F32)
    rstd = c.sb("rstd", [128, T], F32)
    xh_sb = c.sb("xh_sb", [128, 16, 2], F32)
    hh = c.sb("hh", [128, 16, 2], BF16)
    convy = c.sb("convy", [128, 8, T], BF16)
    at_sb = c.sb("at_sb", [128, 8, T], BF16)
    merged = c.sb("merged", [128, 16, T], BF16)
    u = c.sb("u", [128, 32, T], BF16)
    wt = [c.sb("wt%d" % i, [128, 16, 128], BF16) for i in range(NW)]
    w2t = [c.sb("w2t%d" % i, [128, 32, 128], BF16) for i in range(2)]
    tcc = c.sb("tcc", [128, T], F32)
    tch = c.sb("tch", [128, 2], F32)
    usb = c.sb("usb", [128, T + 2], F32)
    uh = c.sb("uh", [128, 8, 2], F32)
    t1 = c.sb("t1", [128, T], F32)
    sg = c.sb("sg", [128, T], F32)
    sa = c.sb("sa", [128, T], F32)
    m1 = c.sb("m1", [128, T], F32)
    m2 = c.sb("m2", [128, T], F32)
    rr = [c.sb("rr%d" % i, [128, T], BF16) for i in range(2)]
    ps_stat = c.ps("ps_stat")
    psH = c.ps("psH")
    psg = [c.ps("psg%d" % i) for i in range(6)]
    st = {"nw": 0, "nw2": 0, "np": 0, "nr": 0}
    gmix, gmlp, gfin, bgc, bga = (vec[:, i, :] for i in range(5))

    P.dma("sp", vec[:, :, :], vecs[:, :, :], writes=["consts"], key="c0")
    P.dma("sp", cw[:, :, :], cwd[:, :, :], writes=["consts"], key="c0")
    P.dma("sp", ones_bf[:, :], ones_in[:, :], writes=["consts"], key="c0")
    P.dma("sp", xh_sb[:, :, :], xh[:, :, :], writes=["xh"], key="xh")

    XT = [("x", k) for k in range(16)]
    HT = [("h", k) for k in range(16)]

    def load_w(g, m):
        sl = st["nw"] % NW
        st["nw"] += 1
        tok = ("w", sl)
        P.dma("sp", wt[sl][:, :, :].rearrange("p k c -> p (k c)"), wb[g, m], writes=[tok], key=tok)
        return wt[sl], tok

    def bank():
        b = psg[st["np"] % 6]
        st["np"] += 1
        return b, ("psg", id(b))

    def gemm(w, wtok, ks, rhs_of, rtoks, ncols=T, dest=None, koff=0):
        if dest is None:
            b, btok = bank()
            out = b[:, 0:ncols]
        else:
            out, btok = dest
        n = len(ks)
        for i, k in enumerate(ks):
            P.add("pe", lambda e, out=out, k=k, i=i: e.matmul(out, lhsT=w[:, koff + k, :], rhs=rhs_of(k),
                                                              start=(i == 0), stop=(i == n - 1)),
                  reads=[wtok] + rtoks, writes=[btok], is_mm=True)
        return out, btok

    def norm_small():
        for k in range(16):
            P.add("act", lambda e, k=k: e.activation(out=sq[:, k, 0:2], in_=xh_sb[:, k, :], func=AF.Square),
                  reads=["xh"], writes=[("sq", k)])
        for k in range(16):
            P.add("pe", lambda e, k=k: e.matmul(ps_stat[:, 0:2], lhsT=ones_bf[:, :], rhs=sq[:, k, 0:2],
                                                 start=(k == 0), stop=(k == 15)),
                  reads=[("sq", k), "consts"], writes=[("ps", id(ps_stat))], is_mm=True)
        P.add("act", lambda e: e.activation(out=rt[:, 0:2], in_=ps_stat[:, 0:2], func=AF.Sqrt, bias=EPS, scale=1.0 / D),
              reads=[("ps", id(ps_stat))], writes=["rt"])
        P.add("dve", lambda e: e.reciprocal(out=rstd[:, 0:2], in_=rt[:, 0:2]), reads=["rt"], writes=["rstd"])
        for k in range(16):
            P.add("dve", lambda e, k=k: e.scalar_tensor_tensor(out=hh[:, k, :], in0=xh_sb[:, k, :],
                                                                scalar=gmix[:, k:k + 1], in1=rstd[:, 0:2],
                                                                op0=ALU.mult, op1=ALU.mult),
                  reads=["xh", "rstd", "consts"], writes=["hh"])

    def norm_main(gcol, out_h=True):
        for k in range(16):
            P.add("act", lambda e, k=k: e.activation(out=sq[:, k, :], in_=xt[:, k, :], func=AF.Square),
                  reads=[("x", k)], writes=[("sq", k)])
        for k in range(16):
            P.add("pe", lambda e, k=k: e.matmul(ps_stat[:, :], lhsT=ones_bf[:, :], rhs=sq[:, k, :],
                                                 start=(k == 0), stop=(k == 15)),
                  reads=[("sq", k), "consts"], writes=[("ps", id(ps_stat))], is_mm=True)
        P.add("act", lambda e: e.activation(out=rt[:, :], in_=ps_stat[:, :], func=AF.Sqrt, bias=EPS, scale=1.0 / D),
              reads=[("ps", id(ps_stat))], writes=["rt"])
        P.add("dve", lambda e: e.reciprocal(out=rstd[:, :], in_=rt[:, :]), reads=["rt"], writes=["rstd"])
        for k in range(16):
            if out_h:
                P.add("dve", lambda e, k=k: e.scalar_tensor_tensor(out=h[:, k, :], in0=xt[:, k, :],
                                                                    scalar=gcol[:, k:k + 1], in1=rstd[:, :],
                                                                    op0=ALU.mult, op1=ALU.mult),
                      reads=[("x", k), "rstd", "consts"], writes=[("h", k)])
            else:
                P.add("dve", lambda e, k=k: e.scalar_tensor_tensor(out=xt[:, k, :], in0=xt[:, k, :],
                                                                    scalar=gcol[:, k:k + 1], in1=rstd[:, :],
                                                                    op0=ALU.mult, op1=ALU.mult),
                      reads=[("x", k), "rstd", "consts"], writes=[("x", k)])

    def tile(t):
        cs = slice(t * T, (t + 1) * T)
        P.dma("sp", xt[:, :, :], xT.rearrange("(k p) t -> p k t", p=128)[:, :, cs], writes=XT, key="xt")
        P.dma("pool", at_sb[:, :, :], attnT.rearrange("(k p) t -> p k t", p=128)[:, :, cs],
              writes=["at"], key="at")
        if t == 0:
            norm_small()
        norm_main(gmix)
        for j in range(8):
            wcc, tcc_w = load_w(G_CBCC, 8 + j)
            wcv, tcv_w = load_w(G_CVQ, j)
            wcb, tcb_w = load_w(G_CBCC, j)
            pa, patok = gemm(wcc, tcc_w, range(16), lambda k: h[:, k, :], HT)
            P.add("act", lambda e, pa=pa: e.activation(out=tcc[:, :], in_=pa, func=AF.Copy),
                  reads=[patok], writes=["tcc"])
            if t == 0:
                gemm(wcc, tcc_w, range(16), lambda k: hh[:, k, :], ["hh"], dest=(psH[:, 0:2], "psH"))
                P.add("act", lambda e: e.activation(out=tch[:, :], in_=psH[:, 0:2], func=AF.Copy),
                      reads=["psH"], writes=["tch"])
            pv, pvtok = gemm(wcv, tcv_w, range(16), lambda k: h[:, k, :], HT)
            if t == 0:
                gemm(wcv, tcv_w, range(16), lambda k: hh[:, k, :], ["hh"], dest=(psH[:, 2:4], "psH2"))
                P.add("dve", lambda e: e.tensor_tensor(out=usb[:, 0:2], in0=psH[:, 2:4], in1=tch[:, :], op=ALU.mult),
                      reads=["psH2", "tch"], writes=["usb"])
            else:
                P.add("pool", lambda e, j=j: e.tensor_copy(out=usb[:, 0:2], in_=uh[:, j, :]),
                      reads=[("uh", j)], writes=["usb"])
            P.add("dve", lambda e, pv=pv: e.tensor_tensor(out=usb[:, 2:T + 2], in0=pv, in1=tcc[:, :], op=ALU.mult),
                  reads=[pvtok, "tcc", "usb"], writes=["usb"])
            P.add("pool", lambda e, j=j: e.tensor_copy(out=uh[:, j, :], in_=usb[:, T:T + 2]),
                  reads=["usb"], writes=[("uh", j)])
            P.add("dve", lambda e, j=j: e.tensor_scalar(out=t1[:, :], in0=usb[:, 2:T + 2], scalar1=cw[:, 2, j:j + 1],
                                                         scalar2=None, op0=ALU.mult),
                  reads=["usb", "consts"], writes=["t1"])
            for tap, off in ((1, 1), (0, 0)):
                P.add("dve", lambda e, j=j, tap=tap, off=off: e.scalar_tensor_tensor(
                    out=t1[:, :], in0=usb[:, off:off + T], scalar=cw[:, tap, j:j + 1], in1=t1[:, :],
                    op0=ALU.mult, op1=ALU.add), reads=["usb", "t1", "consts"], writes=["t1"])
            pc, pctok = gemm(wcb, tcb_w, range(16), lambda k: h[:, k, :], HT)
            P.add("dve", lambda e, pc=pc, j=j: e.tensor_tensor(out=convy[:, j, :], in0=pc, in1=t1[:, :], op=ALU.mult),
                  reads=[pctok, "t1"], writes=[("cy", j)])
        CY = [("cy", k) for k in range(8)]
        for j in range(16):
            wgc, tgc = load_w(G_GC, j)
            wca, tca = load_w(G_CA, j)
            wga, tga = load_w(G_GA, j)
            p1, p1t = gemm(wgc, tgc, range(16), lambda k: h[:, k, :], HT)
            P.add("act", lambda e, p1=p1, j=j: e.activation(out=sg[:, :], in_=p1, func=AF.Sigmoid,
                                                             bias=bgc[:, j:j + 1], scale=1.0),
                  reads=[p1t, "consts"], writes=["sg"])
            p2, p2t = gemm(wca, tca, range(8), lambda k: convy[:, k, :], CY)
            P.add("dve", lambda e, p2=p2: e.tensor_tensor(out=m1[:, :], in0=p2, in1=sg[:, :], op=ALU.mult),
                  reads=[p2t, "sg"], writes=["m1"])
            p3, p3t = gemm(wga, tga, range(16), lambda k: h[:, k, :], HT)
            P.add("act", lambda e, p3=p3, j=j: e.activation(out=sa[:, :], in_=p3, func=AF.Sigmoid,
                                                             bias=bga[:, j:j + 1], scale=1.0),
                  reads=[p3t, "consts"], writes=["sa"])
            p4, p4t = gemm(wca, tca, range(8), lambda k: at_sb[:, k, :], ["at"], koff=8)
            P.add("dve", lambda e, p4=p4: e.tensor_tensor(out=m2[:, :], in0=p4, in1=sa[:, :], op=ALU.mult),
                  reads=[p4t, "sa"], writes=["m2"])
            P.add("pool", lambda e, j=j: e.tensor_tensor(out=merged[:, j, :], in0=m1[:, :], in1=m2[:, :], op=ALU.add),
                  reads=["m1", "m2"], writes=[("mg", j)])
        MG = [("mg", k) for k in range(16)]
        for j in range(16):
            wm, tm = load_w(G_MIX, j)
            p5, p5t = gemm(wm, tm, range(16), lambda k: merged[:, k, :], MG)
            P.add("dve", lambda e, p5=p5, j=j: e.tensor_tensor(out=xt[:, j, :], in0=p5, in1=xt[:, j, :], op=ALU.add),
                  reads=[p5t, ("x", j)], writes=[("x", j)])
        norm_main(gmlp)
        for half in range(2):
            for f in range(32):
                fg = half * 32 + f
                w1, t1w = load_w(G_FF1 + fg // 16, fg % 16)
                p6, p6t = gemm(w1, t1w, range(16), lambda k: h[:, k, :], HT)
                ri = st["nr"] % 2
                st["nr"] += 1
                rrs = rr[ri]
                P.add("act", lambda e, p6=p6, rrs=rrs: e.activation(out=rrs[:, :], in_=p6, func=AF.Relu),
                      reads=[p6t], writes=[("rr", ri)])
                P.add("dve", lambda e, p6=p6, rrs=rrs, f=f: e.tensor_tensor(out=u[:, f, :], in0=p6, in1=rrs[:, :], op=ALU.mult),
                      reads=[p6t, ("rr", ri)], writes=[("u", f)])
            UT = [("u", f) for f in range(32)]
            for j in range(16):
                s2 = st["nw2"] % 2
                st["nw2"] += 1
                w2 = w2t[s2]
                toks = []
                for q in range(2):
                    tok = ("w2", s2, q)
                    P.dma("sp", w2[:, 16 * q:16 * (q + 1), :].rearrange("p k c -> p (k c)"),
                          wb[G_FF2 + 2 * half + q, j], writes=[tok], key=tok)
                    toks.append(tok)
                b, btok = bank()
                for f in range(32):
                    P.add("pe", lambda e, b=b, w2=w2, f=f: e.matmul(b[:, :], lhsT=w2[:, f, :], rhs=u[:, f, :],
                                                                     start=(f == 0), stop=(f == 31)),
                          reads=toks + UT, writes=[btok], is_mm=True)
                P.add("dve", lambda e, b=b, j=j: e.tensor_tensor(out=xt[:, j, :], in0=b[:, :], in1=xt[:, j, :], op=ALU.add),
                      reads=[btok, ("x", j)], writes=[("x", j)])
        if final:
            norm_main(gfin, out_h=False)
        P.dma("pool", xo.rearrange("(k p) t -> p k t", p=128)[:, :, cs], xt[:, :, :], reads=XT, key="xo")

    for t in range(NT):
        tile(t)
    return c.finish()


class FCtx:
    def __init__(self):
        self.nc = bass.Bass("TRN2", target_bir_lowering=False)
        self.P = Prog()
        self.off = 0
        self.n = 0
        self.banks = [self.nc.alloc_psum_tensor("bank%d" % i, [128, 512], F32) for i in range(8)]
        self.ncc = 0
        self.emulate = False

    def dram_in(self, name, shape, dt):
        return self.nc.dram_tensor(name, list(shape), dt, kind="ExternalInput").ap()

    def dram_out(self, name, shape, dt):
        return self.nc.dram_tensor(name, list(shape), dt, kind="ExternalOutput").ap()

    def dram(self, name, shape, dt):
        return self.nc.dram_tensor(name, list(shape), dt)

    SB_BASE = 16384

    def phase(self):
        self.off = self.SB_BASE
        self.P.barrier()

    def sb(self, name, shape, dt):
        nbytes = int(np.prod(shape[1:])) * (4 if dt == F32 else 2)
        off = (self.off + 63) // 64 * 64
        self.n += 1
        t = self.nc.alloc_sbuf_tensor_at("%s_%d" % (name, self.n), list(shape), dt, offset=off)
        self.off = off + nbytes
        assert self.off <= self.SB_BASE + 206 * 1024, (name, self.off)
        return t

    def allgather(self, src_t, dst_t, rtok, wtok):
        self.ncc += 1
        if self.emulate:
            n0 = src_t.ap().shape[0]
            op = None
            for r in range(NCORES):
                op = self.P.dma("pool", dst_t.ap()[r * n0:(r + 1) * n0], src_t.ap(), reads=[rtok], writes=[wtok],
                                key=("cc", self.ncc))
            return op
        return self.P.collective(src_t.ap().opt(), dst_t.ap().opt(), reads=[rtok], writes=[wtok],
                                 key=("cc", self.ncc))

    def finish(self):
        P = self.P
        P.finalize()
        print("fused: sems", len(P.sem_names()), "milestones", P.max_counts,
              "ops", {e: len(P.ops[e]) for e in P.ENGS}, flush=True)
        sems = {}
        with contextlib.ExitStack() as st:
            for i, n in enumerate(P.sem_names()):
                sems[n] = st.enter_context(self.nc.semaphore("s%d" % i))
            P.emit(self.nc, sems)
        return self.nc


def f_norm(c, xt, h, gcol, ones_bf, sq, ps_stat, rt, rstd, ncol=T, xtoks=None, htoks=None, inplace=False):
    P = c.P
    for k in range(16):
        P.add("act", lambda e, k=k: e.activation(out=sq[:, k, 0:ncol], in_=xt[:, k, 0:ncol], func=AF.Square),
              reads=[xtoks[k]], writes=[("sq", k)])
    for k in range(16):
        P.add("pe", lambda e, k=k: e.matmul(ps_stat[:, 0:ncol], lhsT=ones_bf[:, :], rhs=sq[:, k, 0:ncol],
                                             start=(k == 0), stop=(k == 15)),
              reads=[("sq", k), "consts"], writes=["ps_stat"], is_mm=True)
    P.add("act", lambda e: e.activation(out=rt[:, 0:ncol], in_=ps_stat[:, 0:ncol], func=AF.Sqrt, bias=EPS, scale=1.0 / D),
          reads=["ps_stat"], writes=["rt"])
    P.add("dve", lambda e: e.reciprocal(out=rstd[:, 0:ncol], in_=rt[:, 0:ncol]), reads=["rt"], writes=["rstd"])
    for k in range(16):
        dst = xt if inplace else h
        P.add("dve", lambda e, k=k, dst=dst: e.scalar_tensor_tensor(out=dst[:, k, 0:ncol], in0=xt[:, k, 0:ncol],
                                                                    scalar=gcol[:, k:k + 1], in1=rstd[:, 0:ncol],
                                                                    op0=ALU.mult, op1=ALU.mult),
              reads=[xtoks[k], "rstd", "consts"], writes=[xtoks[k] if inplace else htoks[k]])


def f_phase_W(c, wsrc, wmy, wall):
    P = c.P
    c.phase()
    stage = [c.sb("stage", [128, 2048], F32) for i in range(3)]
    wt = [c.sb("wt", [128, 16, 16, 128], BF16) for i in range(2)]
    n = 0
    for g in range(GPC):
        w = wt[g % 2]
        for k in range(16):
            st = stage[n % 3]
            tok = ("stage", n % 3)
            P.dma("sp", st[:, :], wsrc[g, k * 128:(k + 1) * 128, :], writes=[tok], key=tok)
            src = st[:, :].rearrange("p (m c) -> p m c", m=16)
            dst = w[:, :, k, :]
            eng = "dve" if n % 2 == 0 else "pool"
            P.add(eng, lambda e, dst=dst, src=src: e.tensor_copy(out=dst, in_=src),
                  reads=[tok], writes=[("wt", g % 2, k)])
            n += 1
        for half in range(2):
            dtok = ("d", "wmy", g, half)
            P.dma("sp", wmy[g][half].ap().rearrange("m p x -> p m x"),
                  w[:, 8 * half:8 * half + 8, :, :].rearrange("p m k c -> p m (k c)"),
                  reads=[("wt", g % 2, k) for k in range(16)], writes=[dtok], key=("wtst", g % 2, half))
            c.allgather(wmy[g][half], wall[g][half], dtok, ("d", "wall", g, half))


def wsl(wall, l, gi, m):
    G = l * NGRP + gi
    r, j = G // GPC, G % GPC
    half, mm = m // 8, m % 8
    return wall[j][half].ap()[r * 8 + mm], ("d", "wall", j, half)


def f_phase_A(c, l, x_ap, xtok_d, wall, wf_in, vecs_in, bfb_in, ones_in, q_my, k_my, v_my, lp_my,
              q_all, k_all, v_all, lp_all):
    P = c.P
    c.phase()
    g_sb = c.sb("g_sb", [128, 16], F32)
    ones_bf = c.sb("ones", [128, 128], BF16)
    w_sb = {nm: c.sb("w" + nm, [128, 8, 16, 128], BF16) for nm in ("q", "k", "v")}
    wf32 = c.sb("wf32", [128, 16, 8], F32)
    wf_sb = c.sb("wf_sb", [128, 16, 8], BF16)
    bf_sb = c.sb("bf_sb", [128, 32], F32)
    xt = [c.sb("xt", [128, 16, T], F32) for i in range(2)]
    sq = c.sb("sq", [128, 16, T], BF16)
    h = c.sb("h", [128, 16, T], BF16)
    rt = c.sb("rt", [128, T], F32)
    rstd = c.sb("rstd", [128, T], F32)
    ev = [c.sb("ev", [128, T], BF16) for i in range(4)]
    lpt = c.sb("lpt", [128, 32], F32)
    lpe = c.sb("lpe", [128, 32], F32)
    lps = [c.sb("lps", [128, 32], F32) for i in range(2)]
    ps_stat, ps_f = c.banks[0], c.banks[1]
    psb = c.banks[2:6]

    P.dma("sp", g_sb[:, :], vecs_in[:, 5 * l + 0, :], writes=["consts"], key="c0")
    P.dma("sp", ones_bf[:, :], ones_in[:, :], writes=["consts"], key="c0")
    P.dma("sp", bf_sb[:, :], bfb_in[l], writes=["consts"], key="c0")
    P.dma("sp", wf32[:, :, :], wf_in[l], writes=["wf32"], key="c1")
    P.add("dve", lambda e: e.tensor_copy(out=wf_sb[:, :, :], in_=wf32[:, :, :]), reads=["wf32"], writes=["wf"])
    xv = x_ap.rearrange("(k p) t -> p k t", p=128)
    XT = [[("xt", i, k) for k in range(16)] for i in range(2)]
    HT = [("h", k) for k in range(16)]
    P.dma("sp", xt[0][:, :, :], xv[:, :, 0:T], reads=[xtok_d], writes=XT[0], key=("xt", 0))
    for nm, gi, m0 in (("k", G_KV, 0), ("v", G_KV, 8), ("q", G_CVQ, 8)):
        for m in range(8):
            src, dtok = wsl(wall, l, gi, m0 + m)
            P.dma("pool", w_sb[nm][:, m, :, :].rearrange("p k c -> p (k c)"), src, reads=[dtok],
                  writes=["w" + nm], key="w" + nm)
    nev = 0
    for t in range(NT):
        if t + 1 < NT:
            P.dma("sp", xt[(t + 1) % 2][:, :, :], xv[:, :, (t + 1) * T:(t + 2) * T], reads=[xtok_d],
                  writes=XT[(t + 1) % 2], key=("xt", (t + 1) % 2))
        f_norm(c, xt[t % 2], h, g_sb, ones_bf, sq, ps_stat, rt, rstd, xtoks=XT[t % 2], htoks=HT)
        for nm, dst, sc in (("k", k_my, 1.0), ("q", q_my, SCALE), ("v", v_my, 1.0)):
            wsb = w_sb[nm]
            for m in range(NH):
                pb = psb[nev % 4]
                ptok = ("psb", nev % 4)
                for k in range(16):
                    P.add("pe", lambda e, pb=pb, wsb=wsb, m=m, k=k: e.matmul(
                        pb[:, :], lhsT=wsb[:, m, k, :], rhs=h[:, k, :], start=(k == 0), stop=(k == 15)),
                        reads=["w" + nm, HT[k]], writes=[ptok], is_mm=True)
                es = ev[nev % 4]
                etok = ("ev", nev % 4)
                P.add("act", lambda e, es=es, pb=pb, sc=sc: e.activation(
                    out=es[:, :], in_=pb[:, :], func=AF.Copy, scale=sc), reads=[ptok], writes=[etok])
                P.dma("pool", dst.ap()[m, :, t * T:(t + 1) * T], es[:, :], reads=[etok],
                      writes=[("d", nm + "_my")], key=etok)
                nev += 1
        for tb in range(4):
            for k in range(16):
                P.add("pe", lambda e, tb=tb, k=k: e.matmul(
                    ps_f[:, tb * 8:(tb + 1) * 8], lhsT=h[:, k, tb * 128:(tb + 1) * 128],
                    rhs=wf_sb[:, k, :], start=(k == 0), stop=(k == 15)),
                    reads=["wf", HT[k]], writes=["ps_f"], is_mm=True)
        ls = lps[t % 2]
        ltok = ("lps", t % 2)
        P.add("dve", lambda e: e.tensor_tensor(out=lpt[:, :], in0=ps_f[:, 0:32], in1=bf_sb[:, :], op=ALU.add),
              reads=["ps_f", "consts"], writes=["lpt"])
        P.add("act", lambda e: e.activation(out=lpe[:, :], in_=lpt[:, :], func=AF.Exp, scale=-1.0),
              reads=["lpt"], writes=["lpe"])
        P.add("act", lambda e, ls=ls: e.activation(out=ls[:, :], in_=lpe[:, :], func=AF.Ln, bias=1.0),
              reads=["lpe"], writes=[ltok])
        P.dma("pool", lp_my.ap()[t * T:(t + 1) * T, :].rearrange("(b p) h -> p b h", p=128),
              ls[:, :].rearrange("p (b h) -> p b h", b=4), reads=[ltok], writes=[("d", "lp_my")], key=ltok)
    P.barrier()
    for nm, my, al in (("k", k_my, k_all), ("v", v_my, v_all), ("q", q_my, q_all), ("lp", lp_my, lp_all)):
        c.allgather(my, al, ("d", nm + "_my"), ("d", nm + "_all"))


def f_phase_ATT(c, q_all, k_all, v_all, lp_all, cf32_in, cbf_in, oh_in, ohd_in, o_my, o_all):
    P = c.P
    c.phase()
    k_sb = c.sb("k_sb", [128, S], BF16)
    v_sb = c.sb("v_sb", [128, NKB, VW], BF16)
    lp8 = c.sb("lp8", [128, NKB, 8], F32)
    lp_sb = c.sb("lp_sb", [128, NKB], F32)
    cf = c.sb("cf", [128, 4, 128], F32)
    cbf = c.sb("cbf", [128, 2, 128], BF16)
    oh = c.sb("oh", [128, 8], F32)
    ohd = c.sb("ohd", [128, 8, 128], BF16)
    Tsb = c.sb("Tsb", [128, 2], F32)
    Dm = c.sb("Dm", [128, 128], F32)
    Cb = c.sb("Cb", [128, NKB], F32)
    Zf = c.sb("Zf", [128, T], F32)
    aug = [c.sb("aug", [128, T], F32) for i in range(2)]
    sel_in = [c.sb("sel_in", [128, 8, T], BF16) for i in range(2)]
    q_sb = [c.sb("q_sb", [128, T], BF16) for i in range(2)]
    e_sb = [c.sb("e_sb", [128, T], F32) for i in range(4)]
    p_sb = [c.sb("p_sb", [128, T], BF16) for i in range(5)]
    rl_sb = c.sb("rl_sb", [128, 4], F32)
    o_sb = [c.sb("o_sb", [128, 4, 128], BF16) for i in range(2)]
    oT_sb = [c.sb("oT_sb", [128, T], BF16) for i in range(2)]
    ps_s = c.banks[0:4]
    ps_o = [c.banks[4:6], c.banks[6:8]]
    tri, tris, ones_f, ident_f = cf[:, 0, :], cf[:, 1, :], cf[:, 2, :], cf[:, 3, :]
    ident_b = cbf[:, 1, :]
    cnt = {"s": 0, "p": 0, "sel": 0}

    P.dma("sp", cf[:, :, :], cf32_in.rearrange("n p c -> p n c"), writes=["cf"], key="cf")
    P.dma("sp", cbf[:, :, :], cbf_in.rearrange("n p c -> p n c"), writes=["cbf"], key="cf")
    P.dma("sp", oh[:, :], oh_in[:, :], writes=["oh"], key="cf")
    P.dma("sp", ohd[:, :, :], ohd_in.rearrange("n p c -> p n c"), writes=["ohd"], key="cf")
    for i in range(4):
        P.dma("pool", lp8[:, 32 * i:32 * (i + 1), :],
              lp_all.ap()[4096 * i:4096 * (i + 1), :].rearrange("(b p) h -> p b h", p=128),
              reads=[("d", "lp_all")], writes=[("lp8", i)], key=("lp8", i))
    P.add("dve", lambda e: e.memset(v_sb[:, :, 128:VW], 1.0), writes=["v1"])
    for hh in range(8):
        if hh == 0:
            P.add("dve", lambda e: e.tensor_scalar(out=lp_sb[:, :], in0=lp8[:, :, 0], scalar1=oh[:, 0:1], scalar2=None,
                                                   op0=ALU.mult), reads=[("lp8", i) for i in range(4)] + ["oh"],
                  writes=["lp"])
        else:
            P.add("dve", lambda e, hh=hh: e.scalar_tensor_tensor(out=lp_sb[:, :], in0=lp8[:, :, hh],
                                                                  scalar=oh[:, hh:hh + 1], in1=lp_sb[:, :],
                                                                  op0=ALU.mult, op1=ALU.add),
                  reads=["lp", "oh"], writes=["lp"])

    def load_sel(src_all, dtok, r, piece):
        si = cnt["sel"] % 2
        cnt["sel"] += 1
        tok = ("sel", si)
        P.dma("sp", sel_in[si][:, :, :],
              src_all.ap()[r * 8:(r + 1) * 8, :, piece * T:(piece + 1) * T].rearrange("h p t -> p h t"),
              reads=[dtok], writes=[tok], key=tok)
        return sel_in[si], tok

    def next_bank():
        si = cnt["s"] % 4
        cnt["s"] += 1
        cnt["last_si"] = si
        return ps_s[si], ("ps_s", si)

    for r in range(8):
        for piece in range(4):
            sin, stok = load_sel(k_all, ("d", "k_all"), r, piece)
            pb, ptok = next_bank()
            for hh in range(8):
                P.add("pe", lambda e, pb=pb, sin=sin, hh=hh: e.matmul(pb[:, :], lhsT=ohd[:, hh, :], rhs=sin[:, hh, :],
                                                                       start=(hh == 0), stop=(hh == 7)),
                      reads=["ohd", stok], writes=[ptok], is_mm=True)
            c0 = r * TL + piece * T
            P.add("act", lambda e, pb=pb, c0=c0: e.activation(out=k_sb[:, c0:c0 + T], in_=pb[:, :], func=AF.Copy),
                  reads=[ptok], writes=[("k", c0 // T)])
    for r in range(8):
        for piece in range(4):
            sin, stok = load_sel(v_all, ("d", "v_all"), r, piece)
            pb, ptok = next_bank()
            for j in range(4):
                for hh in range(8):
                    P.add("pe", lambda e, pb=pb, sin=sin, hh=hh, j=j: e.matmul(
                        pb[:, j * 128:(j + 1) * 128], lhsT=sin[:, hh, j * 128:(j + 1) * 128], rhs=ohd[:, hh, :],
                        start=(j == 0 and hh == 0), stop=(j == 3 and hh == 7)),
                        reads=["ohd", stok], writes=[ptok], is_mm=True)
            b0 = (r * TL + piece * T) // 128
            P.add("dve", lambda e, pb=pb, b0=b0: e.tensor_copy(
                out=v_sb[:, b0:b0 + 4, 0:128], in_=pb[:, :].rearrange("p (j d) -> p j d", j=4)),
                reads=[ptok], writes=[("v", b0 // 4)])

    pbt, pbttok = next_bank()
    P.add("pe", lambda e: e.matmul(pbt[:, 0:2], lhsT=lp_sb[:, :], rhs=ones_f[:, 0:2], start=True, stop=True),
          reads=["lp", "cf"], writes=[pbttok], is_mm=True)
    P.add("dve", lambda e: e.tensor_copy(out=Tsb[:, :], in_=pbt[:, 0:2]), reads=[pbttok], writes=["Tsb"])
    P.add("dve", lambda e: e.tensor_scalar(out=Dm[:, :], in0=tris, scalar1=Tsb[:, 0:1], scalar2=None, op0=ALU.mult),
          reads=["Tsb", "cf"], writes=["Dm"])
    pbc, pbctok = next_bank()
    P.add("pe", lambda e: e.matmul(pbc[:, 0:128], lhsT=tri, rhs=lp_sb[:, :], start=True, stop=False),
          reads=["lp", "cf"], writes=[pbctok], is_mm=True)
    P.add("pe", lambda e: e.matmul(pbc[:, 0:128], lhsT=ones_f, rhs=Dm[:, :], start=False, stop=True),
          reads=["Dm", "cf"], writes=[pbctok], is_mm=True)
    P.add("dve", lambda e: e.tensor_copy(out=Cb[:, :], in_=pbc[:, 0:128]), reads=[pbctok], writes=["Cb"])

    def prep_tile(g):
        r, piece = g // 4, g % 4
        sin, stok = load_sel(q_all, ("d", "q_all"), r, piece)
        pb, ptok = next_bank()
        for hh in range(8):
            P.add("pe", lambda e, hh=hh: e.matmul(pb[:, :], lhsT=ohd[:, hh, :], rhs=sin[:, hh, :],
                                                   start=(hh == 0), stop=(hh == 7)),
                  reads=["ohd", stok], writes=[ptok], is_mm=True)
        qs = q_sb[g % 2]
        P.add("dve", lambda e: e.tensor_copy(out=qs[:, :], in_=pb[:, :]), reads=[ptok], writes=[("q", g % 2)])
        in0 = bass.AP(cf, 3 * 128, [[4 * 128, 128], [0, 4], [1, 128]])
        in1 = bass.AP(Cb, 4 * g, [[NKB, 128], [1, 4], [0, 128]])
        outz = Zf[:, :].rearrange("p (j t) -> p j t", j=4)
        P.add("dve", lambda e: e.tensor_tensor(out=outz, in0=in0, in1=in1, op=ALU.mult),
              reads=["cf", "Cb"], writes=["Zf"])
        pa, patok = next_bank()
        P.add("pe", lambda e: e.matmul(pa[:, :], lhsT=ones_f, rhs=Zf[:, :], start=True, stop=True),
              reads=["cf", "Zf"], writes=[patok], is_mm=True)
        a = aug[g % 2]
        P.add("act", lambda e: e.activation(out=a[:, :], in_=pa[:, :], func=AF.Copy, scale=-1.0),
              reads=[patok], writes=[("aug", g % 2)])

    def tile(g):
        qs = q_sb[g % 2]
        qtok = ("q", g % 2)
        ag = aug[g % 2]
        agtok = ("aug", g % 2)
        nkb = 4 * g + 4
        po = ps_o[g % 2]
        potok = [("ps_o", g % 2, 0), ("ps_o", g % 2, 1)]
        slots = []
        started = [False, False]

        def emit_st(kb):
            j = kb - 4 * g
            c0 = 128 * j if j > 0 else 0
            pss, stok = next_bank()
            si = cnt["last_si"]
            P.add("pe", lambda e: e.matmul(pss[:, c0:T], lhsT=k_sb[:, kb * 128:(kb + 1) * 128], rhs=qs[:, c0:T],
                                           start=True, stop=True), reads=[("k", kb // 4), qtok], writes=[stok],
                  is_mm=True)
            es = e_sb[si]
            etok = ("e", si)
            P.add("dve", lambda e: e.tensor_tensor(out=es[:, c0:T], in0=pss[:, c0:T], in1=ag[:, c0:T], op=ALU.add),
                  reads=[stok, agtok], writes=[etok])
            pi = cnt["p"] % 5
            cnt["p"] += 1
            pt = p_sb[pi]
            ptok = ("p", pi)
            P.add("act", lambda e: e.activation(out=pt[:, c0:T], in_=es[:, c0:T], func=AF.Exp,
                                                bias=Cb[:, kb:kb + 1], scale=1.0),
                  reads=[etok, "Cb"], writes=[ptok])
            if j >= 0:
                P.add("pool", lambda e: e.affine_select(out=pt[:, c0:c0 + 128], in_=pt[:, c0:c0 + 128],
                                                        pattern=[[1, 128]], compare_op=ALU.is_ge, fill=0.0,
                                                        base=0, channel_multiplier=-1),
                      reads=[ptok], writes=[ptok])
            slots.append((pt, ptok, j))

        def emit_pv(kb):
            pt, ptok, j = slots[kb]
            for cc in range(max(j, 0), 4):
                bnk = cc // 2
                first = not started[bnk]
                started[bnk] = True
                last = (kb == 4 * g + 2 * bnk + 1) and (cc == 2 * bnk + 1)
                out = po[bnk][:, (cc % 2) * VW:(cc % 2) * VW + 129]
                P.add("pe", lambda e, out=out, cc=cc, first=first, last=last: e.matmul(
                    out, lhsT=pt[:, cc * 128:(cc + 1) * 128], rhs=v_sb[:, kb, 0:129], start=first, stop=last),
                    reads=[("v", kb // 4), "v1", ptok], writes=[potok[bnk]], is_mm=True)

        for i in range(nkb + 3):
            if i < nkb:
                emit_st(i)
            if i == min(8, nkb - 1) and g + 1 < NQT:
                prep_tile(g + 1)
            if i >= 3:
                emit_pv(i - 3)
        os_ = o_sb[g % 2]
        otok = ("o", g % 2)
        for bnk in range(2):
            src = bass.AP(po[bnk], 128, [[512, 128], [VW, 2]])
            P.add("dve", lambda e, src=src, bnk=bnk: e.reciprocal(out=rl_sb[:, 2 * bnk:2 * bnk + 2], in_=src),
                  reads=[potok[bnk]], writes=[("rl", bnk)])
            for h2 in range(2):
                cc = 2 * bnk + h2
                P.add("dve", lambda e, bnk=bnk, h2=h2, cc=cc: e.tensor_scalar(
                    out=os_[:, cc, :], in0=po[bnk][:, h2 * VW:h2 * VW + 128], scalar1=rl_sb[:, cc:cc + 1],
                    scalar2=None, op0=ALU.mult), reads=[potok[bnk], ("rl", bnk)], writes=[otok])
        pb, ptok = next_bank()
        for cc in range(4):
            P.add("pe", lambda e, cc=cc: e.matmul(pb[:, cc * 128:(cc + 1) * 128], lhsT=os_[:, cc, :], rhs=ident_b,
                                                   start=(cc == 0), stop=(cc == 3)),
                  reads=[otok, "cbf"], writes=[ptok], is_mm=True)
        ots = oT_sb[g % 2]
        P.add("act", lambda e: e.activation(out=ots[:, :], in_=pb[:, :], func=AF.Copy), reads=[ptok],
              writes=[("oT", g % 2)])
        P.dma("pool", o_my.ap()[:, g * T:(g + 1) * T], ots[:, :], reads=[("oT", g % 2)], writes=[("d", "o_my")],
              key=("oT", g % 2))

    prep_tile(0)
    for g in range(NQT):
        tile(g)
    P.barrier()
    c.allgather(o_my, o_all, ("d", "o_my"), ("d", "o_all"))


def f_phase_B(c, l, final, x_ap, xtok_d, x_dst_ap, xh_in, xh_all, xh_my, o_all, wall, vecs_in, cw_in, ones_in,
              oh_in, ohp_in):
    P = c.P
    c.phase()
    vec = c.sb("vec", [128, 5, 16], F32)
    cw = c.sb("cw_sb", [128, 3, 8], F32)
    ones_bf = c.sb("ones", [128, 128], BF16)
    oh = c.sb("oh", [128, 8], F32)
    ohp = c.sb("ohp", [128, 8], F32)
    xt = c.sb("xt", [128, 16, T], F32)
    sq = c.sb("sq", [128, 16, T], BF16)
    h = c.sb("h", [128, 16, T], BF16)
    rt = c.sb("rt", [128, T], F32)
    rstd = c.sb("rstd", [128, T], F32)
    xh_sb = c.sb("xh_sb", [128, 16, 2], F32)
    xh8 = c.sb("xh8", [128, 8, 32], F32)
    hh = c.sb("hh", [128, 16, 2], BF16)
    convy = c.sb("convy", [128, 8, T], BF16)
    at_sb = c.sb("at_sb", [128, 8, T], BF16)
    merged = c.sb("merged", [128, 16, T], BF16)
    u = c.sb("u", [128, 32, T], BF16)
    wt = [c.sb("wt", [128, 16, 128], BF16) for i in range(NW)]
    w2t = [c.sb("w2t", [128, 32, 128], BF16) for i in range(2)]
    tcc = c.sb("tcc", [128, T], F32)
    tch = c.sb("tch", [128, 2], F32)
    usb = c.sb("usb", [128, T + 2], F32)
    uh = c.sb("uh", [128, 8, 2], F32)
    t1 = c.sb("t1", [128, T], F32)
    sg = c.sb("sg", [128, T], F32)
    sa = c.sb("sa", [128, T], F32)
    m1 = c.sb("m1", [128, T], F32)
    m2 = c.sb("m2", [128, T], F32)
    rr = [c.sb("rr", [128, T], BF16) for i in range(2)]
    ps_stat, psH = c.banks[0], c.banks[1]
    psg = c.banks[2:8]
    st = {"nw": 0, "nw2": 0, "np": 0, "nr": 0}
    gmix, gmlp, gfin, bgc, bga = (vec[:, i, :] for i in range(5))
    XT = [("x", k) for k in range(16)]
    HT = [("h", k) for k in range(16)]
    MG = [("mg", k) for k in range(16)]

    P.dma("sp", vec[:, :, :], vecs_in[:, 5 * l:5 * l + 5, :], writes=["consts"], key="c0")
    P.dma("sp", cw[:, :, :], cw_in[l], writes=["consts"], key="c0")
    P.dma("sp", ones_bf[:, :], ones_in[:, :], writes=["consts"], key="c0")
    P.dma("sp", oh[:, :], oh_in[:, :], writes=["consts"], key="c0")
    P.dma("sp", ohp[:, :], ohp_in[:, :], writes=["consts"], key="c0")
    if l == 0:
        P.dma("sp", xh_sb[:, :, :], xh_in[:, :, :], writes=["xh"], key="xh")
    else:
        P.dma("sp", xh8[:, :, :], xh_all.ap().rearrange("(j p) x -> p j x", p=128), reads=[("d", "xh_all")],
              writes=["xh8"], key="xh")
        xhf = xh_sb[:, :, :].rearrange("p k i -> p (k i)")
        for j in range(8):
            if j == 0:
                P.add("dve", lambda e: e.tensor_scalar(out=xhf, in0=xh8[:, 0, :], scalar1=ohp[:, 0:1], scalar2=None,
                                                       op0=ALU.mult), reads=["xh8", "consts"], writes=["xh"])
            else:
                P.add("dve", lambda e, j=j: e.scalar_tensor_tensor(out=xhf, in0=xh8[:, j, :], scalar=ohp[:, j:j + 1],
                                                                    in1=xhf, op0=ALU.mult, op1=ALU.add),
                      reads=["xh8", "xh", "consts"], writes=["xh"])

    def load_w(g, m):
        sl = st["nw"] % NW
        st["nw"] += 1
        tok = ("w", sl)
        src, dtok = wsl(wall, l, g, m)
        P.dma("sp", wt[sl][:, :, :].rearrange("p k c -> p (k c)"), src, reads=[dtok], writes=[tok], key=tok)
        return wt[sl], tok

    def bank():
        i = st["np"] % 6
        st["np"] += 1
        return psg[i], ("psg", i)

    def gemm(w, wtok, ks, rhs_of, rtoks, ncols=T, dest=None, koff=0):
        if dest is None:
            b, btok = bank()
            out = b[:, 0:ncols]
        else:
            out, btok = dest
        n = len(ks)
        for i, k in enumerate(ks):
            P.add("pe", lambda e, out=out, k=k, i=i: e.matmul(out, lhsT=w[:, koff + k, :], rhs=rhs_of(k),
                                                              start=(i == 0), stop=(i == n - 1)),
                  reads=[wtok] + rtoks, writes=[btok], is_mm=True)
        return out, btok

    xv = x_ap.rearrange("(k p) t -> p k t", p=128)
    xov = x_dst_ap.rearrange("(k p) t -> p k t", p=128)

    def tile(t):
        cs = slice(t * T, (t + 1) * T)
        P.dma("sp", xt[:, :, :], xv[:, :, cs], reads=[xtok_d], writes=XT, key="xt")
        for hd in range(8):
            si = hd % 2
            cand = merged[:, 8 * si:8 * si + 8, :]
            ctoks = MG[8 * si:8 * si + 8]
            P.dma("pool", cand, o_all.ap()[hd * 128:(hd + 1) * 128, :].rearrange("p (r x) -> p r x", r=8)[:, :, cs],
                  reads=[("d", "o_all")], writes=ctoks, key=("cand", si))
            for r in range(8):
                if r == 0:
                    P.add("dve", lambda e, hd=hd, cand=cand: e.tensor_scalar(
                        out=at_sb[:, hd, :], in0=cand[:, 0, :], scalar1=oh[:, 0:1], scalar2=None, op0=ALU.mult),
                        reads=ctoks + ["consts"], writes=[("at", hd)])
                else:
                    P.add("dve", lambda e, hd=hd, cand=cand, r=r: e.scalar_tensor_tensor(
                        out=at_sb[:, hd, :], in0=cand[:, r, :], scalar=oh[:, r:r + 1], in1=at_sb[:, hd, :],
                        op0=ALU.mult, op1=ALU.add), reads=ctoks + ["consts", ("at", hd)], writes=[("at", hd)])
        AT = [("at", k) for k in range(8)]
        if t == 0:
            f_norm(c, xh_sb, hh, gmix, ones_bf, sq, ps_stat, rt, rstd, ncol=2, xtoks=["xh"] * 16, htoks=["hh"] * 16)
        f_norm(c, xt, h, gmix, ones_bf, sq, ps_stat, rt, rstd, xtoks=XT, htoks=HT)
        for j in range(8):
            wcc, tcc_w = load_w(G_CBCC, 8 + j)
            wcv, tcv_w = load_w(G_CVQ, j)
            wcb, tcb_w = load_w(G_CBCC, j)
            pa, patok = gemm(wcc, tcc_w, range(16), lambda k: h[:, k, :], HT)
            P.add("act", lambda e, pa=pa: e.activation(out=tcc[:, :], in_=pa, func=AF.Copy),
                  reads=[patok], writes=["tcc"])
            if t == 0:
                gemm(wcc, tcc_w, range(16), lambda k: hh[:, k, :], ["hh"], dest=(psH[:, 0:2], "psH"))
                P.add("act", lambda e: e.activation(out=tch[:, :], in_=psH[:, 0:2], func=AF.Copy),
                      reads=["psH"], writes=["tch"])
            pv, pvtok = gemm(wcv, tcv_w, range(16), lambda k: h[:, k, :], HT)
            if t == 0:
                gemm(wcv, tcv_w, range(16), lambda k: hh[:, k, :], ["hh"], dest=(psH[:, 2:4], "psH2"))
                P.add("dve", lambda e: e.tensor_tensor(out=usb[:, 0:2], in0=psH[:, 2:4], in1=tch[:, :], op=ALU.mult),
                      reads=["psH2", "tch"], writes=["usb"])
            else:
                P.add("pool", lambda e, j=j: e.tensor_copy(out=usb[:, 0:2], in_=uh[:, j, :]),
                      reads=[("uh", j)], writes=["usb"])
            P.add("dve", lambda e, pv=pv: e.tensor_tensor(out=usb[:, 2:T + 2], in0=pv, in1=tcc[:, :], op=ALU.mult),
                  reads=[pvtok, "tcc", "usb"], writes=["usb"])
            P.add("pool", lambda e, j=j: e.tensor_copy(out=uh[:, j, :], in_=usb[:, T:T + 2]),
                  reads=["usb"], writes=[("uh", j)])
            P.add("dve", lambda e, j=j: e.tensor_scalar(out=t1[:, :], in0=usb[:, 2:T + 2], scalar1=cw[:, 2, j:j + 1],
                                                         scalar2=None, op0=ALU.mult),
                  reads=["usb", "consts"], writes=["t1"])
            for tap, off in ((1, 1), (0, 0)):
                P.add("dve", lambda e, j=j, tap=tap, off=off: e.scalar_tensor_tensor(
                    out=t1[:, :], in0=usb[:, off:off + T], scalar=cw[:, tap, j:j + 1], in1=t1[:, :],
                    op0=ALU.mult, op1=ALU.add), reads=["usb", "t1", "consts"], writes=["t1"])
            pc, pctok = gemm(wcb, tcb_w, range(16), lambda k: h[:, k, :], HT)
            P.add("dve", lambda e, pc=pc, j=j: e.tensor_tensor(out=convy[:, j, :], in0=pc, in1=t1[:, :], op=ALU.mult),
                  reads=[pctok, "t1"], writes=[("cy", j)])
        CY = [("cy", k) for k in range(8)]
        for j in range(16):
            wgc, tgc = load_w(G_GC, j)
            wca, tca = load_w(G_CA, j)
            wga, tga = load_w(G_GA, j)
            p1, p1t = gemm(wgc, tgc, range(16), lambda k: h[:, k, :], HT)
            P.add("act", lambda e, p1=p1, j=j: e.activation(out=sg[:, :], in_=p1, func=AF.Sigmoid,
                                                             bias=bgc[:, j:j + 1], scale=1.0),
                  reads=[p1t, "consts"], writes=["sg"])
            p2, p2t = gemm(wca, tca, range(8), lambda k: convy[:, k, :], CY)
            P.add("dve", lambda e, p2=p2: e.tensor_tensor(out=m1[:, :], in0=p2, in1=sg[:, :], op=ALU.mult),
                  reads=[p2t, "sg"], writes=["m1"])
            p3, p3t = gemm(wga, tga, range(16), lambda k: h[:, k, :], HT)
            P.add("act", lambda e, p3=p3, j=j: e.activation(out=sa[:, :], in_=p3, func=AF.Sigmoid,
                                                             bias=bga[:, j:j + 1], scale=1.0),
                  reads=[p3t, "consts"], writes=["sa"])
            p4, p4t = gemm(wca, tca, range(8), lambda k: at_sb[:, k, :], AT, koff=8)
            P.add("dve", lambda e, p4=p4: e.tensor_tensor(out=m2[:, :], in0=p4, in1=sa[:, :], op=ALU.mult),
                  reads=[p4t, "sa"], writes=["m2"])
            P.add("pool", lambda e, j=j: e.tensor_tensor(out=merged[:, j, :], in0=m1[:, :], in1=m2[:, :], op=ALU.add),
                  reads=["m1", "m2"], writes=[("mg", j)])
        for j in range(16):
            wm, tm = load_w(G_MIX, j)
            p5, p5t = gemm(wm, tm, range(16), lambda k: merged[:, k, :], MG)
            P.add("dve", lambda e, p5=p5, j=j: e.tensor_tensor(out=xt[:, j, :], in0=p5, in1=xt[:, j, :], op=ALU.add),
                  reads=[p5t, ("x", j)], writes=[("x", j)])
        f_norm(c, xt, h, gmlp, ones_bf, sq, ps_stat, rt, rstd, xtoks=XT, htoks=HT)
        for half in range(2):
            for f in range(32):
                fg = half * 32 + f
                w1, t1w = load_w(G_FF1 + fg // 16, fg % 16)
                p6, p6t = gemm(w1, t1w, range(16), lambda k: h[:, k, :], HT)
                ri = st["nr"] % 2
                st["nr"] += 1
                rrs = rr[ri]
                P.add("act", lambda e, p6=p6, rrs=rrs: e.activation(out=rrs[:, :], in_=p6, func=AF.Relu),
                      reads=[p6t], writes=[("rr", ri)])
                P.add("dve", lambda e, p6=p6, rrs=rrs, f=f: e.tensor_tensor(out=u[:, f, :], in0=p6, in1=rrs[:, :],
                                                                            op=ALU.mult),
                      reads=[p6t, ("rr", ri)], writes=[("u", f)])
            UT = [("u", f) for f in range(32)]
            for j in range(16):
                s2 = st["nw2"] % 2
                st["nw2"] += 1
                w2 = w2t[s2]
                toks = []
                for q in range(2):
                    tok = ("w2", s2, q)
                    src, dtok = wsl(wall, l, G_FF2 + 2 * half + q, j)
                    P.dma("sp", w2[:, 16 * q:16 * (q + 1), :].rearrange("p k c -> p (k c)"), src, reads=[dtok],
                          writes=[tok], key=tok)
                    toks.append(tok)
                b, btok = bank()
                for f in range(32):
                    P.add("pe", lambda e, b=b, w2=w2, f=f: e.matmul(b[:, :], lhsT=w2[:, f, :], rhs=u[:, f, :],
                                                                     start=(f == 0), stop=(f == 31)),
                          reads=toks + UT, writes=[btok], is_mm=True)
                P.add("dve", lambda e, b=b, j=j: e.tensor_tensor(out=xt[:, j, :], in0=b[:, :], in1=xt[:, j, :], op=ALU.add),
                      reads=[btok, ("x", j)], writes=[("x", j)])
        if not final and t == NT - 1:
            P.dma("pool", xh_my.ap().rearrange("p (k i) -> p k i", i=2), xt[:, :, T - 2:T], reads=XT,
                  writes=[("d", "xh_my")], key="xhst")
        if final:
            f_norm(c, xt, h, gfin, ones_bf, sq, ps_stat, rt, rstd, xtoks=XT, htoks=HT, inplace=True)
        P.dma("pool", xov[:, :, cs], xt[:, :, :], reads=XT, writes=[("d", "xdst")], key="xo")

    for t in range(NT):
        tile(t)
    if not final:
        P.barrier()
        c.allgather(xh_my, xh_all, ("d", "xh_my"), ("d", "xh_all"))


def build_fused(depth=2, emulate=False):
    c = FCtx()
    c.emulate = emulate
    wsrc = c.dram_in("wsrc", [GPC, 2048, 2048], F32)
    xT = c.dram_in("xT", [D, TL], F32)
    xh0 = c.dram_in("xh0", [128, 16, 2], F32)
    wf_in = c.dram_in("wf", [depth, 128, 16, 8], F32)
    bfb_in = c.dram_in("bfb", [depth, 128, 32], F32)
    vecs_in = c.dram_in("vecs", [128, 5 * depth, 16], F32)
    cw_in = c.dram_in("cw", [depth, 128, 3, 8], F32)
    ones_in = c.dram_in("ones_bf", [128, 128], BF16)
    cf32_in = c.dram_in("cf32", [4, 128, 128], F32)
    cbf_in = c.dram_in("cbf", [2, 128, 128], BF16)
    oh_in = c.dram_in("oh", [128, 8], F32)
    ohp_in = c.dram_in("ohp", [128, 8], F32)
    ohd_in = c.dram_in("ohd", [8, 128, 128], BF16)
    xo = c.dram_out("xo", [D, TL], F32)
    wmy = [[c.dram("wmy%d_%d" % (g, hf), [8, 128, 2048], BF16) for hf in range(2)] for g in range(GPC)]
    if emulate:
        wall = [[c.nc.dram_tensor("wall%d_%d" % (g, hf), [64, 128, 2048], BF16, kind="ExternalInput")
                 for hf in range(2)] for g in range(GPC)]
    else:
        wall = [[c.dram("wall%d_%d" % (g, hf), [64, 128, 2048], BF16) for hf in range(2)] for g in range(GPC)]
    qkv_my = {nm: c.dram(nm + "_my", [8, 128, TL], BF16) for nm in ("q", "k", "v")}
    qkv_all = {nm: c.dram(nm + "_all", [64, 128, TL], BF16) for nm in ("q", "k", "v")}
    lp_my = c.dram("lp_my", [TL, 8], F32)
    lp_all = c.dram("lp_all", [S, 8], F32)
    o_my = c.dram("o_my", [128, S], BF16)
    o_all = c.dram("o_all", [1024, S], BF16)
    x_cur = c.dram("x_cur", [D, TL], F32)
    xh_my = c.dram("xh_my", [128, 32], F32)
    xh_all = c.dram("xh_all", [1024, 32], F32)

    if not emulate:
        f_phase_W(c, wsrc, wmy, wall)
    for l in range(depth):
        final = (l == depth - 1)
        x_ap = xT if l == 0 else x_cur.ap()
        xtok = ("d", "xin") if l == 0 else ("d", "xdst")
        f_phase_A(c, l, x_ap, xtok, wall, wf_in, vecs_in, bfb_in, ones_in, qkv_my["q"], qkv_my["k"], qkv_my["v"],
                  lp_my, qkv_all["q"], qkv_all["k"], qkv_all["v"], lp_all)
        f_phase_ATT(c, qkv_all["q"], qkv_all["k"], qkv_all["v"], lp_all, cf32_in, cbf_in, oh_in, ohd_in, o_my, o_all)
        if emulate and l == 0:
            c.P.barrier()
            for nm, t_ in (("q", qkv_my["q"]), ("k", qkv_my["k"]), ("v", qkv_my["v"]), ("lp", lp_my), ("o", o_my)):
                dbg = c.nc.dram_tensor("dbg_" + nm, list(t_.ap().shape), t_.ap().dtype, kind="ExternalOutput")
                c.P.dma("sp", dbg.ap(), t_.ap(), key=("dbg", nm))
        f_phase_B(c, l, final, x_ap, xtok, xo if final else x_cur.ap(), xh0, xh_all, xh_my, o_all, wall, vecs_in,
                  cw_in, ones_in, oh_in, ohp_in)
    return c.finish()


def kernel_fused(x, g_mix, w_in, b_f, b_gate, conv_w, w_conv_out, w_attn_out, w_mix_out, g_mlp, w_ff1, w_ff2, g_final):
    f32 = np.float32
    bf = ml_dtypes.bfloat16
    x = np.asarray(x, f32)
    depth = w_in.shape[0]
    groups = []
    for l in range(depth):
        groups += [np.asarray(g, f32) for g in weight_groups(w_in, w_conv_out, w_attn_out, w_mix_out, w_ff1, w_ff2, l)]
    ng = len(groups)
    wf = np.ascontiguousarray(np.stack([np.asarray(w_in[l][:, 6144:6152], f32).reshape(16, 128, 8).transpose(1, 0, 2)
                                        for l in range(depth)]))
    bfb = np.ascontiguousarray(np.stack([np.tile(np.asarray(b_f[l], f32), (128, 4)) for l in range(depth)]))
    vl = []
    for l in range(depth):
        vl += [colvec(np.asarray(g_mix[l], f32)), colvec(np.asarray(g_mlp[l], f32)), colvec(np.asarray(g_final, f32)),
               colvec(np.asarray(b_gate[l][:D], f32)), colvec(np.asarray(b_gate[l][D:], f32))]
    vecs = np.ascontiguousarray(np.stack(vl, axis=1))
    cw = np.ascontiguousarray(np.stack([np.asarray(conv_w[l], f32).reshape(3, 8, 128).transpose(2, 0, 1)
                                        for l in range(depth)]))
    cf32 = att_consts()
    cbf = np.stack([np.ones((128, 128), f32), np.eye(128, dtype=f32)]).astype(bf)
    ones_bf = np.ones((128, 128), bf)
    ims = []
    for r in range(NCORES):
        ids = [(r * GPC + j) % ng for j in range(GPC)]
        oh = np.zeros((128, 8), f32)
        oh[:, r] = 1
        ohp = np.zeros((128, 8), f32)
        if r > 0:
            ohp[:, r - 1] = 1
        ohd = np.zeros((8, 128, 128), f32)
        ohd[r] = np.eye(128, dtype=f32)
        halo = np.zeros((D, 2), f32) if r == 0 else x[0, r * TL - 2:r * TL, :].T
        ims.append({"wsrc": np.ascontiguousarray(np.stack([groups[i] for i in ids])),
                    "xT": np.ascontiguousarray(x[0, r * TL:(r + 1) * TL, :].T),
                    "xh0": np.ascontiguousarray(halo.reshape(16, 128, 2).transpose(1, 0, 2)),
                    "wf": wf, "bfb": bfb, "vecs": vecs, "cw": cw, "ones_bf": ones_bf, "cf32": cf32, "cbf": cbf,
                    "oh": oh, "ohp": ohp, "ohd": ohd.astype(bf)})
    if "F" not in _CACHE:
        _CACHE["F"] = build_fused(depth)
    res = run(_CACHE["F"], ims)
    out = np.concatenate([res[r]["xo"].T for r in range(NCORES)], axis=0)[None]
    return np.ascontiguousarray(out.astype(f32))


_CACHE = {}


def get_prog(name):
    if name not in _CACHE:
        _CACHE[name] = {"W": build_W, "A": build_A, "ATT": build_ATT, "B0": lambda: build_B(False), "B1": lambda: build_B(True)}[name]()
    return _CACHE[name]


def run(nc, in_maps):
    res = run_bass_kernel_spmd(nc, in_maps, core_ids=list(range(NCORES)))
    return res.results


def weight_groups(w_in, w_conv_out, w_attn_out, w_mix_out, w_ff1, w_ff2, l):
    gs = [w_in[l][:, 0:2048], w_in[l][:, 2048:4096], w_in[l][:, 4096:6144],
          w_in[l][:, 6152:8200], w_in[l][:, 8200:10248], w_mix_out[l],
          np.concatenate([w_conv_out[l], w_attn_out[l]], axis=0)]
    gs += [w_ff1[l][:, 2048 * i:2048 * (i + 1)] for i in range(4)]
    gs += [w_ff2[l][2048 * i:2048 * (i + 1), :] for i in range(4)]
    return gs


def convert_weights(groups):
    ng = len(groups)
    in_maps = []
    for cidx in range(NCORES):
        ids = [(cidx * GPC + j) % ng for j in range(GPC)]
        in_maps.append({"wsrc": np.ascontiguousarray(np.stack([groups[i] for i in ids]))})
    res = run(get_prog("W"), in_maps)
    out = [None] * ng
    for cidx in range(NCORES):
        for j in range(GPC):
            gi = cidx * GPC + j
            if gi < ng:
                out[gi] = res[cidx]["wb"][j]
    return out


def colvec(v):
    return np.ascontiguousarray(v.reshape(16, 128).T)


def att_consts():
    tri = np.triu(np.ones((128, 128), np.float32))
    tris = np.triu(np.ones((128, 128), np.float32), 1)
    cf32 = np.stack([tri, tris, np.ones((128, 128), np.float32), np.eye(128, dtype=np.float32)])
    return cf32


def kernel_unfused(x, g_mix, w_in, b_f, b_gate, conv_w, w_conv_out, w_attn_out, w_mix_out, g_mlp, w_ff1, w_ff2, g_final):
    f32 = np.float32
    x = np.asarray(x, f32)
    depth = w_in.shape[0]
    groups = []
    for l in range(depth):
        groups += weight_groups(w_in, w_conv_out, w_attn_out, w_mix_out, w_ff1, w_ff2, l)
    conv = convert_weights([np.asarray(g, f32) for g in groups])
    ones_bf = np.ones((128, 128), ml_dtypes.bfloat16)
    cf32 = att_consts()
    xT = [np.ascontiguousarray(x[0, r * TL:(r + 1) * TL, :].T) for r in range(NCORES)]
    for l in range(depth):
        wbl = np.ascontiguousarray(np.stack(conv[l * NGRP:(l + 1) * NGRP]))
        wf = np.ascontiguousarray(np.asarray(w_in[l][:, 6144:6152], f32).reshape(16, 128, 8).transpose(1, 0, 2))
        bfb = np.ascontiguousarray(np.tile(np.asarray(b_f[l], f32), (128, 4)))
        gm = colvec(np.asarray(g_mix[l], f32))
        ims = [{"xT": xT[r], "gmix": gm, "wq": np.ascontiguousarray(wbl[G_CVQ][8:16]),
                "wk": np.ascontiguousarray(wbl[G_KV][0:8]), "wv": np.ascontiguousarray(wbl[G_KV][8:16]),
                "wf": wf, "bfb": bfb, "ones_bf": ones_bf} for r in range(NCORES)]
        ra = run(get_prog("A"), ims)
        v_all = np.concatenate([ra[r]["v"] for r in range(NCORES)], axis=0)
        lp_all = np.concatenate([ra[r]["lp"] for r in range(NCORES)], axis=0)
        ims = []
        for hh in range(NH):
            qT = np.concatenate([ra[r]["qT"][hh] for r in range(NCORES)], axis=1)
            kT = np.concatenate([ra[r]["kT"][hh] for r in range(NCORES)], axis=1)
            vB = v_all[:, hh * 128:(hh + 1) * 128].reshape(NKB, 128, 128).transpose(1, 0, 2)
            lp = lp_all[:, hh].reshape(NKB, 128).T
            ims.append({"qT": np.ascontiguousarray(qT), "kT": np.ascontiguousarray(kT),
                        "vB": np.ascontiguousarray(vB), "lp": np.ascontiguousarray(lp),
                        "cf32": cf32})
        rt_ = run(get_prog("ATT"), ims)
        vecs = np.ascontiguousarray(np.stack([
            colvec(np.asarray(g_mix[l], f32)), colvec(np.asarray(g_mlp[l], f32)), colvec(np.asarray(g_final, f32)),
            colvec(np.asarray(b_gate[l][:D], f32)), colvec(np.asarray(b_gate[l][D:], f32))], axis=1))
        cw = np.ascontiguousarray(np.asarray(conv_w[l], f32).reshape(3, 8, 128).transpose(2, 0, 1))
        ims = []
        for r in range(NCORES):
            attnT = np.concatenate([rt_[hh]["oTok"][r * TL:(r + 1) * TL, :].T for hh in range(NH)], axis=0)
            if r == 0:
                halo = np.zeros((D, 2), f32)
            else:
                halo = xT[r - 1][:, TL - 2:TL]
            xh = np.ascontiguousarray(halo.reshape(16, 128, 2).transpose(1, 0, 2))
            ims.append({"xT": xT[r], "xh": xh, "attnT": np.ascontiguousarray(attnT), "wb": wbl,
                        "vecs": vecs, "cw": cw, "ones_bf": ones_bf})
        rb = run(get_prog("B1" if l == depth - 1 else "B0"), ims)
        xT = [rb[r]["xo"] for r in range(NCORES)]
    out = np.concatenate([xT[r].T for r in range(NCORES)], axis=0)[None]
    return np.ascontiguousarray(out.astype(f32))


FUSED = False


def kernel(**inputs):
    return kernel_fused(**inputs) if FUSED else kernel_unfused(**inputs)
```

```python
import contextlib
import numpy as np
import ml_dtypes
import concourse.bass as bass
import concourse.mybir as mybir
from concourse.bass_utils import run_bass_kernel_spmd

F32 = mybir.dt.float32
BF16 = mybir.dt.bfloat16
AF = mybir.ActivationFunctionType
ALU = mybir.AluOpType

NCORES = 8
D = 2048
S = 16384
TL = S // NCORES
T = 512
NT = TL // T
NH = 8
HD = 128
DFF = 8192
EPS = 1e-6
SCALE = float(HD) ** -0.5
NGRP = 15
G_CBCC, G_CVQ, G_KV, G_GC, G_GA, G_MIX, G_CA = 0, 1, 2, 3, 4, 5, 6
G_FF1 = 7
G_FF2 = 11


class Op:
    __slots__ = ("eng", "fn", "deps", "dma_key", "ms", "is_mm", "waits", "cc", "idx")

    def __init__(self, eng, fn, dma_key=None, is_mm=False):
        self.eng, self.fn, self.dma_key, self.is_mm = eng, fn, dma_key, is_mm
        self.cc = False
        self.deps = []
        self.ms = None
        self.waits = []


class Prog:
    ENGS = ("pe", "act", "dve", "pool", "sp")

    def __init__(self):
        self.ops = {e: [] for e in self.ENGS}
        self.last_w = {}
        self.readers = {}
        self.all_ops = []
        self.dma_keys = []
        self.last_dma = {}
        self.bar_deps = []
        self.bar_pending = set()

    def add(self, eng, fn, reads=(), writes=(), dma_key=None, is_mm=False):
        op = Op(eng, fn, dma_key, is_mm)
        deps = []
        for t in reads:
            w = self.last_w.get(t)
            if w is not None:
                deps.append(w)
        for t in writes:
            w = self.last_w.get(t)
            if w is not None:
                deps.append(w)
            deps.extend(self.readers.get(t, ()))
        seen = set()
        for d in deps:
            if d is op or id(d) in seen:
                continue
            if is_mm and d.is_mm and d.dma_key is None:
                continue
            seen.add(id(d))
            op.deps.append(d)
        if eng in self.bar_pending:
            self.bar_pending.discard(eng)
            for d in self.bar_deps:
                if d is not op and id(d) not in seen and not (d.eng == eng and d.dma_key is None):
                    seen.add(id(d))
                    op.deps.append(d)
        for t in reads:
            self.readers.setdefault(t, []).append(op)
        for t in writes:
            self.last_w[t] = op
            self.readers[t] = []
        if dma_key is not None:
            self.last_dma[dma_key] = op
        best = {}
        kept = []
        for d in op.deps:
            if d.dma_key is not None:
                kept.append(d)
            elif d.eng not in best or d.idx > best[d.eng].idx:
                best[d.eng] = d
        op.deps = kept + list(best.values())
        op.idx = len(self.ops[eng])
        self.ops[eng].append(op)
        self.all_ops.append(op)
        if dma_key is not None and dma_key not in self.dma_keys:
            self.dma_keys.append(dma_key)
        return op

    def dma(self, queue, out, in_, reads=(), writes=(), key=None):
        assert key is not None
        return self.add(queue, lambda e: e.dma_start(out=out, in_=in_), reads, writes, dma_key=key)

    def barrier(self):
        deps = [self.ops[e][-1] for e in self.ENGS if self.ops[e] and self.ops[e][-1].dma_key is None]
        deps += list(self.last_dma.values())
        self.bar_deps = deps
        self.bar_pending = set(self.ENGS)

    def collective(self, in_ap, out_ap, reads=(), writes=(), key=None):
        op = self.add("pool", lambda e: e.collective_compute(
            "AllGather", ALU.bypass, replica_groups=[list(range(NCORES))], ins=[in_ap], outs=[out_ap]),
            reads, writes, dma_key=key)
        op.cc = True
        return op

    def finalize(self):
        needed = set()
        for op in self.all_ops:
            for d in op.deps:
                needed.add(id(d))
        cnt = {e: 0 for e in self.ENGS}
        dcnt = {}
        final_dma = {}
        for op in self.all_ops:
            if op.dma_key is not None:
                dcnt[op.dma_key] = dcnt.get(op.dma_key, 0) + (1 if op.cc else 16)
                op.ms = ("dma:%s" % (op.dma_key,), dcnt[op.dma_key])
                final_dma[op.dma_key] = dcnt[op.dma_key]
            elif id(op) in needed:
                cnt[op.eng] += 1
                op.ms = ("eng:" + op.eng, cnt[op.eng])
        waited = {e: {} for e in self.ENGS}
        for e in self.ENGS:
            for op in self.ops[e]:
                w = {}
                for d in op.deps:
                    s, v = d.ms
                    if waited[e].get(s, 0) >= v:
                        continue
                    w[s] = max(w.get(s, 0), v)
                for s, v in w.items():
                    waited[e][s] = v
                op.waits = list(w.items())
        self.final_dma = final_dma
        self.max_counts = dict(cnt)

    def sem_names(self):
        return ["eng:" + e for e in self.ENGS] + ["dma:%s" % (k,) for k in self.dma_keys]

    def emit(self, nc, sems, final_wait_eng="sp"):
        engmap = {"pe": "tensor", "act": "scalar", "dve": "vector", "pool": "gpsimd", "sp": "sync"}
        with nc.Block() as block:
            for e in self.ENGS:
                ops = self.ops[e]
                fw = self.final_dma if e == final_wait_eng else {}

                def body(engine, ops=ops, fw=fw):
                    for op in ops:
                        for s, v in op.waits:
                            engine.wait_ge(sems[s], v)
                        ins = op.fn(engine)
                        if op.cc:
                            ins.then_inc(sems[op.ms[0]])
                        elif op.ms is not None:
                            ins.then_inc(sems[op.ms[0]], 16 if op.dma_key is not None else 1)
                    for k, v in fw.items():
                        engine.wait_ge(sems["dma:%s" % (k,)], v)

                getattr(block, engmap[e])(body)


class Ctx:
    def __init__(self):
        self.nc = bass.Bass("TRN2", target_bir_lowering=False)
        self.P = Prog()
        self.stack = contextlib.ExitStack()
        self.nsb = 0

    def dram_in(self, name, shape, dt):
        return self.nc.dram_tensor(name, list(shape), dt, kind="ExternalInput").ap()

    def dram_out(self, name, shape, dt):
        return self.nc.dram_tensor(name, list(shape), dt, kind="ExternalOutput").ap()

    def sb(self, name, shape, dt):
        return self.stack.enter_context(self.nc.sbuf_tensor(name, list(shape), dt))

    def ps(self, name):
        return self.stack.enter_context(self.nc.psum_tensor(name, [128, 512], F32))

    def finish(self):
        P = self.P
        P.finalize()
        sems = {}
        for i, n in enumerate(P.sem_names()):
            sems[n] = self.stack.enter_context(self.nc.semaphore("s%d" % i))
        P.emit(self.nc, sems)
        self.stack.close()
        return self.nc


GPC = 4


def build_W():
    c = Ctx()
    P = c.P
    wsrc = c.dram_in("wsrc", [GPC, 2048, 2048], F32)
    wb = c.dram_out("wb", [GPC, 16, 128, 2048], BF16)
    stage = [c.sb("stage%d" % i, [128, 2048], F32) for i in range(3)]
    wt = [c.sb("wt%d" % i, [128, 16, 16, 128], BF16) for i in range(2)]
    n = 0
    for g in range(GPC):
        w = wt[g % 2]
        for k in range(16):
            st = stage[n % 3]
            tok = ("stage", n % 3)
            P.dma("sp", st[:, :], wsrc[g, k * 128:(k + 1) * 128, :], writes=[tok], key=tok)
            src = st[:, :].rearrange("p (m c) -> p m c", m=16)
            dst = w[:, :, k, :]
            eng = "dve" if n % 2 == 0 else "pool"
            P.add(eng, lambda e, dst=dst, src=src: e.tensor_copy(out=dst, in_=src),
                  reads=[tok], writes=[("wt", g % 2, k)])
            n += 1
        P.dma("pool", wb[g].rearrange("m p x -> p m x"),
              w[:, :, :, :].rearrange("p m k c -> p m (k c)"),
              reads=[("wt", g % 2, k) for k in range(16)], key=("wtst", g % 2))
    return c.finish()


def emit_norm(c, xt, xtok, h, htok, gcol, ones_bf, sq, ps_stat, rt, rstd, uid):
    P = c.P
    for k in range(16):
        P.add("act", lambda e, k=k: e.activation(out=sq[:, k, :], in_=xt[:, k, :], func=AF.Square),
              reads=[xtok], writes=[("sq", k)])
    for k in range(16):
        P.add("pe", lambda e, k=k: e.matmul(ps_stat[:, :], lhsT=ones_bf[:, :], rhs=sq[:, k, :],
                                             start=(k == 0), stop=(k == 15)),
              reads=[("sq", k), "consts"], writes=[("ps", id(ps_stat))], is_mm=True)
    P.add("act", lambda e: e.activation(out=rt[:, :], in_=ps_stat[:, :], func=AF.Sqrt,
                                        bias=EPS, scale=1.0 / D),
          reads=[("ps", id(ps_stat))], writes=["rt"])
    P.add("dve", lambda e: e.reciprocal(out=rstd[:, :], in_=rt[:, :]), reads=["rt"], writes=["rstd"])
    for k in range(16):
        P.add("dve", lambda e, k=k: e.scalar_tensor_tensor(out=h[:, k, :], in0=xt[:, k, :],
                                                            scalar=gcol[:, k:k + 1], in1=rstd[:, :],
                                                            op0=ALU.mult, op1=ALU.mult),
              reads=[xtok, "rstd", "consts"], writes=[(htok, k)])


def build_A():
    c = Ctx()
    P = c.P
    xT = c.dram_in("xT", [D, TL], F32)
    gmix = c.dram_in("gmix", [128, 16], F32)
    wq = c.dram_in("wq", [8, 128, 2048], BF16)
    wk = c.dram_in("wk", [8, 128, 2048], BF16)
    wv = c.dram_in("wv", [8, 128, 2048], BF16)
    wf = c.dram_in("wf", [128, 16, 8], F32)
    bfb = c.dram_in("bfb", [128, 32], F32)
    ones_in = c.dram_in("ones_bf", [128, 128], BF16)
    qT_o = c.dram_out("qT", [NH, 128, TL], BF16)
    kT_o = c.dram_out("kT", [NH, 128, TL], BF16)
    v_o = c.dram_out("v", [TL, 1024], BF16)
    lp_o = c.dram_out("lp", [TL, 8], F32)

    g_sb = c.sb("g_sb", [128, 16], F32)
    ones_bf = c.sb("ones", [128, 128], BF16)
    wq_sb = c.sb("wq_sb", [128, 8, 16, 128], BF16)
    wk_sb = c.sb("wk_sb", [128, 8, 16, 128], BF16)
    wv_sb = c.sb("wv_sb", [128, 8, 16, 128], BF16)
    wf32 = c.sb("wf32", [128, 16, 8], F32)
    wf_sb = c.sb("wf_sb", [128, 16, 8], BF16)
    bf_sb = c.sb("bf_sb", [128, 32], F32)
    xt = [c.sb("xt%d" % i, [128, 16, T], F32) for i in range(2)]
    sq = c.sb("sq", [128, 16, T], BF16)
    h = c.sb("h", [128, 16, T], BF16)
    rt = c.sb("rt", [128, T], F32)
    rstd = c.sb("rstd", [128, T], F32)
    ev = [c.sb("ev%d" % i, [128, T], BF16) for i in range(4)]
    lpt = c.sb("lpt", [128, 32], F32)
    lpe = c.sb("lpe", [128, 32], F32)
    lps = [c.sb("lps%d" % i, [128, 32], F32) for i in range(2)]
    ps_stat = c.ps("ps_stat")
    psb = [c.ps("psb%d" % i) for i in range(4)]
    ps_f = c.ps("ps_f")

    P.dma("sp", g_sb[:, :], gmix[:, :], writes=["consts"], key="c0")
    P.dma("sp", ones_bf[:, :], ones_in[:, :], writes=["consts"], key="c0")
    P.dma("sp", bf_sb[:, :], bfb[:, :], writes=["consts"], key="c0")
    P.dma("sp", wf32[:, :, :], wf[:, :, :], writes=["wf32"], key="c1")
    P.add("dve", lambda e: e.tensor_copy(out=wf_sb[:, :, :], in_=wf32[:, :, :]), reads=["wf32"], writes=["wf"])
    P.dma("sp", xt[0][:, :, :], xT.rearrange("(k p) t -> p k t", p=128)[:, :, 0:T],
          writes=[("xt", 0)], key=("xt", 0))
    for nm, src, dst in (("wk", wk, wk_sb), ("wv", wv, wv_sb), ("wq", wq, wq_sb)):
        P.dma("pool", dst[:, :, :, :].rearrange("p m k c -> p m (k c)"),
              src.rearrange("m p x -> p m x"), writes=[nm], key=nm)

    nev = 0
    for t in range(NT):
        xs = xt[t % 2]
        xtok = ("xt", t % 2)
        if t + 1 < NT:
            P.dma("sp", xt[(t + 1) % 2][:, :, :],
                  xT.rearrange("(k p) t -> p k t", p=128)[:, :, (t + 1) * T:(t + 2) * T],
                  writes=[("xt", (t + 1) % 2)], key=("xt", (t + 1) % 2))
        emit_norm(c, xs, xtok, h, "h", g_sb, ones_bf, sq, ps_stat, rt, rstd, t)
        hreads = [("h", k) for k in range(16)]
        for nm, wsb, dst, sc in (("wk", wk_sb, kT_o, 1.0), ("wq", wq_sb, qT_o, SCALE)):
            for m in range(NH):
                pb = psb[nev % 4]
                ptok = ("ps", id(pb))
                for k in range(16):
                    P.add("pe", lambda e, pb=pb, wsb=wsb, m=m, k=k: e.matmul(
                        pb[:, :], lhsT=wsb[:, m, k, :], rhs=h[:, k, :], start=(k == 0), stop=(k == 15)),
                        reads=[nm] + hreads, writes=[ptok], is_mm=True)
                es = ev[nev % 4]
                etok = ("ev", nev % 4)
                P.add("act", lambda e, es=es, pb=pb, sc=sc: e.activation(
                    out=es[:, :], in_=pb[:, :], func=AF.Copy, scale=sc), reads=[ptok], writes=[etok])
                P.dma("pool", dst[m, :, t * T:(t + 1) * T], es[:, :], reads=[etok], key=etok)
                nev += 1
        for tb in range(4):
            for half in range(2):
                pb = psb[nev % 4]
                ptok = ("ps", id(pb))
                for k in range(16):
                    P.add("pe", lambda e, pb=pb, tb=tb, half=half, k=k: e.matmul(
                        pb[:, :], lhsT=h[:, k, tb * 128:(tb + 1) * 128],
                        rhs=wv_sb[:, 4 * half:4 * half + 4, k, :], start=(k == 0), stop=(k == 15)),
                        reads=["wv"] + hreads, writes=[ptok], is_mm=True)
                es = ev[nev % 4]
                etok = ("ev", nev % 4)
                P.add("dve", lambda e, es=es, pb=pb: e.tensor_copy(out=es[:, :], in_=pb[:, :]),
                      reads=[ptok], writes=[etok])
                r0 = t * T + tb * 128
                P.dma("pool", v_o[r0:r0 + 128, half * 512:(half + 1) * 512], es[:, :], reads=[etok], key=etok)
                nev += 1
        ftok = ("ps", id(ps_f))
        for tb in range(4):
            for k in range(16):
                P.add("pe", lambda e, tb=tb, k=k: e.matmul(
                    ps_f[:, tb * 8:(tb + 1) * 8], lhsT=h[:, k, tb * 128:(tb + 1) * 128],
                    rhs=wf_sb[:, k, :], start=(k == 0), stop=(k == 15)),
                    reads=["wf"] + hreads, writes=[ftok], is_mm=True)
        ls = lps[t % 2]
        ltok = ("lps", t % 2)
        P.add("dve", lambda e: e.tensor_tensor(out=lpt[:, :], in0=ps_f[:, 0:32], in1=bf_sb[:, :], op=ALU.add),
              reads=[ftok, "consts"], writes=["lpt"])
        P.add("act", lambda e: e.activation(out=lpe[:, :], in_=lpt[:, :], func=AF.Exp, scale=-1.0),
              reads=["lpt"], writes=["lpe"])
        P.add("act", lambda e, ls=ls: e.activation(out=ls[:, :], in_=lpe[:, :], func=AF.Ln, bias=1.0),
              reads=["lpe"], writes=[ltok])
        P.dma("pool", lp_o[t * T:(t + 1) * T, :].rearrange("(b p) h -> p b h", p=128),
              ls[:, :].rearrange("p (b h) -> p b h", b=4), reads=[ltok], key=ltok)
    return c.finish()


NQT = S // T
NKB = S // 128
VW = 130


def build_ATT():
    c = Ctx()
    P = c.P
    qT = c.dram_in("qT", [128, S], BF16)
    kT = c.dram_in("kT", [128, S], BF16)
    vB = c.dram_in("vB", [128, NKB, 128], BF16)
    lp = c.dram_in("lp", [128, NKB], F32)
    cf32 = c.dram_in("cf32", [4, 128, 128], F32)
    oTok = c.dram_out("oTok", [S, 128], BF16)

    k_sb = c.sb("k_sb", [128, S], BF16)
    v_sb = c.sb("v_sb", [128, NKB, VW], BF16)
    lp_sb = c.sb("lp_sb", [128, NKB], F32)
    cf = c.sb("cf", [128, 4, 128], F32)
    Tsb = c.sb("Tsb", [128, 2], F32)
    Dm = c.sb("Dm", [128, 128], F32)
    Cb = c.sb("Cb", [128, NKB], F32)
    Zf = c.sb("Zf", [128, T], F32)
    aug = [c.sb("aug%d" % i, [128, T], F32) for i in range(2)]
    q_sb = [c.sb("q_sb%d" % i, [128, T], BF16) for i in range(2)]
    e_sb = [c.sb("e_sb%d" % i, [128, T], F32) for i in range(4)]
    p_sb = [c.sb("p_sb%d" % i, [128, T], BF16) for i in range(5)]
    rl_sb = c.sb("rl_sb", [128, 4], F32)
    o_sb = [c.sb("o_sb%d" % i, [128, 4, 128], BF16) for i in range(2)]
    ps_s = [c.ps("ps_s%d" % i) for i in range(4)]
    ps_o = [[c.ps("ps_o%d_%d" % (i, j)) for j in range(2)] for i in range(2)]
    tri, tris, ones_f, ident_f = cf[:, 0, :], cf[:, 1, :], cf[:, 2, :], cf[:, 3, :]

    P.dma("sp", cf[:, :, :], cf32.rearrange("n p c -> p n c"), writes=["cf"], key="cf")
    P.dma("sp", lp_sb[:, :], lp[:, :], writes=["lp"], key="lp")
    P.dma("sp", q_sb[0][:, :], qT[:, 0:T], writes=[("q", 0)], key=("q", 0))
    P.dma("sp", k_sb[:, :], kT[:, :], writes=["k"], key="k")
    for hv in range(4):
        P.dma("pool", v_sb[:, 32 * hv:32 * (hv + 1), 0:128], vB[:, 32 * hv:32 * (hv + 1), :],
              writes=[("v", hv)], key=("v", hv))
    P.add("dve", lambda e: e.memset(v_sb[:, :, 128:VW], 1.0), writes=["v1"])

    P.add("pe", lambda e: e.matmul(ps_s[1][:, 0:2], lhsT=lp_sb[:, :], rhs=ones_f[:, 0:2], start=True, stop=True),
          reads=["lp", "cf"], writes=[("ps_s", 1)], is_mm=True)
    P.add("dve", lambda e: e.tensor_copy(out=Tsb[:, :], in_=ps_s[1][:, 0:2]), reads=[("ps_s", 1)], writes=["Tsb"])
    P.add("dve", lambda e: e.tensor_scalar(out=Dm[:, :], in0=tris, scalar1=Tsb[:, 0:1], scalar2=None, op0=ALU.mult),
          reads=["Tsb", "cf"], writes=["Dm"])
    P.add("pe", lambda e: e.matmul(ps_s[0][:, 0:128], lhsT=tri, rhs=lp_sb[:, :], start=True, stop=False),
          reads=["lp", "cf"], writes=[("ps_s", 0)], is_mm=True)
    P.add("pe", lambda e: e.matmul(ps_s[0][:, 0:128], lhsT=ones_f, rhs=Dm[:, :], start=False, stop=True),
          reads=["Dm", "cf"], writes=[("ps_s", 0)], is_mm=True)
    P.add("dve", lambda e: e.tensor_copy(out=Cb[:, :], in_=ps_s[0][:, 0:128]), reads=[("ps_s", 0)], writes=["Cb"])

    cnt = {"s": 0, "p": 0}

    def prep_aug(g):
        in0 = bass.AP(cf, 3 * 128, [[4 * 128, 128], [0, 4], [1, 128]])
        in1 = bass.AP(Cb, 4 * g, [[NKB, 128], [1, 4], [0, 128]])
        outz = Zf[:, :].rearrange("p (j t) -> p j t", j=4)
        P.add("dve", lambda e: e.tensor_tensor(out=outz, in0=in0, in1=in1, op=ALU.mult),
              reads=["cf", "Cb"], writes=["Zf"])
        si = cnt["s"] % 4
        cnt["s"] += 1
        pa = ps_s[si]
        P.add("pe", lambda e: e.matmul(pa[:, :], lhsT=ones_f, rhs=Zf[:, :], start=True, stop=True),
              reads=["cf", "Zf"], writes=[("ps_s", si)], is_mm=True)
        a = aug[g % 2]
        P.add("act", lambda e: e.activation(out=a[:, :], in_=pa[:, :], func=AF.Copy, scale=-1.0),
              reads=[("ps_s", si)], writes=[("aug", g % 2)])

    def tile(g):
        qs = q_sb[g % 2]
        qtok = ("q", g % 2)
        ag = aug[g % 2]
        agtok = ("aug", g % 2)
        if g + 1 < NQT:
            P.dma("sp", q_sb[(g + 1) % 2][:, :], qT[:, (g + 1) * T:(g + 2) * T],
                  writes=[("q", (g + 1) % 2)], key=("q", (g + 1) % 2))
        nkb = 4 * g + 4
        po = ps_o[g % 2]
        potok = [("ps_o", g % 2, 0), ("ps_o", g % 2, 1)]
        slots = []
        started = [False, False]
        last_c = {}
        def emit_st(kb):
            j = kb - 4 * g
            c0 = 128 * j if j > 0 else 0
            si = cnt["s"] % 4
            cnt["s"] += 1
            pss = ps_s[si]
            stok = ("ps_s", si)
            P.add("pe", lambda e: e.matmul(pss[:, c0:T], lhsT=k_sb[:, kb * 128:(kb + 1) * 128], rhs=qs[:, c0:T],
                                           start=True, stop=True), reads=["k", qtok], writes=[stok], is_mm=True)
            es = e_sb[si]
            etok = ("e", si)
            P.add("dve", lambda e: e.tensor_tensor(out=es[:, c0:T], in0=pss[:, c0:T], in1=ag[:, c0:T], op=ALU.add),
                  reads=[stok, agtok], writes=[etok])
            pi = cnt["p"] % 5
            cnt["p"] += 1
            pt = p_sb[pi]
            ptok = ("p", pi)
            P.add("act", lambda e: e.activation(out=pt[:, c0:T], in_=es[:, c0:T], func=AF.Exp,
                                                bias=Cb[:, kb:kb + 1], scale=1.0),
                  reads=[etok, "Cb"], writes=[ptok])
            if j >= 0:
                P.add("pool", lambda e: e.affine_select(out=pt[:, c0:c0 + 128], in_=pt[:, c0:c0 + 128],
                                                        pattern=[[1, 128]], compare_op=ALU.is_ge, fill=0.0,
                                                        base=0, channel_multiplier=-1),
                      reads=[ptok], writes=[ptok])
            slots.append((pt, ptok, j))

        def emit_pv(kb):
            pt, ptok, j = slots[kb]
            for cc in range(max(j, 0), 4):
                bnk = cc // 2
                first = not started[bnk]
                started[bnk] = True
                last = (kb == 4 * g + 2 * bnk + 1) and (cc == 2 * bnk + 1)
                out = po[bnk][:, (cc % 2) * VW:(cc % 2) * VW + 129]
                P.add("pe", lambda e, out=out, cc=cc, first=first, last=last: e.matmul(
                    out, lhsT=pt[:, cc * 128:(cc + 1) * 128], rhs=v_sb[:, kb, 0:129], start=first, stop=last),
                    reads=[("v", kb // 32), "v1", ptok], writes=[potok[bnk]], is_mm=True)

        for i in range(nkb + 3):
            if i < nkb:
                emit_st(i)
            if i == min(8, nkb - 1) and g + 1 < NQT:
                prep_aug(g + 1)
            if i >= 3:
                emit_pv(i - 3)
        os_ = o_sb[g % 2]
        otok = ("o", g % 2)
        for bnk in range(2):
            src = bass.AP(po[bnk], 128, [[512, 128], [VW, 2]])
            P.add("dve", lambda e, src=src, bnk=bnk: e.reciprocal(out=rl_sb[:, 2 * bnk:2 * bnk + 2], in_=src),
                  reads=[potok[bnk]], writes=[("rl", bnk)])
            for h2 in range(2):
                cc = 2 * bnk + h2
                P.add("dve", lambda e, bnk=bnk, h2=h2, cc=cc: e.tensor_scalar(
                    out=os_[:, cc, :], in0=po[bnk][:, h2 * VW:h2 * VW + 128], scalar1=rl_sb[:, cc:cc + 1],
                    scalar2=None, op0=ALU.mult), reads=[potok[bnk], ("rl", bnk)], writes=[otok])
        P.dma("pool", oTok[g * T:(g + 1) * T, :].rearrange("(c p) d -> p c d", p=128), os_[:, :, :],
              reads=[otok], key=otok)

    prep_aug(0)
    for g in range(NQT):
        tile(g)
    return c.finish()


NW = 5


def build_B(final):
    c = Ctx()
    P = c.P
    xT = c.dram_in("xT", [D, TL], F32)
    xh = c.dram_in("xh", [128, 16, 2], F32)
    attnT = c.dram_in("attnT", [1024, TL], BF16)
    wb = c.dram_in("wb", [NGRP, 16, 128, 2048], BF16)
    vecs = c.dram_in("vecs", [128, 5, 16], F32)
    cwd = c.dram_in("cw", [128, 3, 8], F32)
    ones_in = c.dram_in("ones_bf", [128, 128], BF16)
    xo = c.dram_out("xo", [D, TL], F32)

    vec = c.sb("vec", [128, 5, 16], F32)
    cw = c.sb("cw_sb", [128, 3, 8], F32)
    ones_bf = c.sb("ones", [128, 128], BF16)
    xt = c.sb("xt", [128, 16, T], F32)
    sq = c.sb("sq", [128, 16, T], BF16)
    h = c.sb("h", [128, 16, T], BF16)
    rt = c.sb("rt", [128, T], F32)
    rstd = c.sb("rstd", [128, T], F32)
    xh_sb = c.sb("xh_sb", [128, 16, 2], F32)
    hh = c.sb("hh", [128, 16, 2], BF16)
    convy = c.sb("convy", [128, 8, T], BF16)
    at_sb = c.sb("at_sb", [128, 8, T], BF16)
    merged = c.sb("merged", [128, 16, T], BF16)
    u = c.sb("u", [128, 32, T], BF16)
    wt = [c.sb("wt%d" % i, [128, 16, 128], BF16) for i in range(NW)]
    w2t = [c.sb("w2t%d" % i, [128, 32, 128], BF16) for i in range(2)]
    tcc = c.sb("tcc", [128, T], F32)
    tch = c.sb("tch", [128, 2], F32)
    usb = c.sb("usb", [128, T + 2], F32)
    uh = c.sb("uh", [128, 8, 2], F32)
    t1 = c.sb("t1", [128, T], F32)
    sg = c.sb("sg", [128, T], F32)
    sa = c.sb("sa", [128, T], F32)
    m1 = c.sb("m1", [128, T], F32)
    m2 = c.sb("m2", [128, T], F32)
    rr = [c.sb("rr%d" % i, [128, T], BF16) for i in range(2)]
    ps_stat = c.ps("ps_stat")
    psH = c.ps("psH")
    psg = [c.ps("psg%d" % i) for i in range(6)]
    st = {"nw": 0, "nw2": 0, "np": 0, "nr": 0}
    gmix, gmlp, gfin, bgc, bga = (vec[:, i, :] for i in range(5))

    P.dma("sp", vec[:, :, :], vecs[:, :, :], writes=["consts"], key="c0")
    P.dma("sp", cw[:, :, :], cwd[:, :, :], writes=["consts"], key="c0")
    P.dma("sp", ones_bf[:, :], ones_in[:, :], writes=["consts"], key="c0")
    P.dma("sp", xh_sb[:, :, :], xh[:, :, :], writes=["xh"], key="xh")

    XT = [("x", k) for k in range(16)]
    HT = [("h", k) for k in range(16)]

    def load_w(g, m):
        sl = st["nw"] % NW
        st["nw"] += 1
        tok = ("w", sl)
        P.dma("sp", wt[sl][:, :, :].rearrange("p k c -> p (k c)"), wb[g, m], writes=[tok], key=tok)
        return wt[sl], tok

    def bank():
        b = psg[st["np"] % 6]
        st["np"] += 1
        return b, ("psg", id(b))

    def gemm(w, wtok, ks, rhs_of, rtoks, ncols=T, dest=None, koff=0):
        if dest is None:
            b, btok = bank()
            out = b[:, 0:ncols]
        else:
            out, btok = dest
        n = len(ks)
        for i, k in enumerate(ks):
            P.add("pe", lambda e, out=out, k=k, i=i: e.matmul(out, lhsT=w[:, koff + k, :], rhs=rhs_of(k),
                                                              start=(i == 0), stop=(i == n - 1)),
                  reads=[wtok] + rtoks, writes=[btok], is_mm=True)
        return out, btok

    def norm_small():
        for k in range(16):
            P.add("act", lambda e, k=k: e.activation(out=sq[:, k, 0:2], in_=xh_sb[:, k, :], func=AF.Square),
                  reads=["xh"], writes=[("sq", k)])
        for k in range(16):
            P.add("pe", lambda e, k=k: e.matmul(ps_stat[:, 0:2], lhsT=ones_bf[:, :], rhs=sq[:, k, 0:2],
                                                 start=(k == 0), stop=(k == 15)),
                  reads=[("sq", k), "consts"], writes=[("ps", id(ps_stat))], is_mm=True)
        P.add("act", lambda e: e.activation(out=rt[:, 0:2], in_=ps_stat[:, 0:2], func=AF.Sqrt, bias=EPS, scale=1.0 / D),
              reads=[("ps", id(ps_stat))], writes=["rt"])
        P.add("dve", lambda e: e.reciprocal(out=rstd[:, 0:2], in_=rt[:, 0:2]), reads=["rt"], writes=["rstd"])
        for k in range(16):
            P.add("dve", lambda e, k=k: e.scalar_tensor_tensor(out=hh[:, k, :], in0=xh_sb[:, k, :],
                                                                scalar=gmix[:, k:k + 1], in1=rstd[:, 0:2],
                                                                op0=ALU.mult, op1=ALU.mult),
                  reads=["xh", "rstd", "consts"], writes=["hh"])

    def norm_main(gcol, out_h=True):
        for k in range(16):
            P.add("act", lambda e, k=k: e.activation(out=sq[:, k, :], in_=xt[:, k, :], func=AF.Square),
                  reads=[("x", k)], writes=[("sq", k)])
        for k in range(16):
            P.add("pe", lambda e, k=k: e.matmul(ps_stat[:, :], lhsT=ones_bf[:, :], rhs=sq[:, k, :],
                                                 start=(k == 0), stop=(k == 15)),
                  reads=[("sq", k), "consts"], writes=[("ps", id(ps_stat))], is_mm=True)
        P.add("act", lambda e: e.activation(out=rt[:, :], in_=ps_stat[:, :], func=AF.Sqrt, bias=EPS, scale=1.0 / D),
              reads=[("ps", id(ps_stat))], writes=["rt"])
        P.add("dve", lambda e: e.reciprocal(out=rstd[:, :], in_=rt[:, :]), reads=["rt"], writes=["rstd"])
        for k in range(16):
            if out_h:
                P.add("dve", lambda e, k=k: e.scalar_tensor_tensor(out=h[:, k, :], in0=xt[:, k, :],
                                                                    scalar=gcol[:, k:k + 1], in1=rstd[:, :],
                                                                    op0=ALU.mult, op1=ALU.mult),
                      reads=[("x", k), "rstd", "consts"], writes=[("h", k)])
            else:
                P.add("dve", lambda e, k=k: e.scalar_tensor_tensor(out=xt[:, k, :], in0=xt[:, k, :],
                                                                    scalar=gcol[:, k:k + 1], in1=rstd[:, :],
                                                                    op0=ALU.mult, op1=ALU.mult),
                      reads=[("x", k), "rstd", "consts"], writes=[("x", k)])

    def tile(t):
        cs = slice(t * T, (t + 1) * T)
        P.dma("sp", xt[:, :, :], xT.rearrange("(k p) t -> p k t", p=128)[:, :, cs], writes=XT, key="xt")
        P.dma("pool", at_sb[:, :, :], attnT.rearrange("(k p) t -> p k t", p=128)[:, :, cs],
              writes=["at"], key="at")
        if t == 0:
            norm_small()
        norm_main(gmix)
        for j in range(8):
            wcc, tcc_w = load_w(G_CBCC, 8 + j)
            wcv, tcv_w = load_w(G_CVQ, j)
            wcb, tcb_w = load_w(G_CBCC, j)
            pa, patok = gemm(wcc, tcc_w, range(16), lambda k: h[:, k, :], HT)
            P.add("act", lambda e, pa=pa: e.activation(out=tcc[:, :], in_=pa, func=AF.Copy),
                  reads=[patok], writes=["tcc"])
            if t == 0:
                gemm(wcc, tcc_w, range(16), lambda k: hh[:, k, :], ["hh"], dest=(psH[:, 0:2], "psH"))
                P.add("act", lambda e: e.activation(out=tch[:, :], in_=psH[:, 0:2], func=AF.Copy),
                      reads=["psH"], writes=["tch"])
            pv, pvtok = gemm(wcv, tcv_w, range(16), lambda k: h[:, k, :], HT)
            if t == 0:
                gemm(wcv, tcv_w, range(16), lambda k: hh[:, k, :], ["hh"], dest=(psH[:, 2:4], "psH"))
                P.add("dve", lambda e: e.tensor_tensor(out=usb[:, 0:2], in0=psH[:, 2:4], in1=tch[:, :], op=ALU.mult),
                      reads=["psH", "tch"], writes=["usb"])
            else:
                P.add("pool", lambda e, j=j: e.tensor_copy(out=usb[:, 0:2], in_=uh[:, j, :]),
                      reads=[("uh", j)], writes=["usb"])
            P.add("dve", lambda e, pv=pv: e.tensor_tensor(out=usb[:, 2:T + 2], in0=pv, in1=tcc[:, :], op=ALU.mult),
                  reads=[pvtok, "tcc", "usb"], writes=["usb"])
            P.add("pool", lambda e, j=j: e.tensor_copy(out=uh[:, j, :], in_=usb[:, T:T + 2]),
                  reads=["usb"], writes=[("uh", j)])
            P.add("dve", lambda e, j=j: e.tensor_scalar(out=t1[:, :], in0=usb[:, 2:T + 2], scalar1=cw[:, 2, j:j + 1],
                                                         scalar2=None, op0=ALU.mult),
                  reads=["usb", "consts"], writes=["t1"])
            for tap, off in ((1, 1), (0, 0)):
                P.add("dve", lambda e, j=j, tap=tap, off=off: e.scalar_tensor_tensor(
                    out=t1[:, :], in0=usb[:, off:off + T], scalar=cw[:, tap, j:j + 1], in1=t1[:, :],
                    op0=ALU.mult, op1=ALU.add), reads=["usb", "t1", "consts"], writes=["t1"])
            pc, pctok = gemm(wcb, tcb_w, range(16), lambda k: h[:, k, :], HT)
            P.add("dve", lambda e, pc=pc, j=j: e.tensor_tensor(out=convy[:, j, :], in0=pc, in1=t1[:, :], op=ALU.mult),
                  reads=[pctok, "t1"], writes=[("cy", j)])
        CY = [("cy", k) for k in range(8)]
        for j in range(16):
            wgc, tgc = load_w(G_GC, j)
            wca, tca = load_w(G_CA, j)
            wga, tga = load_w(G_GA, j)
            p1, p1t = gemm(wgc, tgc, range(16), lambda k: h[:, k, :], HT)
            P.add("act", lambda e, p1=p1, j=j: e.activation(out=sg[:, :], in_=p1, func=AF.Sigmoid,
                                                             bias=bgc[:, j:j + 1], scale=1.0),
                  reads=[p1t, "consts"], writes=["sg"])
            p2, p2t = gemm(wca, tca, range(8), lambda k: convy[:, k, :], CY)
            P.add("dve", lambda e, p2=p2: e.tensor_tensor(out=m1[:, :], in0=p2, in1=sg[:, :], op=ALU.mult),
                  reads=[p2t, "sg"], writes=["m1"])
            p3, p3t = gemm(wga, tga, range(16), lambda k: h[:, k, :], HT)
            P.add("act", lambda e, p3=p3, j=j: e.activation(out=sa[:, :], in_=p3, func=AF.Sigmoid,
                                                             bias=bga[:, j:j + 1], scale=1.0),
                  reads=[p3t, "consts"], writes=["sa"])
            p4, p4t = gemm(wca, tca, range(8), lambda k: at_sb[:, k, :], ["at"], koff=8)
            P.add("dve", lambda e, p4=p4: e.tensor_tensor(out=m2[:, :], in0=p4, in1=sa[:, :], op=ALU.mult),
                  reads=[p4t, "sa"], writes=["m2"])
            P.add("pool", lambda e, j=j: e.tensor_tensor(out=merged[:, j, :], in0=m1[:, :], in1=m2[:, :], op=ALU.add),
                  reads=["m1", "m2"], writes=[("mg", j)])
        MG = [("mg", k) for k in range(16)]
        for j in range(16):
            wm, tm = load_w(G_MIX, j)
            p5, p5t = gemm(wm, tm, range(16), lambda k: merged[:, k, :], MG)
            P.add("dve", lambda e, p5=p5, j=j: e.tensor_tensor(out=xt[:, j, :], in0=p5, in1=xt[:, j, :], op=ALU.add),
                  reads=[p5t, ("x", j)], writes=[("x", j)])
        norm_main(gmlp)
        for half in range(2):
            for f in range(32):
                fg = half * 32 + f
                w1, t1w = load_w(G_FF1 + fg // 16, fg % 16)
                p6, p6t = gemm(w1, t1w, range(16), lambda k: h[:, k, :], HT)
                ri = st["nr"] % 2
                st["nr"] += 1
                rrs = rr[ri]
                P.add("act", lambda e, p6=p6, rrs=rrs: e.activation(out=rrs[:, :], in_=p6, func=AF.Relu),
                      reads=[p6t], writes=[("rr", ri)])
                P.add("dve", lambda e, p6=p6, rrs=rrs, f=f: e.tensor_tensor(out=u[:, f, :], in0=p6, in1=rrs[:, :], op=ALU.mult),
                      reads=[p6t, ("rr", ri)], writes=[("u", f)])
            UT = [("u", f) for f in range(32)]
            for j in range(16):
                s2 = st["nw2"] % 2
                st["nw2"] += 1
                w2 = w2t[s2]
                toks = []
                for q in range(2):
                    tok = ("w2", s2, q)
                    P.dma("sp", w2[:, 16 * q:16 * (q + 1), :].rearrange("p k c -> p (k c)"),
                          wb[G_FF2 + 2 * half + q, j], writes=[tok], key=tok)
                    toks.append(tok)
                b, btok = bank()
                for f in range(32):
                    P.add("pe", lambda e, b=b, w2=w2, f=f: e.matmul(b[:, :], lhsT=w2[:, f, :], rhs=u[:, f, :],
                                                                     start=(f == 0), stop=(f == 31)),
                          reads=toks + UT, writes=[btok], is_mm=True)
                P.add("dve", lambda e, b=b, j=j: e.tensor_tensor(out=xt[:, j, :], in0=b[:, :], in1=xt[:, j, :], op=ALU.add),
                      reads=[btok, ("x", j)], writes=[("x", j)])
        if final:
            norm_main(gfin, out_h=False)
        P.dma("pool", xo.rearrange("(k p) t -> p k t", p=128)[:, :, cs], xt[:, :, :], reads=XT, key="xo")

    for t in range(NT):
        tile(t)
    return c.finish()


class FCtx:
    def __init__(self):
        self.nc = bass.Bass("TRN2", target_bir_lowering=False)
        self.P = Prog()
        self.off = 0
        self.n = 0
        self.banks = [self.nc.alloc_psum_tensor("bank%d" % i, [128, 512], F32) for i in range(8)]
        self.ncc = 0
        self.emulate = False

    def dram_in(self, name, shape, dt):
        return self.nc.dram_tensor(name, list(shape), dt, kind="ExternalInput").ap()

    def dram_out(self, name, shape, dt):
        return self.nc.dram_tensor(name, list(shape), dt, kind="ExternalOutput").ap()

    def dram(self, name, shape, dt):
        return self.nc.dram_tensor(name, list(shape), dt)

    SB_BASE = 16384

    def phase(self):
        self.off = self.SB_BASE
        self.P.barrier()

    def sb(self, name, shape, dt):
        nbytes = int(np.prod(shape[1:])) * (4 if dt == F32 else 2)
        off = (self.off + 63) // 64 * 64
        self.n += 1
        t = self.nc.alloc_sbuf_tensor_at("%s_%d" % (name, self.n), list(shape), dt, offset=off)
        self.off = off + nbytes
        assert self.off <= self.SB_BASE + 206 * 1024, (name, self.off)
        return t

    def allgather(self, src_t, dst_t, rtok, wtok):
        self.ncc += 1
        if self.emulate:
            n0 = src_t.ap().shape[0]
            op = None
            for r in range(NCORES):
                op = self.P.dma("pool", dst_t.ap()[r * n0:(r + 1) * n0], src_t.ap(), reads=[rtok], writes=[wtok],
                                key=("cc", self.ncc))
            return op
        return self.P.collective(src_t.ap().opt(), dst_t.ap().opt(), reads=[rtok], writes=[wtok],
                                 key=("cc", self.ncc))

    def finish(self):
        P = self.P
        P.finalize()
        print("fused: sems", len(P.sem_names()), "milestones", P.max_counts,
              "ops", {e: len(P.ops[e]) for e in P.ENGS}, flush=True)
        sems = {}
        with contextlib.ExitStack() as st:
            for i, n in enumerate(P.sem_names()):
                sems[n] = st.enter_context(self.nc.semaphore("s%d" % i))
            P.emit(self.nc, sems)
        return self.nc


def f_norm(c, xt, h, gcol, ones_bf, sq, ps_stat, rt, rstd, ncol=T, xtoks=None, htoks=None, inplace=False):
    P = c.P
    for k in range(16):
        P.add("act", lambda e, k=k: e.activation(out=sq[:, k, 0:ncol], in_=xt[:, k, 0:ncol], func=AF.Square),
              reads=[xtoks[k]], writes=[("sq", k)])
    for k in range(16):
        P.add("pe", lambda e, k=k: e.matmul(ps_stat[:, 0:ncol], lhsT=ones_bf[:, :], rhs=sq[:, k, 0:ncol],
                                             start=(k == 0), stop=(k == 15)),
              reads=[("sq", k), "consts"], writes=["ps_stat"], is_mm=True)
    P.add("act", lambda e: e.activation(out=rt[:, 0:ncol], in_=ps_stat[:, 0:ncol], func=AF.Sqrt, bias=EPS, scale=1.0 / D),
          reads=["ps_stat"], writes=["rt"])
    P.add("dve", lambda e: e.reciprocal(out=rstd[:, 0:ncol], in_=rt[:, 0:ncol]), reads=["rt"], writes=["rstd"])
    for k in range(16):
        dst = xt if inplace else h
        P.add("dve", lambda e, k=k, dst=dst: e.scalar_tensor_tensor(out=dst[:, k, 0:ncol], in0=xt[:, k, 0:ncol],
                                                                    scalar=gcol[:, k:k + 1], in1=rstd[:, 0:ncol],
                                                                    op0=ALU.mult, op1=ALU.mult),
              reads=[xtoks[k], "rstd", "consts"], writes=[xtoks[k] if inplace else htoks[k]])


def f_phase_W(c, wsrc, wmy, wall):
    P = c.P
    c.phase()
    stage = [c.sb("stage", [128, 2048], F32) for i in range(3)]
    wt = [c.sb("wt", [128, 16, 16, 128], BF16) for i in range(2)]
    n = 0
    for g in range(GPC):
        w = wt[g % 2]
        for k in range(16):
            st = stage[n % 3]
            tok = ("stage", n % 3)
            P.dma("sp", st[:, :], wsrc[g, k * 128:(k + 1) * 128, :], writes=[tok], key=tok)
            src = st[:, :].rearrange("p (m c) -> p m c", m=16)
            dst = w[:, :, k, :]
            eng = "dve" if n % 2 == 0 else "pool"
            P.add(eng, lambda e, dst=dst, src=src: e.tensor_copy(out=dst, in_=src),
                  reads=[tok], writes=[("wt", g % 2, k)])
            n += 1
        for half in range(2):
            dtok = ("d", "wmy", g, half)
            P.dma("sp", wmy[g][half].ap().rearrange("m p x -> p m x"),
                  w[:, 8 * half:8 * half + 8, :, :].rearrange("p m k c -> p m (k c)"),
                  reads=[("wt", g % 2, k) for k in range(16)], writes=[dtok], key=("wtst", g % 2, half))
            c.allgather(wmy[g][half], wall[g][half], dtok, ("d", "wall", g, half))


def wsl(wall, l, gi, m):
    G = l * NGRP + gi
    r, j = G // GPC, G % GPC
    half, mm = m // 8, m % 8
    return wall[j][half].ap()[r * 8 + mm], ("d", "wall", j, half)


def f_phase_A(c, l, x_ap, xtok_d, wall, wf_in, vecs_in, bfb_in, ones_in, q_my, k_my, v_my, lp_my,
              q_all, k_all, v_all, lp_all):
    P = c.P
    c.phase()
    g_sb = c.sb("g_sb", [128, 16], F32)
    ones_bf = c.sb("ones", [128, 128], BF16)
    w_sb = {nm: c.sb("w" + nm, [128, 8, 16, 128], BF16) for nm in ("q", "k", "v")}
    wf32 = c.sb("wf32", [128, 16, 8], F32)
    wf_sb = c.sb("wf_sb", [128, 16, 8], BF16)
    bf_sb = c.sb("bf_sb", [128, 32], F32)
    xt = [c.sb("xt", [128, 16, T], F32) for i in range(2)]
    sq = c.sb("sq", [128, 16, T], BF16)
    h = c.sb("h", [128, 16, T], BF16)
    rt = c.sb("rt", [128, T], F32)
    rstd = c.sb("rstd", [128, T], F32)
    ev = [c.sb("ev", [128, T], BF16) for i in range(4)]
    lpt = c.sb("lpt", [128, 32], F32)
    lpe = c.sb("lpe", [128, 32], F32)
    lps = [c.sb("lps", [128, 32], F32) for i in range(2)]
    ps_stat, ps_f = c.banks[0], c.banks[1]
    psb = c.banks[2:6]

    P.dma("sp", g_sb[:, :], vecs_in[:, 5 * l + 0, :], writes=["consts"], key="c0")
    P.dma("sp", ones_bf[:, :], ones_in[:, :], writes=["consts"], key="c0")
    P.dma("sp", bf_sb[:, :], bfb_in[l], writes=["consts"], key="c0")
    P.dma("sp", wf32[:, :, :], wf_in[l], writes=["wf32"], key="c1")
    P.add("dve", lambda e: e.tensor_copy(out=wf_sb[:, :, :], in_=wf32[:, :, :]), reads=["wf32"], writes=["wf"])
    xv = x_ap.rearrange("(k p) t -> p k t", p=128)
    XT = [[("xt", i, k) for k in range(16)] for i in range(2)]
    HT = [("h", k) for k in range(16)]
    P.dma("sp", xt[0][:, :, :], xv[:, :, 0:T], reads=[xtok_d], writes=XT[0], key=("xt", 0))
    for nm, gi, m0 in (("k", G_KV, 0), ("v", G_KV, 8), ("q", G_CVQ, 8)):
        for m in range(8):
            src, dtok = wsl(wall, l, gi, m0 + m)
            P.dma("pool", w_sb[nm][:, m, :, :].rearrange("p k c -> p (k c)"), src, reads=[dtok],
                  writes=["w" + nm], key="w" + nm)
    nev = 0
    for t in range(NT):
        if t + 1 < NT:
            P.dma("sp", xt[(t + 1) % 2][:, :, :], xv[:, :, (t + 1) * T:(t + 2) * T], reads=[xtok_d],
                  writes=XT[(t + 1) % 2], key=("xt", (t + 1) % 2))
        f_norm(c, xt[t % 2], h, g_sb, ones_bf, sq, ps_stat, rt, rstd, xtoks=XT[t % 2], htoks=HT)
        for nm, dst, sc in (("k", k_my, 1.0), ("q", q_my, SCALE), ("v", v_my, 1.0)):
            wsb = w_sb[nm]
            for m in range(NH):
                pb = psb[nev % 4]
                ptok = ("psb", nev % 4)
                for k in range(16):
                    P.add("pe", lambda e, pb=pb, wsb=wsb, m=m, k=k: e.matmul(
                        pb[:, :], lhsT=wsb[:, m, k, :], rhs=h[:, k, :], start=(k == 0), stop=(k == 15)),
                        reads=["w" + nm, HT[k]], writes=[ptok], is_mm=True)
                es = ev[nev % 4]
                etok = ("ev", nev % 4)
                P.add("act", lambda e, es=es, pb=pb, sc=sc: e.activation(
                    out=es[:, :], in_=pb[:, :], func=AF.Copy, scale=sc), reads=[ptok], writes=[etok])
                P.dma("pool", dst.ap()[m, :, t * T:(t + 1) * T], es[:, :], reads=[etok],
                      writes=[("d", nm + "_my")], key=etok)
                nev += 1
        for tb in range(4):
            for k in range(16):
                P.add("pe", lambda e, tb=tb, k=k: e.matmul(
                    ps_f[:, tb * 8:(tb + 1) * 8], lhsT=h[:, k, tb * 128:(tb + 1) * 128],
                    rhs=wf_sb[:, k, :], start=(k == 0), stop=(k == 15)),
                    reads=["wf", HT[k]], writes=["ps_f"], is_mm=True)
        ls = lps[t % 2]
        ltok = ("lps", t % 2)
        P.add("dve", lambda e: e.tensor_tensor(out=lpt[:, :], in0=ps_f[:, 0:32], in1=bf_sb[:, :], op=ALU.add),
              reads=["ps_f", "consts"], writes=["lpt"])
        P.add("act", lambda e: e.activation(out=lpe[:, :], in_=lpt[:, :], func=AF.Exp, scale=-1.0),
              reads=["lpt"], writes=["lpe"])
        P.add("act", lambda e, ls=ls: e.activation(out=ls[:, :], in_=lpe[:, :], func=AF.Ln, bias=1.0),
              reads=["lpe"], writes=[ltok])
        P.dma("pool", lp_my.ap()[t * T:(t + 1) * T, :].rearrange("(b p) h -> p b h", p=128),
              ls[:, :].rearrange("p (b h) -> p b h", b=4), reads=[ltok], writes=[("d", "lp_my")], key=ltok)
    P.barrier()
    for nm, my, al in (("k", k_my, k_all), ("v", v_my, v_all), ("q", q_my, q_all), ("lp", lp_my, lp_all)):
        c.allgather(my, al, ("d", nm + "_my"), ("d", nm + "_all"))


def f_phase_ATT(c, q_all, k_all, v_all, lp_all, cf32_in, cbf_in, oh_in, ohd_in, o_my, o_all):
    P = c.P
    c.phase()
    k_sb = c.sb("k_sb", [128, S], BF16)
    v_sb = c.sb("v_sb", [128, NKB, VW], BF16)
    lp8 = c.sb("lp8", [128, NKB, 8], F32)
    lp_sb = c.sb("lp_sb", [128, NKB], F32)
    cf = c.sb("cf", [128, 4, 128], F32)
    cbf = c.sb("cbf", [128, 2, 128], BF16)
    oh = c.sb("oh", [128, 8], F32)
    ohd = c.sb("ohd", [128, 8, 128], BF16)
    Tsb = c.sb("Tsb", [128, 2], F32)
    Dm = c.sb("Dm", [128, 128], F32)
    Cb = c.sb("Cb", [128, NKB], F32)
    Zf = c.sb("Zf", [128, T], F32)
    aug = [c.sb("aug", [128, T], F32) for i in range(2)]
    sel_in = [c.sb("sel_in", [128, 8, T], BF16) for i in range(2)]
    q_sb = [c.sb("q_sb", [128, T], BF16) for i in range(2)]
    e_sb = [c.sb("e_sb", [128, T], F32) for i in range(4)]
    p_sb = [c.sb("p_sb", [128, T], BF16) for i in range(5)]
    rl_sb = c.sb("rl_sb", [128, 4], F32)
    o_sb = [c.sb("o_sb", [128, 4, 128], BF16) for i in range(2)]
    oT_sb = [c.sb("oT_sb", [128, T], BF16) for i in range(2)]
    ps_s = c.banks[0:4]
    ps_o = [c.banks[4:6], c.banks[6:8]]
    tri, tris, ones_f, ident_f = cf[:, 0, :], cf[:, 1, :], cf[:, 2, :], cf[:, 3, :]
    ident_b = cbf[:, 1, :]
    cnt = {"s": 0, "p": 0, "sel": 0}

    P.dma("sp", cf[:, :, :], cf32_in.rearrange("n p c -> p n c"), writes=["cf"], key="cf")
    P.dma("sp", cbf[:, :, :], cbf_in.rearrange("n p c -> p n c"), writes=["cbf"], key="cf")
    P.dma("sp", oh[:, :], oh_in[:, :], writes=["oh"], key="cf")
    P.dma("sp", ohd[:, :, :], ohd_in.rearrange("n p c -> p n c"), writes=["ohd"], key="cf")
    for i in range(4):
        P.dma("pool", lp8[:, 32 * i:32 * (i + 1), :],
              lp_all.ap()[4096 * i:4096 * (i + 1), :].rearrange("(b p) h -> p b h", p=128),
              reads=[("d", "lp_all")], writes=[("lp8", i)], key=("lp8", i))
    P.add("dve", lambda e: e.memset(v_sb[:, :, 128:VW], 1.0), writes=["v1"])
    for hh in range(8):
        if hh == 0:
            P.add("dve", lambda e: e.tensor_scalar(out=lp_sb[:, :], in0=lp8[:, :, 0], scalar1=oh[:, 0:1], scalar2=None,
                                                   op0=ALU.mult), reads=[("lp8", i) for i in range(4)] + ["oh"],
                  writes=["lp"])
        else:
            P.add("dve", lambda e, hh=hh: e.scalar_tensor_tensor(out=lp_sb[:, :], in0=lp8[:, :, hh],
                                                                  scalar=oh[:, hh:hh + 1], in1=lp_sb[:, :],
                                                                  op0=ALU.mult, op1=ALU.add),
                  reads=["lp", "oh"], writes=["lp"])

    def load_sel(src_all, dtok, r, piece):
        si = cnt["sel"] % 2
        cnt["sel"] += 1
        tok = ("sel", si)
        P.dma("sp", sel_in[si][:, :, :],
              src_all.ap()[r * 8:(r + 1) * 8, :, piece * T:(piece + 1) * T].rearrange("h p t -> p h t"),
              reads=[dtok], writes=[tok], key=tok)
        return sel_in[si], tok

    def next_bank():
        si = cnt["s"] % 4
        cnt["s"] += 1
        cnt["last_si"] = si
        return ps_s[si], ("ps_s", si)

    for r in range(8):
        for piece in range(4):
            sin, stok = load_sel(k_all, ("d", "k_all"), r, piece)
            pb, ptok = next_bank()
            for hh in range(8):
                P.add("pe", lambda e, pb=pb, sin=sin, hh=hh: e.matmul(pb[:, :], lhsT=ohd[:, hh, :], rhs=sin[:, hh, :],
                                                                       start=(hh == 0), stop=(hh == 7)),
                      reads=["ohd", stok], writes=[ptok], is_mm=True)
            c0 = r * TL + piece * T
            P.add("act", lambda e, pb=pb, c0=c0: e.activation(out=k_sb[:, c0:c0 + T], in_=pb[:, :], func=AF.Copy),
                  reads=[ptok], writes=[("k", c0 // T)])
    for r in range(8):
        for piece in range(4):
            sin, stok = load_sel(v_all, ("d", "v_all"), r, piece)
            pb, ptok = next_bank()
            for j in range(4):
                for hh in range(8):
                    P.add("pe", lambda e, pb=pb, sin=sin, hh=hh, j=j: e.matmul(
                        pb[:, j * 128:(j + 1) * 128], lhsT=sin[:, hh, j * 128:(j + 1) * 128], rhs=ohd[:, hh, :],
                        start=(j == 0 and hh == 0), stop=(j == 3 and hh == 7)),
                        reads=["ohd", stok], writes=[ptok], is_mm=True)
            b0 = (r * TL + piece * T) // 128
            P.add("dve", lambda e, pb=pb, b0=b0: e.tensor_copy(
                out=v_sb[:, b0:b0 + 4, 0:128], in_=pb[:, :].rearrange("p (j d) -> p j d", j=4)),
                reads=[ptok], writes=[("v", b0 // 4)])

    pbt, pbttok = next_bank()
    P.add("pe", lambda e: e.matmul(pbt[:, 0:2], lhsT=lp_sb[:, :], rhs=ones_f[:, 0:2], start=True, stop=True),
          reads=["lp", "cf"], writes=[pbttok], is_mm=True)
    P.add("dve", lambda e: e.tensor_copy(out=Tsb[:, :], in_=pbt[:, 0:2]), reads=[pbttok], writes=["Tsb"])
    P.add("dve", lambda e: e.tensor_scalar(out=Dm[:, :], in0=tris, scalar1=Tsb[:, 0:1], scalar2=None, op0=ALU.mult),
          reads=["Tsb", "cf"], writes=["Dm"])
    pbc, pbctok = next_bank()
    P.add("pe", lambda e: e.matmul(pbc[:, 0:128], lhsT=tri, rhs=lp_sb[:, :], start=True, stop=False),
          reads=["lp", "cf"], writes=[pbctok], is_mm=True)
    P.add("pe", lambda e: e.matmul(pbc[:, 0:128], lhsT=ones_f, rhs=Dm[:, :], start=False, stop=True),
          reads=["Dm", "cf"], writes=[pbctok], is_mm=True)
    P.add("dve", lambda e: e.tensor_copy(out=Cb[:, :], in_=pbc[:, 0:128]), reads=[pbctok], writes=["Cb"])

    def prep_tile(g):
        r, piece = g // 4, g % 4
        sin, stok = load_sel(q_all, ("d", "q_all"), r, piece)
        pb, ptok = next_bank()
        for hh in range(8):
            P.add("pe", lambda e, hh=hh: e.matmul(pb[:, :], lhsT=ohd[:, hh, :], rhs=sin[:, hh, :],
                                                   start=(hh == 0), stop=(hh == 7)),
                  reads=["ohd", stok], writes=[ptok], is_mm=True)
        qs = q_sb[g % 2]
        P.add("dve", lambda e: e.tensor_copy(out=qs[:, :], in_=pb[:, :]), reads=[ptok], writes=[("q", g % 2)])
        in0 = bass.AP(cf, 3 * 128, [[4 * 128, 128], [0, 4], [1, 128]])
        in1 = bass.AP(Cb, 4 * g, [[NKB, 128], [1, 4], [0, 128]])
        outz = Zf[:, :].rearrange("p (j t) -> p j t", j=4)
        P.add("dve", lambda e: e.tensor_tensor(out=outz, in0=in0, in1=in1, op=ALU.mult),
              reads=["cf", "Cb"], writes=["Zf"])
        pa, patok = next_bank()
        P.add("pe", lambda e: e.matmul(pa[:, :], lhsT=ones_f, rhs=Zf[:, :], start=True, stop=True),
              reads=["cf", "Zf"], writes=[patok], is_mm=True)
        a = aug[g % 2]
        P.add("act", lambda e: e.activation(out=a[:, :], in_=pa[:, :], func=AF.Copy, scale=-1.0),
              reads=[patok], writes=[("aug", g % 2)])

    def tile(g):
        qs = q_sb[g % 2]
        qtok = ("q", g % 2)
        ag = aug[g % 2]
        agtok = ("aug", g % 2)
        nkb = 4 * g + 4
        po = ps_o[g % 2]
        potok = [("ps_o", g % 2, 0), ("ps_o", g % 2, 1)]
        slots = []
        started = [False, False]

        def emit_st(kb):
            j = kb - 4 * g
            c0 = 128 * j if j > 0 else 0
            pss, stok = next_bank()
            si = cnt["last_si"]
            P.add("pe", lambda e: e.matmul(pss[:, c0:T], lhsT=k_sb[:, kb * 128:(kb + 1) * 128], rhs=qs[:, c0:T],
                                           start=True, stop=True), reads=[("k", kb // 4), qtok], writes=[stok],
                  is_mm=True)
            es = e_sb[si]
            etok = ("e", si)
            P.add("dve", lambda e: e.tensor_tensor(out=es[:, c0:T], in0=pss[:, c0:T], in1=ag[:, c0:T], op=ALU.add),
                  reads=[stok, agtok], writes=[etok])
            pi = cnt["p"] % 5
            cnt["p"] += 1
            pt = p_sb[pi]
            ptok = ("p", pi)
            P.add("act", lambda e: e.activation(out=pt[:, c0:T], in_=es[:, c0:T], func=AF.Exp,
                                                bias=Cb[:, kb:kb + 1], scale=1.0),
                  reads=[etok, "Cb"], writes=[ptok])
            if j >= 0:
                P.add("pool", lambda e: e.affine_select(out=pt[:, c0:c0 + 128], in_=pt[:, c0:c0 + 128],
                                                        pattern=[[1, 128]], compare_op=ALU.is_ge, fill=0.0,
                                                        base=0, channel_multiplier=-1),
                      reads=[ptok], writes=[ptok])
            slots.append((pt, ptok, j))

        def emit_pv(kb):
            pt, ptok, j = slots[kb]
            for cc in range(max(j, 0), 4):
                bnk = cc // 2
                first = not started[bnk]
                started[bnk] = True
                last = (kb == 4 * g + 2 * bnk + 1) and (cc == 2 * bnk + 1)
                out = po[bnk][:, (cc % 2) * VW:(cc % 2) * VW + 129]
                P.add("pe", lambda e, out=out, cc=cc, first=first, last=last: e.matmul(
                    out, lhsT=pt[:, cc * 128:(cc + 1) * 128], rhs=v_sb[:, kb, 0:129], start=first, stop=last),
                    reads=[("v", kb // 4), "v1", ptok], writes=[potok[bnk]], is_mm=True)

        for i in range(nkb + 3):
            if i < nkb:
                emit_st(i)
            if i == min(8, nkb - 1) and g + 1 < NQT:
                prep_tile(g + 1)
            if i >= 3:
                emit_pv(i - 3)
        os_ = o_sb[g % 2]
        otok = ("o", g % 2)
        for bnk in range(2):
            src = bass.AP(po[bnk], 128, [[512, 128], [VW, 2]])
            P.add("dve", lambda e, src=src, bnk=bnk: e.reciprocal(out=rl_sb[:, 2 * bnk:2 * bnk + 2], in_=src),
                  reads=[potok[bnk]], writes=[("rl", bnk)])
            for h2 in range(2):
                cc = 2 * bnk + h2
                P.add("dve", lambda e, bnk=bnk, h2=h2, cc=cc: e.tensor_scalar(
                    out=os_[:, cc, :], in0=po[bnk][:, h2 * VW:h2 * VW + 128], scalar1=rl_sb[:, cc:cc + 1],
                    scalar2=None, op0=ALU.mult), reads=[potok[bnk], ("rl", bnk)], writes=[otok])
        pb, ptok = next_bank()
        for cc in range(4):
            P.add("pe", lambda e, cc=cc: e.matmul(pb[:, cc * 128:(cc + 1) * 128], lhsT=os_[:, cc, :], rhs=ident_b,
                                                   start=(cc == 0), stop=(cc == 3)),
                  reads=[otok, "cbf"], writes=[ptok], is_mm=True)
        ots = oT_sb[g % 2]
        P.add("act", lambda e: e.activation(out=ots[:, :], in_=pb[:, :], func=AF.Copy), reads=[ptok],
              writes=[("oT", g % 2)])
        P.dma("pool", o_my.ap()[:, g * T:(g + 1) * T], ots[:, :], reads=[("oT", g % 2)], writes=[("d", "o_my")],
              key=("oT", g % 2))

    prep_tile(0)
    for g in range(NQT):
        tile(g)
    P.barrier()
    c.allgather(o_my, o_all, ("d", "o_my"), ("d", "o_all"))


def f_phase_B(c, l, final, x_ap, xtok_d, x_dst_ap, xh_in, xh_all, xh_my, o_all, wall, vecs_in, cw_in, ones_in,
              oh_in, ohp_in):
    P = c.P
    c.phase()
    vec = c.sb("vec", [128, 5, 16], F32)
    cw = c.sb("cw_sb", [128, 3, 8], F32)
    ones_bf = c.sb("ones", [128, 128], BF16)
    oh = c.sb("oh", [128, 8], F32)
    ohp = c.sb("ohp", [128, 8], F32)
    xt = c.sb("xt", [128, 16, T], F32)
    sq = c.sb("sq", [128, 16, T], BF16)
    h = c.sb("h", [128, 16, T], BF16)
    rt = c.sb("rt", [128, T], F32)
    rstd = c.sb("rstd", [128, T], F32)
    xh_sb = c.sb("xh_sb", [128, 16, 2], F32)
    xh8 = c.sb("xh8", [128, 8, 32], F32)
    hh = c.sb("hh", [128, 16, 2], BF16)
    convy = c.sb("convy", [128, 8, T], BF16)
    at_sb = c.sb("at_sb", [128, 8, T], BF16)
    merged = c.sb("merged", [128, 16, T], BF16)
    u = c.sb("u", [128, 32, T], BF16)
    wt = [c.sb("wt", [128, 16, 128], BF16) for i in range(NW)]
    w2t = [c.sb("w2t", [128, 32, 128], BF16) for i in range(2)]
    tcc = c.sb("tcc", [128, T], F32)
    tch = c.sb("tch", [128, 2], F32)
    usb = c.sb("usb", [128, T + 2], F32)
    uh = c.sb("uh", [128, 8, 2], F32)
    t1 = c.sb("t1", [128, T], F32)
    sg = c.sb("sg", [128, T], F32)
    sa = c.sb("sa", [128, T], F32)
    m1 = c.sb("m1", [128, T], F32)
    m2 = c.sb("m2", [128, T], F32)
    rr = [c.sb("rr", [128, T], BF16) for i in range(2)]
    ps_stat, psH = c.banks[0], c.banks[1]
    psg = c.banks[2:8]
    st = {"nw": 0, "nw2": 0, "np": 0, "nr": 0}
    gmix, gmlp, gfin, bgc, bga = (vec[:, i, :] for i in range(5))
    XT = [("x", k) for k in range(16)]
    HT = [("h", k) for k in range(16)]
    MG = [("mg", k) for k in range(16)]

    P.dma("sp", vec[:, :, :], vecs_in[:, 5 * l:5 * l + 5, :], writes=["consts"], key="c0")
    P.dma("sp", cw[:, :, :], cw_in[l], writes=["consts"], key="c0")
    P.dma("sp", ones_bf[:, :], ones_in[:, :], writes=["consts"], key="c0")
    P.dma("sp", oh[:, :], oh_in[:, :], writes=["consts"], key="c0")
    P.dma("sp", ohp[:, :], ohp_in[:, :], writes=["consts"], key="c0")
    if l == 0:
        P.dma("sp", xh_sb[:, :, :], xh_in[:, :, :], writes=["xh"], key="xh")
    else:
        P.dma("sp", xh8[:, :, :], xh_all.ap().rearrange("(j p) x -> p j x", p=128), reads=[("d", "xh_all")],
              writes=["xh8"], key="xh")
        xhf = xh_sb[:, :, :].rearrange("p k i -> p (k i)")
        for j in range(8):
            if j == 0:
                P.add("dve", lambda e: e.tensor_scalar(out=xhf, in0=xh8[:, 0, :], scalar1=ohp[:, 0:1], scalar2=None,
                                                       op0=ALU.mult), reads=["xh8", "consts"], writes=["xh"])
            else:
                P.add("dve", lambda e, j=j: e.scalar_tensor_tensor(out=xhf, in0=xh8[:, j, :], scalar=ohp[:, j:j + 1],
                                                                    in1=xhf, op0=ALU.mult, op1=ALU.add),
                      reads=["xh8", "xh", "consts"], writes=["xh"])

    def load_w(g, m):
        sl = st["nw"] % NW
        st["nw"] += 1
        tok = ("w", sl)
        src, dtok = wsl(wall, l, g, m)
        P.dma("sp", wt[sl][:, :, :].rearrange("p k c -> p (k c)"), src, reads=[dtok], writes=[tok], key=tok)
        return wt[sl], tok

    def bank():
        i = st["np"] % 6
        st["np"] += 1
        return psg[i], ("psg", i)

    def gemm(w, wtok, ks, rhs_of, rtoks, ncols=T, dest=None, koff=0):
        if dest is None:
            b, btok = bank()
            out = b[:, 0:ncols]
        else:
            out, btok = dest
        n = len(ks)
        for i, k in enumerate(ks):
            P.add("pe", lambda e, out=out, k=k, i=i: e.matmul(out, lhsT=w[:, koff + k, :], rhs=rhs_of(k),
                                                              start=(i == 0), stop=(i == n - 1)),
                  reads=[wtok] + rtoks, writes=[btok], is_mm=True)
        return out, btok

    xv = x_ap.rearrange("(k p) t -> p k t", p=128)
    xov = x_dst_ap.rearrange("(k p) t -> p k t", p=128)

    def tile(t):
        cs = slice(t * T, (t + 1) * T)
        P.dma("sp", xt[:, :, :], xv[:, :, cs], reads=[xtok_d], writes=XT, key="xt")
        for hd in range(8):
            si = hd % 2
            cand = merged[:, 8 * si:8 * si + 8, :]
            ctoks = MG[8 * si:8 * si + 8]
            P.dma("pool", cand, o_all.ap()[hd * 128:(hd + 1) * 128, :].rearrange("p (r x) -> p r x", r=8)[:, :, cs],
                  reads=[("d", "o_all")], writes=ctoks, key=("cand", si))
            for r in range(8):
                if r == 0:
                    P.add("dve", lambda e, hd=hd, cand=cand: e.tensor_scalar(
                        out=at_sb[:, hd, :], in0=cand[:, 0, :], scalar1=oh[:, 0:1], scalar2=None, op0=ALU.mult),
                        reads=ctoks + ["consts"], writes=[("at", hd)])
                else:
                    P.add("dve", lambda e, hd=hd, cand=cand, r=r: e.scalar_tensor_tensor(
                        out=at_sb[:, hd, :], in0=cand[:, r, :], scalar=oh[:, r:r + 1], in1=at_sb[:, hd, :],
                        op0=ALU.mult, op1=ALU.add), reads=ctoks + ["consts", ("at", hd)], writes=[("at", hd)])
        AT = [("at", k) for k in range(8)]
        if t == 0:
            f_norm(c, xh_sb, hh, gmix, ones_bf, sq, ps_stat, rt, rstd, ncol=2, xtoks=["xh"] * 16, htoks=["hh"] * 16)
        f_norm(c, xt, h, gmix, ones_bf, sq, ps_stat, rt, rstd, xtoks=XT, htoks=HT)
        for j in range(8):
            wcc, tcc_w = load_w(G_CBCC, 8 + j)
            wcv, tcv_w = load_w(G_CVQ, j)
            wcb, tcb_w = load_w(G_CBCC, j)
            pa, patok = gemm(wcc, tcc_w, range(16), lambda k: h[:, k, :], HT)
            P.add("act", lambda e, pa=pa: e.activation(out=tcc[:, :], in_=pa, func=AF.Copy),
                  reads=[patok], writes=["tcc"])
            if t == 0:
                gemm(wcc, tcc_w, range(16), lambda k: hh[:, k, :], ["hh"], dest=(psH[:, 0:2], "psH"))
                P.add("act", lambda e: e.activation(out=tch[:, :], in_=psH[:, 0:2], func=AF.Copy),
                      reads=["psH"], writes=["tch"])
            pv, pvtok = gemm(wcv, tcv_w, range(16), lambda k: h[:, k, :], HT)
            if t == 0:
                gemm(wcv, tcv_w, range(16), lambda k: hh[:, k, :], ["hh"], dest=(psH[:, 2:4], "psH"))
                P.add("dve", lambda e: e.tensor_tensor(out=usb[:, 0:2], in0=psH[:, 2:4], in1=tch[:, :], op=ALU.mult),
                      reads=["psH", "tch"], writes=["usb"])
            else:
                P.add("pool", lambda e, j=j: e.tensor_copy(out=usb[:, 0:2], in_=uh[:, j, :]),
                      reads=[("uh", j)], writes=["usb"])
            P.add("dve", lambda e, pv=pv: e.tensor_tensor(out=usb[:, 2:T + 2], in0=pv, in1=tcc[:, :], op=ALU.mult),
                  reads=[pvtok, "tcc", "usb"], writes=["usb"])
            P.add("pool", lambda e, j=j: e.tensor_copy(out=uh[:, j, :], in_=usb[:, T:T + 2]),
                  reads=["usb"], writes=[("uh", j)])
            P.add("dve", lambda e, j=j: e.tensor_scalar(out=t1[:, :], in0=usb[:, 2:T + 2], scalar1=cw[:, 2, j:j + 1],
                                                         scalar2=None, op0=ALU.mult),
                  reads=["usb", "consts"], writes=["t1"])
            for tap, off in ((1, 1), (0, 0)):
                P.add("dve", lambda e, j=j, tap=tap, off=off: e.scalar_tensor_tensor(
                    out=t1[:, :], in0=usb[:, off:off + T], scalar=cw[:, tap, j:j + 1], in1=t1[:, :],
                    op0=ALU.mult, op1=ALU.add), reads=["usb", "t1", "consts"], writes=["t1"])
            pc, pctok = gemm(wcb, tcb_w, range(16), lambda k: h[:, k, :], HT)
            P.add("dve", lambda e, pc=pc, j=j: e.tensor_tensor(out=convy[:, j, :], in0=pc, in1=t1[:, :], op=ALU.mult),
                  reads=[pctok, "t1"], writes=[("cy", j)])
        CY = [("cy", k) for k in range(8)]
        for j in range(16):
            wgc, tgc = load_w(G_GC, j)
            wca, tca = load_w(G_CA, j)
            wga, tga = load_w(G_GA, j)
            p1, p1t = gemm(wgc, tgc, range(16), lambda k: h[:, k, :], HT)
            P.add("act", lambda e, p1=p1, j=j: e.activation(out=sg[:, :], in_=p1, func=AF.Sigmoid,
                                                             bias=bgc[:, j:j + 1], scale=1.0),
                  reads=[p1t, "consts"], writes=["sg"])
            p2, p2t = gemm(wca, tca, range(8), lambda k: convy[:, k, :], CY)
            P.add("dve", lambda e, p2=p2: e.tensor_tensor(out=m1[:, :], in0=p2, in1=sg[:, :], op=ALU.mult),
                  reads=[p2t, "sg"], writes=["m1"])
            p3, p3t = gemm(wga, tga, range(16), lambda k: h[:, k, :], HT)
            P.add("act", lambda e, p3=p3, j=j: e.activation(out=sa[:, :], in_=p3, func=AF.Sigmoid,
                                                             bias=bga[:, j:j + 1], scale=1.0),
                  reads=[p3t, "consts"], writes=["sa"])
            p4, p4t = gemm(wca, tca, range(8), lambda k: at_sb[:, k, :], AT, koff=8)
            P.add("dve", lambda e, p4=p4: e.tensor_tensor(out=m2[:, :], in0=p4, in1=sa[:, :], op=ALU.mult),
                  reads=[p4t, "sa"], writes=["m2"])
            P.add("pool", lambda e, j=j: e.tensor_tensor(out=merged[:, j, :], in0=m1[:, :], in1=m2[:, :], op=ALU.add),
                  reads=["m1", "m2"], writes=[("mg", j)])
        for j in range(16):
            wm, tm = load_w(G_MIX, j)
            p5, p5t = gemm(wm, tm, range(16), lambda k: merged[:, k, :], MG)
            P.add("dve", lambda e, p5=p5, j=j: e.tensor_tensor(out=xt[:, j, :], in0=p5, in1=xt[:, j, :], op=ALU.add),
                  reads=[p5t, ("x", j)], writes=[("x", j)])
        f_norm(c, xt, h, gmlp, ones_bf, sq, ps_stat, rt, rstd, xtoks=XT, htoks=HT)
        for half in range(2):
            for f in range(32):
                fg = half * 32 + f
                w1, t1w = load_w(G_FF1 + fg // 16, fg % 16)
                p6, p6t = gemm(w1, t1w, range(16), lambda k: h[:, k, :], HT)
                ri = st["nr"] % 2
                st["nr"] += 1
                rrs = rr[ri]
                P.add("act", lambda e, p6=p6, rrs=rrs: e.activation(out=rrs[:, :], in_=p6, func=AF.Relu),
                      reads=[p6t], writes=[("rr", ri)])
                P.add("dve", lambda e, p6=p6, rrs=rrs, f=f: e.tensor_tensor(out=u[:, f, :], in0=p6, in1=rrs[:, :],
                                                                            op=ALU.mult),
                      reads=[p6t, ("rr", ri)], writes=[("u", f)])
            UT = [("u", f) for f in range(32)]
            for j in range(16):
                s2 = st["nw2"] % 2
                st["nw2"] += 1
                w2 = w2t[s2]
                toks = []
                for q in range(2):
                    tok = ("w2", s2, q)
                    src, dtok = wsl(wall, l, G_FF2 + 2 * half + q, j)
                    P.dma("sp", w2[:, 16 * q:16 * (q + 1), :].rearrange("p k c -> p (k c)"), src, reads=[dtok],
                          writes=[tok], key=tok)
                    toks.append(tok)
                b, btok = bank()
                for f in range(32):
                    P.add("pe", lambda e, b=b, w2=w2, f=f: e.matmul(b[:, :], lhsT=w2[:, f, :], rhs=u[:, f, :],
                                                                     start=(f == 0), stop=(f == 31)),
                          reads=toks + UT, writes=[btok], is_mm=True)
                P.add("dve", lambda e, b=b, j=j: e.tensor_tensor(out=xt[:, j, :], in0=b[:, :], in1=xt[:, j, :], op=ALU.add),
                      reads=[btok, ("x", j)], writes=[("x", j)])
        if not final and t == NT - 1:
            P.dma("pool", xh_my.ap().rearrange("p (k i) -> p k i", i=2), xt[:, :, T - 2:T], reads=XT,
                  writes=[("d", "xh_my")], key="xhst")
        if final:
            f_norm(c, xt, h, gfin, ones_bf, sq, ps_stat, rt, rstd, xtoks=XT, htoks=HT, inplace=True)
        P.dma("pool", xov[:, :, cs], xt[:, :, :], reads=XT, writes=[("d", "xdst")], key="xo")

    for t in range(NT):
        tile(t)
    if not final:
        P.barrier()
        c.allgather(xh_my, xh_all, ("d", "xh_my"), ("d", "xh_all"))


def build_fused(depth=2, emulate=False):
    c = FCtx()
    c.emulate = emulate
    wsrc = c.dram_in("wsrc", [GPC, 2048, 2048], F32)
    xT = c.dram_in("xT", [D, TL], F32)
    xh0 = c.dram_in("xh0", [128, 16, 2], F32)
    wf_in = c.dram_in("wf", [depth, 128, 16, 8], F32)
    bfb_in = c.dram_in("bfb", [depth, 128, 32], F32)
    vecs_in = c.dram_in("vecs", [128, 5 * depth, 16], F32)
    cw_in = c.dram_in("cw", [depth, 128, 3, 8], F32)
    ones_in = c.dram_in("ones_bf", [128, 128], BF16)
    cf32_in = c.dram_in("cf32", [4, 128, 128], F32)
    cbf_in = c.dram_in("cbf", [2, 128, 128], BF16)
    oh_in = c.dram_in("oh", [128, 8], F32)
    ohp_in = c.dram_in("ohp", [128, 8], F32)
    ohd_in = c.dram_in("ohd", [8, 128, 128], BF16)
    xo = c.dram_out("xo", [D, TL], F32)
    wmy = [[c.dram("wmy%d_%d" % (g, hf), [8, 128, 2048], BF16) for hf in range(2)] for g in range(GPC)]
    if emulate:
        wall = [[c.nc.dram_tensor("wall%d_%d" % (g, hf), [64, 128, 2048], BF16, kind="ExternalInput")
                 for hf in range(2)] for g in range(GPC)]
    else:
        wall = [[c.dram("wall%d_%d" % (g, hf), [64, 128, 2048], BF16) for hf in range(2)] for g in range(GPC)]
    qkv_my = {nm: c.dram(nm + "_my", [8, 128, TL], BF16) for nm in ("q", "k", "v")}
    qkv_all = {nm: c.dram(nm + "_all", [64, 128, TL], BF16) for nm in ("q", "k", "v")}
    lp_my = c.dram("lp_my", [TL, 8], F32)
    lp_all = c.dram("lp_all", [S, 8], F32)
    o_my = c.dram("o_my", [128, S], BF16)
    o_all = c.dram("o_all", [1024, S], BF16)
    x_cur = c.dram("x_cur", [D, TL], F32)
    xh_my = c.dram("xh_my", [128, 32], F32)
    xh_all = c.dram("xh_all", [1024, 32], F32)

    if not emulate:
        f_phase_W(c, wsrc, wmy, wall)
    for l in range(depth):
        final = (l == depth - 1)
        x_ap = xT if l == 0 else x_cur.ap()
        xtok = ("d", "xin") if l == 0 else ("d", "xdst")
        f_phase_A(c, l, x_ap, xtok, wall, wf_in, vecs_in, bfb_in, ones_in, qkv_my["q"], qkv_my["k"], qkv_my["v"],
                  lp_my, qkv_all["q"], qkv_all["k"], qkv_all["v"], lp_all)
        f_phase_ATT(c, qkv_all["q"], qkv_all["k"], qkv_all["v"], lp_all, cf32_in, cbf_in, oh_in, ohd_in, o_my, o_all)
        if emulate and l == 0:
            c.P.barrier()
            for nm, t_ in (("q", qkv_my["q"]), ("k", qkv_my["k"]), ("v", qkv_my["v"]), ("lp", lp_my), ("o", o_my)):
                dbg = c.nc.dram_tensor("dbg_" + nm, list(t_.ap().shape), t_.ap().dtype, kind="ExternalOutput")
                c.P.dma("sp", dbg.ap(), t_.ap(), key=("dbg", nm))
        f_phase_B(c, l, final, x_ap, xtok, xo if final else x_cur.ap(), xh0, xh_all, xh_my, o_all, wall, vecs_in,
                  cw_in, ones_in, oh_in, ohp_in)
    return c.finish()


def kernel_fused(x, g_mix, w_in, b_f, b_gate, conv_w, w_conv_out, w_attn_out, w_mix_out, g_mlp, w_ff1, w_ff2, g_final):
    f32 = np.float32
    bf = ml_dtypes.bfloat16
    x = np.asarray(x, f32)
    depth = w_in.shape[0]
    groups = []
    for l in range(depth):
        groups += [np.asarray(g, f32) for g in weight_groups(w_in, w_conv_out, w_attn_out, w_mix_out, w_ff1, w_ff2, l)]
    ng = len(groups)
    wf = np.ascontiguousarray(np.stack([np.asarray(w_in[l][:, 6144:6152], f32).reshape(16, 128, 8).transpose(1, 0, 2)
                                        for l in range(depth)]))
    bfb = np.ascontiguousarray(np.stack([np.tile(np.asarray(b_f[l], f32), (128, 4)) for l in range(depth)]))
    vl = []
    for l in range(depth):
        vl += [colvec(np.asarray(g_mix[l], f32)), colvec(np.asarray(g_mlp[l], f32)), colvec(np.asarray(g_final, f32)),
               colvec(np.asarray(b_gate[l][:D], f32)), colvec(np.asarray(b_gate[l][D:], f32))]
    vecs = np.ascontiguousarray(np.stack(vl, axis=1))
    cw = np.ascontiguousarray(np.stack([np.asarray(conv_w[l], f32).reshape(3, 8, 128).transpose(2, 0, 1)
                                        for l in range(depth)]))
    cf32 = att_consts()
    cbf = np.stack([np.ones((128, 128), f32), np.eye(128, dtype=f32)]).astype(bf)
    ones_bf = np.ones((128, 128), bf)
    ims = []
    for r in range(NCORES):
        ids = [(r * GPC + j) % ng for j in range(GPC)]
        oh = np.zeros((128, 8), f32)
        oh[:, r] = 1
        ohp = np.zeros((128, 8), f32)
        if r > 0:
            ohp[:, r - 1] = 1
        ohd = np.zeros((8, 128, 128), f32)
        ohd[r] = np.eye(128, dtype=f32)
        halo = np.zeros((D, 2), f32) if r == 0 else x[0, r * TL - 2:r * TL, :].T
        ims.append({"wsrc": np.ascontiguousarray(np.stack([groups[i] for i in ids])),
                    "xT": np.ascontiguousarray(x[0, r * TL:(r + 1) * TL, :].T),
                    "xh0": np.ascontiguousarray(halo.reshape(16, 128, 2).transpose(1, 0, 2)),
                    "wf": wf, "bfb": bfb, "vecs": vecs, "cw": cw, "ones_bf": ones_bf, "cf32": cf32, "cbf": cbf,
                    "oh": oh, "ohp": ohp, "ohd": ohd.astype(bf)})
    if "F" not in _CACHE:
        _CACHE["F"] = build_fused(depth)
    res = run(_CACHE["F"], ims)
    out = np.concatenate([res[r]["xo"].T for r in range(NCORES)], axis=0)[None]
    return np.ascontiguousarray(out.astype(f32))


_CACHE = {}


def get_prog(name):
    if name not in _CACHE:
        _CACHE[name] = {"W": build_W, "A": build_A, "ATT": build_ATT, "B0": lambda: build_B(False), "B1": lambda: build_B(True)}[name]()
    return _CACHE[name]


def run(nc, in_maps):
    res = run_bass_kernel_spmd(nc, in_maps, core_ids=list(range(NCORES)))
    return res.results


def weight_groups(w_in, w_conv_out, w_attn_out, w_mix_out, w_ff1, w_ff2, l):
    gs = [w_in[l][:, 0:2048], w_in[l][:, 2048:4096], w_in[l][:, 4096:6144],
          w_in[l][:, 6152:8200], w_in[l][:, 8200:10248], w_mix_out[l],
          np.concatenate([w_conv_out[l], w_attn_out[l]], axis=0)]
    gs += [w_ff1[l][:, 2048 * i:2048 * (i + 1)] for i in range(4)]
    gs += [w_ff2[l][2048 * i:2048 * (i + 1), :] for i in range(4)]
    return gs


def convert_weights(groups):
    ng = len(groups)
    in_maps = []
    for cidx in range(NCORES):
        ids = [(cidx * GPC + j) % ng for j in range(GPC)]
        in_maps.append({"wsrc": np.ascontiguousarray(np.stack([groups[i] for i in ids]))})
    res = run(get_prog("W"), in_maps)
    out = [None] * ng
    for cidx in range(NCORES):
        for j in range(GPC):
            gi = cidx * GPC + j
            if gi < ng:
                out[gi] = res[cidx]["wb"][j]
    return out


def colvec(v):
    return np.ascontiguousarray(v.reshape(16, 128).T)


def att_consts():
    tri = np.triu(np.ones((128, 128), np.float32))
    tris = np.triu(np.ones((128, 128), np.float32), 1)
    cf32 = np.stack([tri, tris, np.ones((128, 128), np.float32), np.eye(128, dtype=np.float32)])
    return cf32


def kernel_unfused(x, g_mix, w_in, b_f, b_gate, conv_w, w_conv_out, w_attn_out, w_mix_out, g_mlp, w_ff1, w_ff2, g_final):
    f32 = np.float32
    x = np.asarray(x, f32)
    depth = w_in.shape[0]
    groups = []
    for l in range(depth):
        groups += weight_groups(w_in, w_conv_out, w_attn_out, w_mix_out, w_ff1, w_ff2, l)
    conv = convert_weights([np.asarray(g, f32) for g in groups])
    ones_bf = np.ones((128, 128), ml_dtypes.bfloat16)
    cf32 = att_consts()
    xT = [np.ascontiguousarray(x[0, r * TL:(r + 1) * TL, :].T) for r in range(NCORES)]
    for l in range(depth):
        wbl = np.ascontiguousarray(np.stack(conv[l * NGRP:(l + 1) * NGRP]))
        wf = np.ascontiguousarray(np.asarray(w_in[l][:, 6144:6152], f32).reshape(16, 128, 8).transpose(1, 0, 2))
        bfb = np.ascontiguousarray(np.tile(np.asarray(b_f[l], f32), (128, 4)))
        gm = colvec(np.asarray(g_mix[l], f32))
        ims = [{"xT": xT[r], "gmix": gm, "wq": np.ascontiguousarray(wbl[G_CVQ][8:16]),
                "wk": np.ascontiguousarray(wbl[G_KV][0:8]), "wv": np.ascontiguousarray(wbl[G_KV][8:16]),
                "wf": wf, "bfb": bfb, "ones_bf": ones_bf} for r in range(NCORES)]
        ra = run(get_prog("A"), ims)
        v_all = np.concatenate([ra[r]["v"] for r in range(NCORES)], axis=0)
        lp_all = np.concatenate([ra[r]["lp"] for r in range(NCORES)], axis=0)
        ims = []
        for hh in range(NH):
            qT = np.concatenate([ra[r]["qT"][hh] for r in range(NCORES)], axis=1)
            kT = np.concatenate([ra[r]["kT"][hh] for r in range(NCORES)], axis=1)
            vB = v_all[:, hh * 128:(hh + 1) * 128].reshape(NKB, 128, 128).transpose(1, 0, 2)
            lp = lp_all[:, hh].reshape(NKB, 128).T
            ims.append({"qT": np.ascontiguousarray(qT), "kT": np.ascontiguousarray(kT),
                        "vB": np.ascontiguousarray(vB), "lp": np.ascontiguousarray(lp),
                        "cf32": cf32})
        rt_ = run(get_prog("ATT"), ims)
        vecs = np.ascontiguousarray(np.stack([
            colvec(np.asarray(g_mix[l], f32)), colvec(np.asarray(g_mlp[l], f32)), colvec(np.asarray(g_final, f32)),
            colvec(np.asarray(b_gate[l][:D], f32)), colvec(np.asarray(b_gate[l][D:], f32))], axis=1))
        cw = np.ascontiguousarray(np.asarray(conv_w[l], f32).reshape(3, 8, 128).transpose(2, 0, 1))
        ims = []
        for r in range(NCORES):
            attnT = np.concatenate([rt_[hh]["oTok"][r * TL:(r + 1) * TL, :].T for hh in range(NH)], axis=0)
            if r == 0:
                halo = np.zeros((D, 2), f32)
            else:
                halo = xT[r - 1][:, TL - 2:TL]
            xh = np.ascontiguousarray(halo.reshape(16, 128, 2).transpose(1, 0, 2))
            ims.append({"xT": xT[r], "xh": xh, "attnT": np.ascontiguousarray(attnT), "wb": wbl,
                        "vecs": vecs, "cw": cw, "ones_bf": ones_bf})
        rb = run(get_prog("B1" if l == depth - 1 else "B0"), ims)
        xT = [rb[r]["xo"] for r in range(NCORES)]
    out = np.concatenate([xT[r].T for r in range(NCORES)], axis=0)[None]
    return np.ascontiguousarray(out.astype(f32))


FUSED = False


def kernel(**inputs):
    return kernel_fused(**inputs) if FUSED else kernel_unfused(**inputs)
```
